# Optimizing a Trainium2 kernel written in Bass

```python
import math
import jax, jax.numpy as jnp
from jax import lax
import numpy as np

D_MODEL = 4096
BATCH = 2
SEQ = 8192
DEPTH = 1

PLE_DIM = 256
HEAD_DIM = 128
N_HEADS = D_MODEL // 256
V_HEAD_DIM = 2 * HEAD_DIM
D_QK = N_HEADS * 2 * HEAD_DIM
D_ATTN = N_HEADS * V_HEAD_DIM
Q_BLOCK = 128
D_GMLP = D_MODEL
GMLP_GROUPS = 16
GROUP_DIM = D_GMLP // GMLP_GROUPS
CHUNK = 128
D_FF = -(-8 * D_MODEL // (3 * 256)) * 256
IN_SIZES = (D_GMLP, D_GMLP, D_QK, D_QK, D_ATTN, D_MODEL, D_MODEL)
IN_TOTAL = 2 * D_GMLP + 2 * D_QK + D_ATTN + 2 * D_MODEL
RMS_EPS = 1e-6
LN_EPS = 1e-5

kernel_name = "hybrid_gmlp_diffattn_gated_block"


def _rmsnorm(x, g, eps=RMS_EPS):
    xf = x.astype(jnp.float32)
    y = xf * lax.rsqrt(jnp.mean(xf * xf, axis=-1, keepdims=True) + eps)
    return (y * g.astype(jnp.float32)).astype(x.dtype)


def _layernorm(x, g, b, eps=LN_EPS):
    xf = x.astype(jnp.float32)
    mu = jnp.mean(xf, axis=-1, keepdims=True)
    xc = xf - mu
    var = jnp.mean(xc * xc, axis=-1, keepdims=True)
    y = xc * lax.rsqrt(var + eps) * g.astype(jnp.float32) + b.astype(jnp.float32)
    return y.astype(x.dtype)


def _alibi_slopes(n_heads):
    return jnp.exp2(-8.0 * jnp.arange(1, n_heads + 1, dtype=jnp.float32) / n_heads)


def _spatial_gating(u, v, w_s, b_s, ln_g, ln_b):
    bsz, s, _ = v.shape
    n_chunks = s // CHUNK
    vn = _layernorm(v, ln_g, ln_b).reshape(bsz, n_chunks, CHUNK, GMLP_GROUPS, GROUP_DIM)
    causal = jnp.tril(jnp.ones((CHUNK, CHUNK), dtype=w_s.dtype))
    w = w_s * causal
    mixed = jnp.einsum('gts,bcsgd->bctgd', w, vn) + b_s.T[:, :, None]
    return u * mixed.reshape(bsz, s, D_GMLP)


def _diff_attention(q, k, v, lam, slopes):
    bsz, s = q.shape[0], q.shape[1]
    n_blocks = s // Q_BLOCK
    scale = HEAD_DIM ** -0.5
    k_pos = jnp.arange(s)

    def block(i):
        start = i * Q_BLOCK
        qb = lax.dynamic_slice_in_dim(q, start, Q_BLOCK, axis=1)
        scores = jnp.einsum('bqhmd,bkhmd->bhmqk', qb, k,
                            preferred_element_type=jnp.float32) * scale
        q_pos = start + jnp.arange(Q_BLOCK)
        dist = (q_pos[:, None] - k_pos[None, :]).astype(jnp.float32)
        bias = -slopes[:, None, None, None] * dist
        scores = jnp.where(dist >= 0, scores + bias, -jnp.inf)
        probs = jax.nn.softmax(scores, axis=-1).astype(v.dtype)
        o = jnp.einsum('bhmqk,bkhe->bqhme', probs, v)
        return o[:, :, :, 0] - lam.astype(o.dtype) * o[:, :, :, 1]

    out = lax.map(block, jnp.arange(n_blocks))
    return out.transpose(1, 0, 2, 3, 4).reshape(bsz, s, N_HEADS, V_HEAD_DIM)


def setup_inputs(seed: int = 0) -> dict:
    key = jax.random.key(seed)
    ks = jax.random.split(key, 26)
    f32 = jnp.float32

    def nrm(k, shape, scale):
        return jax.random.normal(k, shape, f32) * scale

    L = DEPTH
    return {
        'x': nrm(ks[0], (BATCH, SEQ, D_MODEL), 1.0),
        'p': nrm(ks[1], (DEPTH, BATCH, SEQ, PLE_DIM), 1.0),
        'g_mix': 1.0 + nrm(ks[2], (L, D_MODEL), 0.02),
        'w_in': nrm(ks[3], (L, D_MODEL, IN_TOTAL), D_MODEL ** -0.5),
        'b_gate': nrm(ks[4], (L, 2, D_MODEL), 0.02),
        'ln_v_g': 1.0 + nrm(ks[5], (L, D_GMLP), 0.02),
        'ln_v_b': nrm(ks[6], (L, D_GMLP), 0.02),
        'w_s': nrm(ks[7], (L, GMLP_GROUPS, CHUNK, CHUNK), CHUNK ** -0.5),
        'b_s': nrm(ks[8], (L, GMLP_GROUPS, CHUNK), 0.02),
        'lambda_q1': nrm(ks[9], (L, HEAD_DIM), 0.1),
        'lambda_k1': nrm(ks[10], (L, HEAD_DIM), 0.1),
        'lambda_q2': nrm(ks[11], (L, HEAD_DIM), 0.1),
        'lambda_k2': nrm(ks[12], (L, HEAD_DIM), 0.1),
        'subln_g': 1.0 + nrm(ks[13], (L, V_HEAD_DIM), 0.02),
        'w_br_a': nrm(ks[14], (L, D_GMLP, D_MODEL), D_GMLP ** -0.5),
        'w_br_b': nrm(ks[15], (L, D_ATTN, D_MODEL), D_ATTN ** -0.5),
        'w_o': nrm(ks[16], (L, D_MODEL, D_MODEL), D_MODEL ** -0.5),
        'g_ffn': 1.0 + nrm(ks[17], (L, D_MODEL), 0.02),
        'w_gu': nrm(ks[18], (L, D_MODEL, 2 * D_FF), D_MODEL ** -0.5),
        'w_down': nrm(ks[19], (L, D_FF, D_MODEL), D_FF ** -0.5),
        'g_ple': 1.0 + nrm(ks[20], (L, D_MODEL), 0.02),
        'w_ple_gate': nrm(ks[21], (L, D_MODEL, D_MODEL), D_MODEL ** -0.5),
        'w_ple_proj': nrm(ks[22], (L, PLE_DIM, D_MODEL), PLE_DIM ** -0.5),
        'g_final': 1.0 + nrm(ks[23], (D_MODEL,), 0.02),
    }


def reference(x, p, g_mix, w_in, b_gate, ln_v_g, ln_v_b, w_s, b_s,
              lambda_q1, lambda_k1, lambda_q2, lambda_k2, subln_g,
              w_br_a, w_br_b, w_o, g_ffn, w_gu, w_down,
              g_ple, w_ple_gate, w_ple_proj, g_final):
    bsz, s, _ = x.shape
    split_points = np.cumsum(IN_SIZES)[:-1].tolist()
    slopes = _alibi_slopes(N_HEADS)
    for i in range(DEPTH):
        h = _rmsnorm(x, g_mix[i])
        proj = h @ w_in[i]
        u_a, v_a, q, k, v_b, gate_a, gate_b = jnp.split(proj, split_points, axis=-1)

        y_a = _spatial_gating(jax.nn.gelu(u_a, approximate=False),
                              jax.nn.gelu(v_a, approximate=False),
                              w_s[i], b_s[i], ln_v_g[i], ln_v_b[i])

        lam_init = 0.8 - 0.6 * math.exp(-0.3 * i)
        lam = (jnp.exp(jnp.sum(lambda_q1[i].astype(jnp.float32) * lambda_k1[i].astype(jnp.float32)))
               - jnp.exp(jnp.sum(lambda_q2[i].astype(jnp.float32) * lambda_k2[i].astype(jnp.float32)))
               + lam_init)
        qh = q.reshape(bsz, s, N_HEADS, 2, HEAD_DIM)
        kh = k.reshape(bsz, s, N_HEADS, 2, HEAD_DIM)
        vh = v_b.reshape(bsz, s, N_HEADS, V_HEAD_DIM)
        o = _diff_attention(qh, kh, vh, lam, slopes)
        y_b = (_rmsnorm(o, subln_g[i], eps=LN_EPS) * (1.0 - lam_init)).reshape(bsz, s, D_ATTN)

        merged = (jax.nn.sigmoid(gate_a + b_gate[i, 0]) * (y_a @ w_br_a[i])
                  + jax.nn.sigmoid(gate_b + b_gate[i, 1]) * (y_b @ w_br_b[i]))
        x = x + merged @ w_o[i]

        h = _rmsnorm(x, g_ffn[i])
        g_ff, u_ff = jnp.split(h @ w_gu[i], 2, axis=-1)
        x = x + (jax.nn.silu(g_ff) * u_ff) @ w_down[i]

        x = x + jax.nn.sigmoid(_rmsnorm(x, g_ple[i]) @ w_ple_gate[i]) * (p[i] @ w_ple_proj[i])
    return _rmsnorm(x, g_final)
```

```python
import contextlib
import math
import numpy as np
import concourse.bass as bass
import concourse.mybir as mybir
from concourse.bass_utils import run_bass_kernel_spmd

F32 = mybir.dt.float32
BF16 = mybir.dt.bfloat16
I32 = mybir.dt.int32
AF = mybir.ActivationFunctionType
ALU = mybir.AluOpType
AX = mybir.AxisListType

RMS_EPS = 1e-6
LN_EPS = 1e-5
HEAD_DIM = 128
NEG_BIG = 30000.0


class Op:
    __slots__ = ("eng", "fn", "deps", "inc", "val", "dma_key", "dma_val")


class Prog:
    ENGS = ("pe", "act", "dve", "pool", "sp")

    def __init__(self, nc):
        self.nc = nc
        self.ops = {e: [] for e in self.ENGS}
        self.last_w = {}
        self.readers = {}
        self.dma_cnt = {}
        self.dma_last = {}
        self.last_real = {e: None for e in self.ENGS}

    def add(self, eng, fn, reads=(), writes=(), dma_key=None, extra_deps=()):
        op = Op()
        op.eng = eng; op.fn = fn
        op.inc = False; op.val = None; op.dma_key = dma_key; op.dma_val = None
        deps = []
        for r in reads:
            w = self.last_w.get(r)
            if w is not None:
                deps.append((w, True))
        for wk in writes:
            w = self.last_w.get(wk)
            if w is not None:
                deps.append((w, False))
            for rd in self.readers.get(wk, ()):
                deps.append((rd, False))
        for d in extra_deps:
            deps.append((d, True))
        if dma_key is not None:
            c = self.dma_cnt.get(dma_key, 0) + 1
            self.dma_cnt[dma_key] = c
            op.dma_val = 16 * c
            prev = self.dma_last.get(dma_key)
            if prev is not None:
                deps.append((prev, True))
            self.dma_last[dma_key] = op
        fd = []
        seen = set()
        for d, raw in deps:
            if d is op or id(d) in seen:
                continue
            if d.dma_key is None and d.eng == eng:
                if eng == "pe" or not raw:
                    continue
            seen.add(id(d))
            fd.append(d)
        op.deps = fd
        for d in fd:
            if d.dma_key is None:
                d.inc = True
        for r in reads:
            self.readers.setdefault(r, []).append(op)
        for wk in writes:
            self.last_w[wk] = op
            self.readers[wk] = []
        self.ops[eng].append(op)
        if fn is not None and dma_key is None:
            self.last_real[eng] = op
        return op

    def barrier(self, engs=("pe", "act", "dve", "sp"), keep=("W",)):
        lasts = [self.last_real[e] for e in self.ENGS if self.last_real[e] is not None and e != "pool"]
        dmas = [op for k, op in self.dma_last.items() if not (isinstance(k, tuple) and k[0] in keep)]
        for e in engs:
            self.add(e, None, extra_deps=lasts + dmas)
        for k in list(self.last_w.keys()):
            if not (isinstance(k, tuple) and k[0] in keep):
                del self.last_w[k]
        for k in list(self.readers.keys()):
            if not (isinstance(k, tuple) and k[0] in keep):
                del self.readers[k]

    def emit(self, final_waits=()):
        nc = self.nc
        for e in self.ENGS:
            c = 0
            for op in self.ops[e]:
                if op.dma_key is None and op.inc:
                    assert op.fn is not None
                    c += 1
                    op.val = c
        with contextlib.ExitStack() as st:
            esem = {e: st.enter_context(nc.semaphore("es_" + e)) for e in self.ENGS}
            dsem = {k: st.enter_context(nc.semaphore("ds_%d" % i)) for i, k in enumerate(self.dma_cnt)}
            block = st.enter_context(nc.Block())

            def run(e, engobj):
                waited = {}
                for op in self.ops[e]:
                    need = {}
                    for d in op.deps:
                        if d.dma_key is not None:
                            s, v = dsem[d.dma_key], d.dma_val
                        else:
                            s, v = esem[d.eng], d.val
                        key = id(s)
                        if waited.get(key, 0) >= v:
                            continue
                        if key not in need or need[key][1] < v:
                            need[key] = (s, v)
                    for key, (s, v) in need.items():
                        engobj.wait_ge(s, v)
                        waited[key] = v
                    if op.fn is None:
                        continue
                    ins = op.fn(engobj)
                    if op.dma_key is not None:
                        ins.then_inc(dsem[op.dma_key], 16)
                    elif op.inc:
                        ins.then_inc(esem[e], 1)
                if e == "sp":
                    for op in final_waits:
                        engobj.wait_ge(dsem[op.dma_key], op.dma_val)

            @block.tensor
            def _(t):
                run("pe", t)

            @block.scalar
            def _(t):
                run("act", t)

            @block.vector
            def _(t):
                run("dve", t)

            @block.gpsimd
            def _(t):
                run("pool", t)

            @block.sync
            def _(t):
                run("sp", t)


class Ring:
    def __init__(self, items):
        self.items = items
        self.i = 0

    def next(self):
        it = self.items[self.i % len(self.items)]
        self.i += 1
        return it


def build(cfg):
    D = cfg["D"]; S = cfg["S"]; DFF = cfg["DFF"]; PLE = cfg["PLE"]
    slopes = [float(s) for s in cfg["slopes"]]
    lam_init = 0.2
    H = D // 256; G = D // 256; KC = D // 128; FC = DFF // 128; PC = PLE // 128
    NBA = S // 128; NOB = NBA // 4; NO = NOB * 128
    TG = min(cfg.get("TG", 1024), NO)
    TGK = 512
    TGD = min(cfg.get("TGD", 512), TG)
    CH = min(16, NBA)
    NTAB = NBA + 3
    NTW = NBA + 16
    DSKIP = []
    for sl in slopes:
        d = 0
        while sl * (128 * d - 63) <= 128.0 and d <= NBA:
            d += 1
        DSKIP.append(d if (d <= NBA and cfg.get("skip", True)) else None)
    WIDE = [bool(cfg.get("wide", True)) and sl * 1663.0 <= 40.0 for sl in slopes]
    WIDX = {h: i for i, h in enumerate([h for h in range(len(slopes)) if WIDE[h]])}
    NW = max(1, len(WIDX))
    SCALE = HEAD_DIM ** -0.5
    assert TG % 512 == 0 and TGK % 512 == 0 and D % 512 == 0 and DFF % 256 == 0

    nc = bass.Bass("TRN2", target_bir_lowering=False)

    def din(name, shape):
        return nc.dram_tensor(name, list(shape), F32, kind="ExternalInput").ap()

    x_all = din("x_all", [S, D]); x_own = din("x_own", [NO, D]); p_own = din("p_own", [NO, PLE])
    jv_d = din("jv", [128, 1])
    g_mix = din("g_mix", [1, D]); w_in = din("w_in", [D, 7 * D]); b_gate = din("b_gate", [2, D])
    ln_v_g = din("ln_v_g", [1, D]); ln_v_b = din("ln_v_b", [1, D])
    w_s = din("w_s", [G, 128, 128]); b_s = din("b_s", [1, G * 128])
    lq1 = din("lambda_q1", [1, 128]); lk1 = din("lambda_k1", [1, 128])
    lq2 = din("lambda_q2", [1, 128]); lk2 = din("lambda_k2", [1, 128])
    subln_g = din("subln_g", [1, 256])
    w_br_a = din("w_br_a", [D, D]); w_br_b = din("w_br_b", [D, D]); w_o = din("w_o", [D, D])
    g_ffn = din("g_ffn", [1, D]); w_gu = din("w_gu", [D, 2 * DFF]); w_down = din("w_down", [DFF, D])
    g_ple = din("g_ple", [1, D]); w_pg = din("w_ple_gate", [D, D]); w_pp = din("w_ple_proj", [PLE, D])
    g_final = din("g_final", [1, D])
    out_d = nc.dram_tensor("out", [NO, D], F32, kind="ExternalOutput").ap()

    def dscr(name, shape, dt):
        if cfg.get("debug"):
            return nc.dram_tensor(name, list(shape), dt, kind="ExternalOutput").ap()
        return nc.dram_tensor(name, list(shape), dt).ap()

    KT = dscr("KT", [D, S], BF16); VV = dscr("VV", [S, D], BF16)
    QT = dscr("QT", [D, NO], BF16); MT = dscr("MT", [D, NO], BF16)
    YAT = dscr("YAT", [D, NO], BF16); YBT = dscr("YBT", [D, NO], BF16)
    GA = dscr("GA", [D, NO], BF16); GB = dscr("GB", [D, NO], BF16)
    T1 = dscr("T1", [D, NO], F32); MGT = dscr("MGT", [D, NO], BF16)
    X1 = dscr("X1", [NO, D], F32); X2 = dscr("X2", [NO, D], F32); X3 = dscr("X3", [NO, D], F32)
    FFA = dscr("FFA", [DFF, NO], BF16)
    WKV = dscr("WKV", [D, 2 * D], BF16)

    P = Prog(nc)
    st = contextlib.ExitStack()
    ARENA_B = 128 * 1024
    arena = st.enter_context(nc.sbuf_tensor("arena", [128, ARENA_B // 2], BF16))
    wring_t = [st.enter_context(nc.sbuf_tensor("wr%d" % i, [128, 8192], BF16)) for i in range(3)]
    CONST_F = 512 + 15 + H * NTAB + NW * NTW + 4 * KC + KC * 128 + 256 + 2 + 3
    cst = st.enter_context(nc.sbuf_tensor("cst", [128, CONST_F], F32))
    cbf = st.enter_context(nc.sbuf_tensor("cbf", [128, 128 + 4 * 128 + G * 128 + 128], BF16))
    psf = [st.enter_context(nc.psum_tensor("psf%d" % i, [128, 512], F32)) for i in range(6)]
    psb = [st.enter_context(nc.psum_tensor("psb%d" % i, [128, 1024], BF16)) for i in range(2)]

    co = [0]

    def cf(n):
        a = cst[:, co[0]:co[0] + n]
        co[0] += n
        return a

    ident_f = cf(128); valf = cf(128); tri_f = cf(128); ones_f = cf(128)
    jv = cf(1); jv128 = cf(1); jsh = cf(4); neglam = cf(1); lamtmp = cf(8)
    tab = cf(H * NTAB).rearrange("p (h c) -> p h c", h=H)
    tabw = cf(NW * NTW).rearrange("p (h c) -> p h c", h=NW)
    cols = cf(4 * KC).rearrange("p (a c) -> p a c", a=4)
    B2 = cf(KC * 128).rearrange("p (c t) -> p c t", c=KC)
    subg = cf(256)
    epsc = cf(2)
    assert co[0] <= CONST_F
    bo = [0]

    def cb(n):
        a = cbf[:, bo[0]:bo[0] + n]
        bo[0] += n
        return a

    ident_b = cb(128)
    masks = cb(4 * 128).rearrange("p (r q) -> p r q", r=4)
    wT = cb(G * 128).rearrange("p (g t) -> p g t", g=G)
    ones_b = cb(128)

    def carve(off, nbytes, dt):
        assert off % 4 == 0 and off + nbytes <= ARENA_B, (off, nbytes)
        a = arena[:, off // 2:(off + nbytes) // 2]
        return a.bitcast(F32) if dt == F32 else a

    wr = Ring([(wring_t[i], ("W", i)) for i in range(3)])

    pool_extra = []

    def wload(src_view, kc, n, reads=()):
        t, key = wr.next()
        v = t[:].rearrange("p (c n) -> p c n", n=n)[:, 0:kc, :]
        P.add("pool", lambda e, v=v, s=src_view: e.dma_start(out=v, in_=s), reads=list(reads), writes=[key], dma_key=key)
        if pool_extra:
            pool_extra.pop(0)()
        return v, key

    def wview_fm(w, c0, n):
        return w[:, c0:c0 + n].rearrange("(c p) n -> p c n", p=128)

    def wview_tm(w, k0, kp, c0, n):
        return w[k0 * 128:(k0 + kp) * 128, c0:c0 + n].rearrange("(c p) n -> p c n", p=128)

    gb = Ring([(psf[i], ("PS", i)) for i in range(4)])

    def gemm_fm(w, col0, ncols, AT, tg, epi, kcn=None, key_at="AT", hook=None, wreads=None):
        kcn = kcn or KC
        for ng in range(ncols // 256):
            wv, wkey = wload(wview_fm(w, col0 + ng * 256, 256), kcn, 256, reads=(wreads(col0 + ng * 256) if wreads else ()))
            for c2 in range(2):
                for ts in range(tg // 512):
                    ps, pkey = gb.next()
                    for k in range(kcn):
                        rk = [(key_at, ts * 4 + i, k // 8) for i in range(4)]
                        P.add("pe", lambda e, ps=ps, wv=wv, k=k, c2=c2, ts=ts: e.matmul(
                            ps[:], wv[:, k, c2 * 128:(c2 + 1) * 128], AT[:, k, ts * 512:(ts + 1) * 512],
                            start=(k == 0), stop=(k == kcn - 1)),
                            reads=[wkey] + rk, writes=[pkey])
                    epi(ng * 2 + c2, ts, ps, pkey)
            if hook is not None:
                hook(ng)

    def gemm_tm(w, kcn, col0, ncols, AT, tbs, epi, key_at="AT", wreads=None):
        pieces = [(k0, min(16, kcn - k0)) for k0 in range(0, kcn, 16)]
        for nt in range(ncols // 512):
            for pi, (k0, kp) in enumerate(pieces):
                wv, wkey = wload(wview_tm(w, k0, kp, col0 + nt * 512, 512), kp, 512, reads=(wreads(col0 + nt * 512) if wreads else ()))
                for tb in tbs:
                    ps, pkey = gb.next()
                    for k in range(kp):
                        P.add("pe", lambda e, ps=ps, wv=wv, k=k, k0=k0, tb=tb: e.matmul(
                            ps[:], AT[:, k0 + k, tb * 128:(tb + 1) * 128], wv[:, k, :],
                            start=(k == 0), stop=(k == kp - 1)),
                            reads=[wkey, (key_at, tb, (k0 + k) // 8)], writes=[pkey])
                    epi(pi, len(pieces), tb, nt, ps, pkey)

    tpr = Ring([(psb[i], ("PB", i)) for i in range(2)])
    evtoggle = [0]

    def copy_any(out, in_, reads, writes):
        evtoggle[0] ^= 1
        if evtoggle[0]:
            P.add("act", lambda e: e.copy(out=out, in_=in_), reads=reads, writes=writes)
        else:
            P.add("dve", lambda e: e.tensor_copy(out=out, in_=in_), reads=reads, writes=writes)

    def load_bcast(dst, vec, key):
        P.add("sp", lambda e: e.dma_start(out=dst, in_=vec.broadcast_to([128, vec.shape[1]])),
              writes=[key], dma_key=("ld", key))

    def norm_block(src, r0, gbc, xb, xbkey, xn, ss, tbslot, AT, final_out=None, key_at="AT"):
        P.add("sp", lambda e: e.dma_start(out=xb, in_=src[r0:r0 + 128, :]), writes=[xbkey], dma_key=("ld", xbkey))
        P.add("act", lambda e: e.activation(out=xn, in_=xb, func=AF.Square, accum_out=ss[:, 0:1]),
              reads=[xbkey], writes=["xn", "ss0"])
        P.add("act", lambda e: e.activation(out=ss[:, 1:2], in_=ss[:, 0:1], func=AF.Sqrt, bias=epsc[:, 0:1], scale=1.0 / D),
              reads=["ss0"], writes=["ss1"])
        P.add("dve", lambda e: e.reciprocal(out=ss[:, 2:3], in_=ss[:, 1:2]), reads=["ss1"], writes=["ss2"])
        if final_out is not None:
            fo, fokey, dst = final_out
            P.add("dve", lambda e: e.scalar_tensor_tensor(out=fo, in0=xb, scalar=ss[:, 2:3], in1=gbc, op0=ALU.mult, op1=ALU.mult),
                  reads=[xbkey, "ss2", "gbc"], writes=[fokey])
            return P.add("sp", lambda e: e.dma_start(out=dst, in_=fo), reads=[fokey], dma_key=("st", fokey))
        P.add("dve", lambda e: e.scalar_tensor_tensor(out=xn, in0=xb, scalar=ss[:, 2:3], in1=gbc, op0=ALU.mult, op1=ALU.mult),
              reads=[xbkey, "ss2", "gbc"], writes=["xn"])
        for c0 in range(0, KC, 8):
            nn = min(8, KC - c0)
            tp, tkey = tpr.next()
            for c in range(nn):
                P.add("pe", lambda e, tp=tp, c=c, c0=c0: e.transpose(out=tp[:, c * 128:(c + 1) * 128],
                                                                     in_=xn[:, (c0 + c) * 128:(c0 + c + 1) * 128], identity=ident_b),
                      reads=["xn"], writes=[tkey])
            copy_any(AT[:, c0:c0 + nn, tbslot * 128:(tbslot + 1) * 128],
                     tp[:, 0:nn * 128].rearrange("p (c t) -> p c t", c=nn), [tkey], [(key_at, tbslot, c0 // 8)])

    def norm_phase(src, row0, ntb, gvec, AT, off):
        xbs = [(carve(off + i * 4 * D, 4 * D, F32), ("xb", i)) for i in range(2)]
        gbc = carve(off + 8 * D, 4 * D, F32)
        xn = carve(off + 12 * D, 2 * D, BF16)
        ss = carve(off + 14 * D, 16, F32)
        load_bcast(gbc, gvec, "gbc")
        for tb in range(ntb):
            xb, xbkey = xbs[tb % 2]
            norm_block(src, row0 + tb * 128, gbc, xb, xbkey, xn, ss, tb, AT)

    def load_AT(AT, src, kcn, c0, ntok, nparts=4):
        step = ((kcn + nparts - 1) // nparts + 7) // 8 * 8
        for i, k0 in enumerate(range(0, kcn, step)):
            k1 = min(kcn, k0 + step)
            P.add("sp", lambda e, k0=k0, k1=k1: e.dma_start(
                out=AT[:, k0:k1, :], in_=src[k0 * 128:k1 * 128, c0:c0 + ntok].rearrange("(c p) t -> p c t", p=128)),
                writes=[("AT", tb, cg) for tb in range(ntok // 128) for cg in range(k0 // 8, (k1 - 1) // 8 + 1)], dma_key=("ldat", i))

    scr_i = carve(0, 128 * 4, F32).bitcast(I32)
    scr_i2 = carve(512, NTAB * 4, F32).bitcast(I32)
    D1 = carve(2048, NTAB * 4, F32); DL = carve(4096, NTAB * 4, F32)
    t1 = carve(6144, NTAB * 4, F32); t2 = carve(8192, NTAB * 4, F32)
    P.add("pool", lambda e: e.iota(out=scr_i, pattern=[[1, 128]], base=0, channel_multiplier=-1), writes=["scr_i"])
    P.add("dve", lambda e: e.tensor_copy(out=valf, in_=scr_i), reads=["scr_i"], writes=["valf"])
    P.add("dve", lambda e: e.tensor_scalar(out=ident_f, in0=valf, scalar1=0.0, scalar2=None, op0=ALU.is_equal), reads=["valf"], writes=["ident_f"])
    P.add("dve", lambda e: e.tensor_scalar(out=tri_f, in0=valf, scalar1=0.0, scalar2=None, op0=ALU.is_ge), reads=["valf"], writes=["tri_f"])
    P.add("dve", lambda e: e.tensor_copy(out=ident_b, in_=ident_f), reads=["ident_f"], writes=["ident_b"])
    P.add("dve", lambda e: e.memset(ones_f, 1.0), writes=["ones_f"])
    P.add("dve", lambda e: e.memset(ones_b, 1.0), writes=["ones_b"])
    P.add("dve", lambda e: e.memset(epsc[:, 0:1], RMS_EPS), writes=["epsc"])
    P.add("dve", lambda e: e.memset(epsc[:, 1:2], LN_EPS), writes=["epsc"])
    P.add("sp", lambda e: e.dma_start(out=jv, in_=jv_d), writes=["jv"], dma_key="s_jv")
    P.add("dve", lambda e: e.tensor_scalar(out=jv128, in0=jv, scalar1=128.0, scalar2=None, op0=ALU.mult), reads=["jv"], writes=["jv128"])
    for r in range(4):
        P.add("dve", lambda e, r=r: e.tensor_scalar(out=jsh[:, r:r + 1], in0=jv, scalar1=128.0, scalar2=-128.0 * r, op0=ALU.mult, op1=ALU.add),
              reads=["jv"], writes=["jsh"])
    for r in range(4):
        P.add("dve", lambda e, r=r: e.tensor_scalar(out=masks[:, r, :], in0=valf, scalar1=jsh[:, r:r + 1], scalar2=0.0, op0=ALU.add, op1=ALU.is_ge),
              reads=["valf", "jsh"], writes=["masks"])
    P.add("pool", lambda e: e.iota(out=scr_i2, pattern=[[-128, NTAB]], base=320, channel_multiplier=1), writes=["scr_i2"])
    P.add("dve", lambda e: e.tensor_copy(out=D1, in_=scr_i2), reads=["scr_i2"], writes=["D1"])
    P.add("dve", lambda e: e.tensor_scalar(out=D1, in0=D1, scalar1=jv128, scalar2=None, op0=ALU.subtract), reads=["D1", "jv128"], writes=["D1"])
    P.add("pool", lambda e: e.iota(out=scr_i2, pattern=[[1, NTAB]], base=-3, channel_multiplier=0), reads=["D1"], writes=["scr_i2"])
    P.add("dve", lambda e: e.tensor_copy(out=DL, in_=scr_i2), reads=["scr_i2"], writes=["DL"])
    P.add("dve", lambda e: e.tensor_scalar(out=DL, in0=DL, scalar1=jv, scalar2=0.0, op0=ALU.add, op1=ALU.is_ge), reads=["DL", "jv"], writes=["DL"])
    P.add("dve", lambda e: e.tensor_tensor(out=t1, in0=D1, in1=DL, op=ALU.mult), reads=["D1", "DL"], writes=["t1"])
    P.add("dve", lambda e: e.tensor_scalar(out=t2, in0=DL, scalar1=1.0, scalar2=NEG_BIG, op0=ALU.subtract, op1=ALU.mult), reads=["DL"], writes=["t2"])
    for h in range(H):
        P.add("dve", lambda e, h=h: e.scalar_tensor_tensor(out=tab[:, h, :], in0=t1, scalar=slopes[h], in1=t2, op0=ALU.mult, op1=ALU.add),
              reads=["t1", "t2"], writes=["tab"])
    DW = carve(10240, NTW * 4, F32)
    scr_i3 = carve(12288, NTW * 4, F32).bitcast(I32)
    P.add("pool", lambda e: e.iota(out=scr_i3, pattern=[[-128, NTW]], base=-832 + 128 * 15, channel_multiplier=1), writes=["scr_i3"])
    P.add("dve", lambda e: e.tensor_copy(out=DW, in_=scr_i3), reads=["scr_i3"], writes=["DW"])
    P.add("dve", lambda e: e.tensor_scalar(out=DW, in0=DW, scalar1=jv128, scalar2=None, op0=ALU.subtract), reads=["DW", "jv128"], writes=["DW"])
    for h, hw in WIDX.items():
        P.add("dve", lambda e, h=h, hw=hw: e.tensor_scalar(out=tabw[:, hw, :], in0=DW, scalar1=slopes[h], scalar2=None, op0=ALU.mult), reads=["DW"], writes=["tabw"])
    lb = carve(16384, 4 * 128 * 4, F32).rearrange("p (a n) -> p a n", a=4)
    junk = carve(20480, 128 * 4, F32)
    for i, v in enumerate((lq1, lk1, lq2, lk2)):
        P.add("sp", lambda e, i=i, v=v: e.dma_start(out=lb[:, i, :], in_=v.broadcast_to([128, 128])), writes=[("lb", i)], dma_key=("s_lb", i))
    for i in range(2):
        P.add("dve", lambda e, i=i: e.tensor_tensor(out=junk, in0=lb[:, 2 * i, :], in1=lb[:, 2 * i + 1, :], op=ALU.mult),
              reads=[("lb", 2 * i), ("lb", 2 * i + 1)], writes=["junk"])
        P.add("dve", lambda e, i=i: e.reduce_sum(out=lamtmp[:, i:i + 1], in_=junk, axis=AX.X), reads=["junk"], writes=[("lt", i)])
        P.add("act", lambda e, i=i: e.activation(out=lamtmp[:, 2 + i:3 + i], in_=lamtmp[:, i:i + 1], func=AF.Exp), reads=[("lt", i)], writes=[("le", i)])
    P.add("dve", lambda e: e.tensor_tensor(out=lamtmp[:, 4:5], in0=lamtmp[:, 3:4], in1=lamtmp[:, 2:3], op=ALU.subtract), reads=[("le", 0), ("le", 1)], writes=["lt4"])
    P.add("dve", lambda e: e.tensor_scalar(out=neglam, in0=lamtmp[:, 4:5], scalar1=-lam_init, scalar2=None, op0=ALU.add), reads=["lt4"], writes=["neglam"])
    P.add("sp", lambda e: e.dma_start(out=subg, in_=subln_g.broadcast_to([128, 256])), writes=["subg"], dma_key="s_subg")
    P.add("dve", lambda e: e.tensor_scalar(out=subg, in0=subg, scalar1=1.0 - lam_init, scalar2=None, op0=ALU.mult), reads=["subg"], writes=["subg"])
    vrows = carve(24576, 128 * 4, F32)
    for a, v in enumerate((ln_v_g, ln_v_b, b_gate[0:1, :], b_gate[1:2, :])):
        P.add("sp", lambda e, v=v: e.dma_start(out=vrows[0:KC, :], in_=v.rearrange("o (c p) -> (o c) p", p=128)), writes=["vrows"], dma_key="s_vr")
        P.add("pe", lambda e: e.transpose(out=psf[4][:, 0:KC], in_=vrows[0:KC, :], identity=ident_f[0:KC, 0:KC]),
              reads=["vrows", "ident_f"], writes=[("PS", 4)])
        P.add("dve", lambda e, a=a: e.tensor_copy(out=cols[:, a, :], in_=psf[4][:, 0:KC]), reads=[("PS", 4)], writes=["cols"])
    wsl = carve(28672, 128 * 4, F32); wtf = carve(32768, 128 * 4, F32)
    bsbc = carve(36864, G * 128 * 4, F32).rearrange("p (g t) -> p g t", g=G)
    P.add("sp", lambda e: e.dma_start(out=bsbc.rearrange("p g t -> p (g t)"), in_=b_s.broadcast_to([128, G * 128])), writes=["bsbc"], dma_key="s_bs")
    for g in range(G):
        P.add("sp", lambda e, g=g: e.dma_start(out=wsl, in_=w_s[g]), writes=["wsl"], dma_key="s_ws")
        P.add("pe", lambda e: e.transpose(out=psf[4][:, 0:128], in_=wsl, identity=ident_f), reads=["wsl", "ident_f"], writes=[("PS", 4)])
        P.add("dve", lambda e: e.tensor_tensor(out=wtf, in0=psf[4][:, 0:128], in1=tri_f, op=ALU.mult), reads=[("PS", 4), "tri_f"], writes=["wtf"])
        P.add("dve", lambda e, g=g: e.tensor_copy(out=wT[:, g, :], in_=wtf), reads=["wtf"], writes=["wT"])
        P.add("pe", lambda e: e.matmul(psf[5][:, 0:128], ones_f, wtf, start=True, stop=True), reads=["wtf", "ones_f"], writes=[("PS", 5)])
        for cc in range(2):
            c = 2 * g + cc
            P.add("dve", lambda e, c=c, g=g: e.scalar_tensor_tensor(out=B2[:, c, :], in0=psf[5][:, 0:128], scalar=cols[:, 1, c:c + 1], in1=bsbc[:, g, :],
                                                                     op0=ALU.mult, op1=ALU.add), reads=[("PS", 5), "cols", "bsbc"], writes=["B2"])
    P.barrier()

    TGK = 512
    NGK = S // TGK
    ATs = [carve(i * KC * TGK * 2, KC * TGK * 2, BF16).rearrange("p (c t) -> p c t", c=KC) for i in range(2)]
    off_n = 2 * KC * TGK * 2
    xb_k = carve(off_n, 4 * D, F32); gbc_k = carve(off_n + 4 * D, 4 * D, F32)
    xn_k = carve(off_n + 8 * D, 2 * D, BF16); ss_k = carve(off_n + 10 * D, 16, F32)
    soff = off_n + 10 * D + 64
    kst = [(carve(soff + i * TGK * 2, TGK * 2, BF16), ("kst", i)) for i in range(2)]
    vst = [(carve(soff + 2 * TGK * 2 + i * 1024, 1024, BF16), ("vst", i)) for i in range(2)]
    acc_off = soff + 2 * TGK * 2 + 2048
    accs_k = [(carve(acc_off + i * 2048, 2048, F32), ("acc", i)) for i in range(TGK // 128)]
    load_bcast(gbc_k, g_mix, "gbc")
    kring = Ring(kst); vring = Ring(vst)

    def kv_norm(kg, tb):
        norm_block(x_all, kg * TGK + tb * 128, gbc_k, xb_k, ("xb", 0), xn_k, ss_k, tb, ATs[kg % 2], key_at=("ATK", kg % 2))

    for tb in range(TGK // 128):
        kv_norm(0, tb)
    for p_ in range(2 * D // 512):
        pool_extra.append(lambda p_=p_: P.add("pool", lambda e: e.dma_start(out=WKV[:, p_ * 512:(p_ + 1) * 512], in_=w_in[:, 3 * D + p_ * 512:3 * D + (p_ + 1) * 512]),
                                             writes=[("WKV", p_)], dma_key=("WKV", p_ % 4)))
    wkv_reads = lambda c0: [("WKV", c0 // 512)]
    for kg in range(NGK):
        pending = [(kg + 1, tb) for tb in range(TGK // 128)] if kg + 1 < NGK else []

        def epi_k(ch, ts, ps, pkey, kg=kg, state={}):
            if ts == 0:
                state["cur"] = kring.next()
            stg, skey = state["cur"]
            copy_any(stg[:, ts * 512:(ts + 1) * 512], ps[:], [pkey], [(skey, ts)])
            if ts == TGK // 512 - 1:
                P.add("sp", lambda e: e.dma_start(out=KT[ch * 128:(ch + 1) * 128, kg * TGK:(kg + 1) * TGK], in_=stg),
                      reads=[(skey, t_) for t_ in range(TGK // 512)], dma_key=("st", skey))

        def hook(ng, pending=pending):
            if ng % 4 == 1 and pending:
                kv_norm(*pending.pop(0))

        if kg == 0:
            gemm_fm(w_in, 3 * D, D, ATs[kg % 2], TGK, epi_k, key_at=("ATK", kg % 2), hook=hook)
        else:
            gemm_fm(WKV, 0, D, ATs[kg % 2], TGK, epi_k, key_at=("ATK", kg % 2), hook=hook, wreads=wkv_reads)
        while pending:
            kv_norm(*pending.pop(0))

        def epi_v(pi, npc, tb, nt, ps, pkey, kg=kg):
            acc, akey = accs_k[tb]
            if pi == 0 and npc > 1:
                copy_any(acc, ps[:], [pkey], [akey])
                return
            stg, skey = vring.next()
            if npc > 1:
                P.add("dve", lambda e: e.tensor_tensor(out=stg, in0=ps[:], in1=acc, op=ALU.add), reads=[pkey, akey], writes=[skey])
            else:
                copy_any(stg, ps[:], [pkey], [skey])
            r0 = kg * TGK + tb * 128
            P.add("sp", lambda e: e.dma_start(out=VV[r0:r0 + 128, nt * 512:(nt + 1) * 512], in_=stg), reads=[skey], dma_key=("st", skey))

        if kg == 0:
            gemm_tm(w_in, KC, 4 * D, D, ATs[kg % 2], list(range(TGK // 128)), epi_v, key_at=("ATK", kg % 2))
            while pool_extra:
                pool_extra.pop(0)()
        else:
            gemm_tm(WKV, KC, D, D, ATs[kg % 2], list(range(TGK // 128)), epi_v, key_at=("ATK", kg % 2), wreads=wkv_reads)
    P.barrier()

    NTB = TG // 128
    def do_group(tg):
        tok0 = tg * TG
        AT = carve(0, KC * TG * 2, BF16).rearrange("p (c t) -> p c t", c=KC)
        off_n = KC * TG * 2
        norm_phase(x_own, tok0, NTB, g_mix, AT, off_n)
        P.barrier()
        gvs = [carve(off_n + i * 4 * D, 4 * D, F32) for i in range(2)]
        nbf = carve(off_n + 8 * D, 2 * D, BF16)
        mts = carve(off_n + 10 * D, 2 * D, BF16).rearrange("p (c t) -> p c t", c=KC)
        soff = off_n + 12 * D
        stat = carve(soff, 256, F32)
        bnst = carve(soff + 256, (D // 512) * 6 * 4, F32).rearrange("p (n s) -> p n s", s=6)
        soff2 = soff + 256 + (D // 512) * 24
        soff2 = (soff2 + 63) // 64 * 64
        accs = [(carve(soff2 + i * 2048, 2048, F32), ("acc", i)) for i in range(2)]
        soff3 = off_n
        for sg in range(NTB // 2):
            def epi_gv(pi, npc, tb, nt, ps, pkey, sg=sg):
                li = tb - 2 * sg
                acc, akey = accs[li]
                if pi == 0 and npc > 1:
                    copy_any(acc, ps[:], [pkey], [akey])
                    return
                src = ps[:]
                rd = [pkey]
                if npc > 1:
                    P.add("dve", lambda e: e.tensor_tensor(out=acc, in0=ps[:], in1=acc, op=ALU.add), reads=[pkey, akey], writes=[akey])
                    src = acc
                    rd = [akey]
                P.add("act", lambda e: e.activation(out=gvs[li][:, nt * 512:(nt + 1) * 512], in_=src, func=AF.Gelu), reads=rd, writes=[("gv", li, nt)])

            gemm_tm(w_in, KC, D, D, AT, [2 * sg, 2 * sg + 1], epi_gv)
            for li in range(2):
                tb = 2 * sg + li
                gv = gvs[li]
                for n_ in range(D // 512):
                    P.add("dve", lambda e, n_=n_, gv=gv: e.bn_stats(out=bnst[:, n_, :], in_=gv[:, n_ * 512:(n_ + 1) * 512]),
                          reads=[("gv", li, n_)], writes=["bnst"])
                P.add("dve", lambda e: e.bn_aggr(out=stat[:, 0:2], in_=bnst.rearrange("p n s -> p (n s)")), reads=["bnst"], writes=["st01"])
                P.add("act", lambda e: e.activation(out=stat[:, 2:3], in_=stat[:, 1:2], func=AF.Sqrt, bias=epsc[:, 1:2], scale=1.0), reads=["st01"], writes=["st2"])
                P.add("dve", lambda e: e.reciprocal(out=stat[:, 3:4], in_=stat[:, 2:3]), reads=["st2"], writes=["st3"])
                P.add("dve", lambda e, gv=gv: e.tensor_scalar(out=nbf, in0=gv, scalar1=stat[:, 0:1], scalar2=stat[:, 3:4], op0=ALU.subtract, op1=ALU.mult),
                      reads=[("gv", li, n_) for n_ in range(D // 512)] + ["st01", "st3"], writes=["nbf"])
                for c0 in range(0, KC, 4):
                    ps, pkey = gb.next()
                    for c in range(4):
                        cc = c0 + c
                        P.add("pe", lambda e, ps=ps, c=c, cc=cc: e.matmul(ps[:, c * 128:(c + 1) * 128], nbf[:, cc * 128:(cc + 1) * 128], wT[:, cc // 2, :], start=True, stop=True),
                              reads=["nbf", "wT"], writes=[pkey])
                    for c in range(4):
                        cc = c0 + c
                        P.add("dve", lambda e, ps=ps, c=c, cc=cc: e.scalar_tensor_tensor(out=mts[:, cc, :], in0=ps[:, c * 128:(c + 1) * 128], scalar=cols[:, 0, cc:cc + 1], in1=B2[:, cc, :],
                                                                                      op0=ALU.mult, op1=ALU.add), reads=[pkey, "cols", "B2"], writes=["mts"])
                c0t = tok0 + tb * 128
                P.add("sp", lambda e, c0t=c0t: e.dma_start(out=MT[:, c0t:c0t + 128].rearrange("(c p) t -> p c t", p=128), in_=mts), reads=["mts"], dma_key=("st", "mts"))
        P.barrier()
        fst = [(carve(soff3 + i * TG * 2, TG * 2, BF16), ("fst", i)) for i in range(3)]
        mtl = [(carve(soff3 + 3 * TG * 2 + i * 1024, 1024, BF16), ("mtl", i)) for i in range(3)]
        gtmp = [(carve(soff3 + 3 * TG * 2 + 3072 + i * 2048, 2048, F32), ("gtmp", i)) for i in range(2)]
        fring = Ring(fst); mring = Ring(mtl); tring = Ring(gtmp)

        def fm_store(dst):
            state = {}

            def put(ch, ts, producer):
                if ts == 0:
                    state["cur"] = fring.next()
                stg, skey = state["cur"]
                producer(stg[:, ts * 512:(ts + 1) * 512], (skey, ts))
                if ts == TG // 512 - 1:
                    P.add("sp", lambda e: e.dma_start(out=dst[ch * 128:(ch + 1) * 128, tok0:tok0 + TG], in_=stg),
                          reads=[(skey, t_) for t_ in range(TG // 512)], dma_key=("st", skey))
            return put

        put_u = fm_store(YAT)

        def epi_u(ch, ts, ps, pkey):
            ml, mkey = mring.next()
            tt, tkey = tring.next()
            c0t = tok0 + ts * 512
            P.add("sp", lambda e: e.dma_start(out=ml, in_=MT[ch * 128:(ch + 1) * 128, c0t:c0t + 512]), writes=[mkey], dma_key=("ld", mkey))
            P.add("act", lambda e: e.activation(out=tt, in_=ps[:], func=AF.Gelu), reads=[pkey], writes=[tkey])
            put_u(ch, ts, lambda o, skey: P.add("dve", lambda e: e.tensor_tensor(out=o, in0=tt, in1=ml, op=ALU.mult), reads=[tkey, mkey], writes=[skey]))

        gemm_fm(w_in, 0, D, AT, TG, epi_u)
        put_q = fm_store(QT)

        def epi_q(ch, ts, ps, pkey):
            put_q(ch, ts, lambda o, skey: copy_any(o, ps[:], [pkey], [skey]))

        gemm_fm(w_in, 2 * D, D, AT, TG, epi_q)
        for a, dst in ((0, GA), (1, GB)):
            put_g = fm_store(dst)

            def epi_g(ch, ts, ps, pkey, a=a, put_g=put_g):
                put_g(ch, ts, lambda o, skey: P.add("act", lambda e: e.activation(out=o, in_=ps[:], func=AF.Sigmoid, bias=cols[:, 2 + a, ch:ch + 1], scale=1.0),
                                                    reads=[pkey, "cols"], writes=[skey]))

            gemm_fm(w_in, (5 + a) * D, D, AT, TG, epi_g)
        P.barrier()

        o = [0]

        def al(nbytes, dt):
            a = carve(o[0], nbytes, dt)
            o[0] += (nbytes + 63) // 64 * 64
            return a

        qbuf = [(al(2 * TG * 2, BF16).rearrange("p (m t) -> p m t", m=2), ("qb", i)) for i in range(2)]
        kbuf = [(al(CH * 128 * 2, BF16), ("kb", i)) for i in range(3)]
        vbuf = [(al(CH * 258 * 2, BF16).rearrange("p (c e) -> p c e", e=258), ("vb", i)) for i in range(3)]
        pbuf = [(al(1024, BF16), ("pt", i)) for i in range(4)]
        o1n = al(4 * 256 * 4, F32).rearrange("p (i e) -> p i e", i=4)
        ofin = [(al(1024, F32), ("of", i)) for i in range(2)]
        ybf = [(al(512, BF16), ("yb", i)) for i in range(2)]
        ybT = [(al(2 * 512 * 2, BF16).rearrange("p (a t) -> p a t", a=2), ("ybT", i)) for i in range(2)]
        sm = al(64 * 4, F32)
        sq = al(1024, F32)
        for vb, vkey in vbuf:
            P.add("dve", lambda e, vb=vb: e.memset(vb[:, :, 256:257], 1.0), writes=[vkey])
        qring = Ring(qbuf); kring = Ring(kbuf); vring = Ring(vbuf); pring = Ring(pbuf)
        ofr = Ring(ofin); ybr = Ring(ybf); ybTr = Ring(ybT)
        sring = Ring([(psf[4], ("PS", 4)), (psf[5], ("PS", 5))])
        smi = [0]
        rows = []
        for h in range(H):
            for qt in range(TG // 512):
                m0 = tg * NTB + 4 * qt
                nkb = 4 * m0 + 16
                for mp in range(2):
                    for kb in range(nkb):
                        i0 = max(0, (kb - 4 * m0) // 4)
                        i1 = 4
                        if DSKIP[h] is not None:
                            i1 = min(4, max(0, (DSKIP[h] + kb - 4 * m0 + 3) // 4))
                        if i1 <= i0:
                            continue
                        rows.append(dict(h=h, qt=qt, mp=mp, kb=kb, m0=m0, nkb=nkb, i0=i0, i1=i1))
        cur = {}

        def emit_S(row):
            h, qt, mp, kb, m0, nkb = row["h"], row["qt"], row["mp"], row["kb"], row["m0"], row["nkb"]
            if cur.get("qh") != h:
                cur["qh"] = h
                qb, qkey = qring.next()
                P.add("sp", lambda e: e.dma_start(out=qb, in_=QT[h * 256:(h + 1) * 256, tok0:tok0 + TG].rearrange("(m p) t -> p m t", p=128)),
                      writes=[qkey], dma_key=("ld", qkey))
                cur["q"] = (qb, qkey)
            if cur.get("unit") != (h, qt):
                cur["unit"] = (h, qt)
                cur["yT"] = ybTr.next()
            if cur.get("chunk") != (h, qt, mp, kb // CH):
                cur["chunk"] = (h, qt, mp, kb // CH)
                kc0 = (kb // CH) * CH
                l0 = kb - kc0
                nk = min(CH, nkb - kc0)
                kb_, kkey = kring.next()
                P.add("sp", lambda e: e.dma_start(
                    out=kb_[:, l0 * 128:nk * 128], in_=KT[h * 256 + mp * 128:h * 256 + (mp + 1) * 128, kb * 128:(kc0 + nk) * 128]),
                    writes=[kkey], dma_key=("ld", kkey))
                vb, vkey = vring.next()
                P.add("sp", lambda e: e.dma_start(
                    out=vb[:, l0:nk, 0:256], in_=VV[kb * 128:(kc0 + nk) * 128, h * 256:(h + 1) * 256].rearrange("(c p) e -> p c e", p=128)),
                    writes=[vkey], dma_key=("ld", vkey))
                cur["k"] = (kb_, kkey)
                cur["v"] = (vb, vkey)
            row["q"] = cur["q"]; row["k"] = cur["k"]; row["v"] = cur["v"]; row["yT"] = cur["yT"]
            qb, qkey = row["q"]
            kb_, kkey = row["k"]
            kl = kb % CH
            i0, i1 = row["i0"], row["i1"]
            sps, skey = sring.next()
            row["s"] = (sps, skey)
            P.add("pe", lambda e: e.matmul(
                sps[:, i0 * 128:i1 * 128], kb_[:, kl * 128:(kl + 1) * 128], qb[:, mp, qt * 512 + i0 * 128:qt * 512 + i1 * 128], start=True, stop=True),
                reads=[kkey, qkey], writes=[skey])

        def emit_rest(row):
            h, qt, mp, kb, m0, nkb = row["h"], row["qt"], row["mp"], row["kb"], row["m0"], row["nkb"]
            sps, skey = row["s"]
            vb, vkey = row["v"]
            i0, i1 = row["i0"], row["i1"]
            kl = kb % CH
            pt, pkey_ = pring.next()
            if WIDE[h]:
                cw = 4 * m0 - kb + 15
                P.add("act", lambda e: e.activation(
                    out=pt[:, i0 * 128:i1 * 128], in_=sps[:, i0 * 128:i1 * 128], func=AF.Exp, bias=tabw[:, WIDX[h], cw:cw + 1], scale=SCALE),
                    reads=[skey], writes=[(pkey_, i) for i in range(i0, i1)])
            for i in range(i0, i1):
                dl = 4 * (m0 + i) - kb
                if not WIDE[h]:
                    P.add("act", lambda e, i=i, dl=dl: e.activation(
                        out=pt[:, i * 128:(i + 1) * 128], in_=sps[:, i * 128:(i + 1) * 128], func=AF.Exp, bias=tab[:, h, dl + 3:dl + 4], scale=SCALE),
                        reads=[skey], writes=[(pkey_, i)])
                r = -dl
                if 0 <= r <= 3:
                    P.add("dve", lambda e, i=i, r=r: e.tensor_tensor(out=pt[:, i * 128:(i + 1) * 128], in0=pt[:, i * 128:(i + 1) * 128], in1=masks[:, r, :], op=ALU.mult),
                          reads=[(pkey_, i)], writes=[(pkey_, i)])
                kfirst = 0 if DSKIP[h] is None else max(0, 4 * (m0 + i) - DSKIP[h] + 1)
                P.add("pe", lambda e, i=i, first=(kb == kfirst), last=(kb == 4 * (m0 + i) + 3): e.matmul(
                    psf[i][:, 0:257], pt[:, i * 128:(i + 1) * 128], vb[:, kl, 0:257], start=first, stop=last),
                    reads=[(pkey_, i), vkey], writes=[("PS", i)])
            if kb != nkb - 1:
                return
            yT, yTkey = row["yT"]
            for i in range(4):
                s0 = (smi[0] % 8) * 8
                smi[0] += 1
                rd = sm[:, s0:s0 + 1]
                P.add("dve", lambda e, i=i, rd=rd: e.reciprocal(out=rd, in_=psf[i][:, 256:257]), reads=[("PS", i)], writes=[("sm", s0)])
                if mp == 0:
                    P.add("dve", lambda e, i=i, rd=rd: e.tensor_scalar(out=o1n[:, i, :], in0=psf[i][:, 0:256], scalar1=rd, scalar2=None, op0=ALU.mult),
                          reads=[("PS", i), ("sm", s0)], writes=[("o1n", i)])
                    continue
                of, okey = ofr.next()
                yb, ykey = ybr.next()
                P.add("dve", lambda e, rd=rd, s0=s0: e.tensor_scalar(out=sm[:, s0 + 1:s0 + 2], in0=rd, scalar1=neglam, scalar2=None, op0=ALU.mult),
                      reads=[("sm", s0), "neglam"], writes=[("sm", s0 + 1)])
                P.add("dve", lambda e, i=i, of=of, s0=s0: e.scalar_tensor_tensor(out=of, in0=psf[i][:, 0:256], scalar=sm[:, s0 + 1:s0 + 2], in1=o1n[:, i, :],
                                                                              op0=ALU.mult, op1=ALU.add), reads=[("PS", i), ("sm", s0 + 1), ("o1n", i)], writes=[okey])
                P.add("act", lambda e, of=of, s0=s0: e.activation(out=sq, in_=of, func=AF.Square, accum_out=sm[:, s0 + 2:s0 + 3]), reads=[okey], writes=["sq", ("sm", s0 + 2)])
                P.add("act", lambda e, s0=s0: e.activation(out=sm[:, s0 + 3:s0 + 4], in_=sm[:, s0 + 2:s0 + 3], func=AF.Sqrt, bias=epsc[:, 1:2], scale=1.0 / 256),
                      reads=[("sm", s0 + 2)], writes=[("sm", s0 + 3)])
                P.add("dve", lambda e, s0=s0: e.reciprocal(out=sm[:, s0 + 4:s0 + 5], in_=sm[:, s0 + 3:s0 + 4]), reads=[("sm", s0 + 3)], writes=[("sm", s0 + 4)])
                P.add("dve", lambda e, of=of, yb=yb, s0=s0: e.scalar_tensor_tensor(out=yb, in0=of, scalar=sm[:, s0 + 4:s0 + 5], in1=subg, op0=ALU.mult, op1=ALU.mult),
                      reads=[okey, ("sm", s0 + 4), "subg"], writes=[ykey])
                tp, tkey = tpr.next()
                for a in range(2):
                    P.add("pe", lambda e, tp=tp, a=a, yb=yb: e.transpose(out=tp[:, a * 128:(a + 1) * 128], in_=yb[:, a * 128:(a + 1) * 128], identity=ident_b),
                          reads=[ykey], writes=[tkey])
                copy_any(yT[:, :, i * 128:(i + 1) * 128], tp[:, 0:256].rearrange("p (a t) -> p a t", a=2), [tkey], [(yTkey, i)])
            if mp == 1:
                c0t = tok0 + qt * 512
                P.add("sp", lambda e: e.dma_start(out=YBT[h * 256:(h + 1) * 256, c0t:c0t + 512].rearrange("(a p) t -> p a t", p=128), in_=yT),
                      reads=[(yTkey, i_) for i_ in range(4)], dma_key=("st", yTkey))

        emit_S(rows[0])
        for ri, row in enumerate(rows):
            if ri + 1 < len(rows):
                emit_S(rows[ri + 1])
            emit_rest(row)
        P.barrier()

        AT = carve(0, KC * TG * 2, BF16).rearrange("p (c t) -> p c t", c=KC)
        soff = KC * TG * 2
        gl = [(carve(soff + i * 1024, 1024, BF16), ("gl", i)) for i in range(3)]
        tl = [(carve(soff + 3072 + i * 2048, 2048, F32), ("tl", i)) for i in range(3)]
        tmpm = [(carve(soff + 3072 + 6144 + i * 2048, 2048, F32), ("tmpm", i)) for i in range(2)]
        fst = [(carve(soff + 3072 + 6144 + 4096 + i * TG * 2, TG * 2, BF16), ("fst", i)) for i in range(3)]
        glr = Ring(gl); tlr = Ring(tl); tmr = Ring(tmpm); fring = Ring(fst)
        load_AT(AT, YAT, KC, tok0, TG)

        def epi_a(ch, ts, ps, pkey):
            g_, gkey = glr.next()
            t_, tkey = tlr.next()
            c0t = tok0 + ts * 512
            P.add("sp", lambda e: e.dma_start(out=g_, in_=GA[ch * 128:(ch + 1) * 128, c0t:c0t + 512]), writes=[gkey], dma_key=("ld", gkey))
            P.add("dve", lambda e: e.tensor_tensor(out=t_, in0=ps[:], in1=g_, op=ALU.mult), reads=[pkey, gkey], writes=[tkey])
            P.add("sp", lambda e: e.dma_start(out=T1[ch * 128:(ch + 1) * 128, c0t:c0t + 512], in_=t_), reads=[tkey], dma_key=("st", tkey))

        gemm_fm(w_br_a, 0, D, AT, TG, epi_a)
        P.barrier()
        load_AT(AT, YBT, KC, tok0, TG)
        put_m = fm_store(MGT)

        def epi_b(ch, ts, ps, pkey):
            g_, gkey = glr.next()
            t_, tkey = tlr.next()
            m_, mkey = tmr.next()
            c0t = tok0 + ts * 512
            P.add("sp", lambda e: e.dma_start(out=g_, in_=GB[ch * 128:(ch + 1) * 128, c0t:c0t + 512]), writes=[gkey], dma_key=("ld", gkey))
            P.add("sp", lambda e: e.dma_start(out=t_, in_=T1[ch * 128:(ch + 1) * 128, c0t:c0t + 512]), writes=[tkey], dma_key=("ld", tkey))
            P.add("dve", lambda e: e.tensor_tensor(out=m_, in0=ps[:], in1=g_, op=ALU.mult), reads=[pkey, gkey], writes=[mkey])
            put_m(ch, ts, lambda o_, skey: P.add("dve", lambda e: e.tensor_tensor(out=o_, in0=m_, in1=t_, op=ALU.add), reads=[mkey, tkey], writes=[skey]))

        gemm_fm(w_br_b, 0, D, AT, TG, epi_b)
        P.barrier()
        load_AT(AT, MGT, KC, tok0, TG)

        def tm_residual(res_src, dst, accs, xl, ost, post=None):
            xlr = Ring(xl); osr = Ring(ost)

            def epi(pi, npc, tb, nt, ps, pkey):
                acc, akey = accs[tb % len(accs)]
                r0 = tok0_cur[0] + tb * 128
                if pi == 0:
                    x_, xkey = xlr.next()
                    P.add("sp", lambda e: e.dma_start(out=x_, in_=res_src[r0:r0 + 128, nt * 512:(nt + 1) * 512]), writes=[xkey], dma_key=("ld", xkey))
                    tgt, tk = (acc, akey) if npc > 1 else osr.next()
                    P.add("dve", lambda e: e.tensor_tensor(out=tgt, in0=ps[:], in1=x_, op=ALU.add), reads=[pkey, xkey], writes=[tk])
                    if npc > 1:
                        return
                elif pi < npc - 1:
                    P.add("dve", lambda e: e.tensor_tensor(out=acc, in0=ps[:], in1=acc, op=ALU.add), reads=[pkey, akey], writes=[akey])
                    return
                else:
                    tgt, tk = osr.next()
                    P.add("dve", lambda e: e.tensor_tensor(out=tgt, in0=ps[:], in1=acc, op=ALU.add), reads=[pkey, akey], writes=[tk])
                P.add("sp", lambda e: e.dma_start(out=dst[r0:r0 + 128, nt * 512:(nt + 1) * 512], in_=tgt), reads=[tk], dma_key=("st", tk))
            return epi

        tok0_cur = [tok0]
        a0 = soff
        accs = [(carve(a0 + i * 2048, 2048, F32), ("acc", i)) for i in range(NTB)]
        xl = [(carve(a0 + NTB * 2048 + i * 2048, 2048, F32), ("xl", i)) for i in range(3)]
        ost = [(carve(a0 + NTB * 2048 + 6144 + i * 2048, 2048, F32), ("ost", i)) for i in range(3)]
        gemm_tm(w_o, KC, 0, D, AT, list(range(NTB)), tm_residual(x_own, X1, accs, xl, ost))
        P.barrier()

        norm_phase(X1, tok0, NTB, g_ffn, AT, off_n)
        P.barrier()
        fst = [(carve(off_n + i * TG * 2, TG * 2, BF16), ("fst", i)) for i in range(3)]
        stmp = [(carve(off_n + 3 * TG * 2 + i * 2048, 2048, F32), ("stmp", i)) for i in range(3)]
        fring = Ring(fst); sring2 = Ring(stmp)
        put_f = fm_store(FFA)
        for pg in range(DFF // 256):
            wg, wgk = wload(wview_fm(w_gu, pg * 256, 256), KC, 256)
            wu, wuk = wload(wview_fm(w_gu, DFF + pg * 256, 256), KC, 256)
            for c2 in range(2):
                for ts in range(TG // 512):
                    pg_, pgk = gb.next()
                    pu_, puk = gb.next()
                    for (ps, pk, wv, wk) in ((pg_, pgk, wg, wgk), (pu_, puk, wu, wuk)):
                        for k in range(KC):
                            rk = [("AT", ts * 4 + i, k // 8) for i in range(4)]
                            P.add("pe", lambda e, ps=ps, wv=wv, k=k, c2=c2, ts=ts: e.matmul(
                                ps[:], wv[:, k, c2 * 128:(c2 + 1) * 128], AT[:, k, ts * 512:(ts + 1) * 512], start=(k == 0), stop=(k == KC - 1)),
                                reads=[wk] + rk, writes=[pk])
                    s_, sk = sring2.next()
                    P.add("act", lambda e, s_=s_, pg_=pg_: e.activation(out=s_, in_=pg_[:], func=AF.Silu), reads=[pgk], writes=[sk])
                    put_f(pg * 2 + c2, ts, lambda o_, skey, s_=s_, sk=sk, pu_=pu_, puk=puk: P.add(
                        "dve", lambda e: e.tensor_tensor(out=o_, in0=pu_[:], in1=s_, op=ALU.mult), reads=[puk, sk], writes=[skey]))
        P.barrier()

        ATd = carve(0, FC * TGD * 2, BF16).rearrange("p (c t) -> p c t", c=FC)
        a0 = FC * TGD * 2
        nd = TGD // 128
        accs = [(carve(a0 + i * 2048, 2048, F32), ("acc", i)) for i in range(nd)]
        xl = [(carve(a0 + nd * 2048 + i * 2048, 2048, F32), ("xl", i)) for i in range(3)]
        ost = [(carve(a0 + nd * 2048 + 6144 + i * 2048, 2048, F32), ("ost", i)) for i in range(3)]
        for sd in range(TG // TGD):
            tok0_cur[0] = tok0 + sd * TGD
            load_AT(ATd, FFA, FC, tok0_cur[0], TGD, nparts=6)
            gemm_tm(w_down, FC, 0, D, ATd, list(range(nd)), tm_residual(X1, X2, accs, xl, ost))
            P.barrier()
        tok0_cur[0] = tok0

        norm_phase(X2, tok0, NTB, g_ple, AT, off_n)
        pT = carve(off_n, PC * TG * 2, BF16).rearrange("p (c t) -> p c t", c=PC)
        pl = carve(off_n + PC * TG * 2, PLE * 4, F32)
        plb = carve(off_n + PC * TG * 2 + PLE * 4, PLE * 2, BF16)
        P.barrier()
        for tb in range(NTB):
            r0 = tok0 + tb * 128
            P.add("sp", lambda e, r0=r0: e.dma_start(out=pl, in_=p_own[r0:r0 + 128, :]), writes=["pl"], dma_key=("ld", "pl"))
            P.add("dve", lambda e: e.tensor_copy(out=plb, in_=pl), reads=["pl"], writes=["plb"])
            tp, tkey = tpr.next()
            for c in range(PC):
                P.add("pe", lambda e, tp=tp, c=c: e.transpose(out=tp[:, c * 128:(c + 1) * 128], in_=plb[:, c * 128:(c + 1) * 128], identity=ident_b),
                      reads=["plb"], writes=[tkey])
            copy_any(pT[:, :, tb * 128:(tb + 1) * 128], tp[:, 0:PC * 128].rearrange("p (c t) -> p c t", c=PC), [tkey], [("pT", tb)])
        a0 = off_n + PC * TG * 2 + PLE * 6
        a0 = (a0 + 63) // 64 * 64
        accs = [(carve(a0 + i * 2048, 2048, F32), ("acc", i)) for i in range(NTB)]
        xl = [(carve(a0 + NTB * 2048 + i * 2048, 2048, F32), ("xl", i)) for i in range(3)]
        ost = [(carve(a0 + NTB * 2048 + 6144 + i * 2048, 2048, F32), ("ost", i)) for i in range(3)]
        sg_ = [(carve(a0 + NTB * 2048 + 12288 + i * 2048, 2048, F32), ("sg", i)) for i in range(2)]
        xlr = Ring(xl); osr = Ring(ost); sgr = Ring(sg_)
        ppr = Ring([(psf[4], ("PS", 4)), (psf[5], ("PS", 5))])
        wpp_cur = {}

        def epi_p(pi, npc, tb, nt, ps, pkey):
            acc, akey = accs[tb]
            if pi == 0 and npc > 1:
                copy_any(acc, ps[:], [pkey], [akey])
                return
            src = ps[:]
            rd = [pkey]
            if npc > 1:
                P.add("dve", lambda e: e.tensor_tensor(out=acc, in0=ps[:], in1=acc, op=ALU.add), reads=[pkey, akey], writes=[akey])
                src = acc
                rd = [akey]
            s_, sk = sgr.next()
            P.add("act", lambda e: e.activation(out=s_, in_=src, func=AF.Sigmoid), reads=rd, writes=[sk])
            if wpp_cur.get("nt") != nt:
                wpp_cur["nt"] = nt
                wpp_cur["w"] = wload(wview_tm(w_pp, 0, PC, nt * 512, 512), PC, 512)
            wv, wkey = wpp_cur["w"]
            pp, ppk = ppr.next()
            for c in range(PC):
                P.add("pe", lambda e, pp=pp, c=c, wv=wv: e.matmul(pp[:], pT[:, c, tb * 128:(tb + 1) * 128], wv[:, c, :], start=(c == 0), stop=(c == PC - 1)),
                      reads=[wkey, ("pT", tb)], writes=[ppk])
            x_, xkey = xlr.next()
            r0 = tok0 + tb * 128
            P.add("sp", lambda e: e.dma_start(out=x_, in_=X2[r0:r0 + 128, nt * 512:(nt + 1) * 512]), writes=[xkey], dma_key=("ld", xkey))
            P.add("dve", lambda e: e.tensor_tensor(out=s_, in0=pp[:], in1=s_, op=ALU.mult), reads=[ppk, sk], writes=[sk])
            o_, ok = osr.next()
            P.add("dve", lambda e: e.tensor_tensor(out=o_, in0=s_, in1=x_, op=ALU.add), reads=[sk, xkey], writes=[ok])
            P.add("sp", lambda e: e.dma_start(out=X3[r0:r0 + 128, nt * 512:(nt + 1) * 512], in_=o_), reads=[ok], dma_key=("st", ok))

        gemm_tm(w_pg, KC, 0, D, AT, list(range(NTB)), epi_p)
        P.barrier()

        xbs = [(carve(i * 4 * D, 4 * D, F32), ("xb", i)) for i in range(2)]
        gbc = carve(8 * D, 4 * D, F32)
        xn = carve(12 * D, 2 * D, BF16)
        ss = carve(14 * D, 16, F32)
        fos = [(carve(14 * D + 64 + i * 4 * D, 4 * D, F32), ("fo", i)) for i in range(2)]
        load_bcast(gbc, g_final, "gbc")
        finals = []
        for tb in range(NTB):
            r0 = tok0 + tb * 128
            xb, xbkey = xbs[tb % 2]
            fo, fokey = fos[tb % 2]
            finals.append(norm_block(X3, r0, gbc, xb, xbkey, xn, ss, tb, None, final_out=(fo, fokey, out_d[r0:r0 + 128, :])))
        P.barrier()

    for tg_ in range(NO // TG):
        do_group(tg_)

    P.emit(final_waits=[op for op in P.dma_last.values() if not (isinstance(op.dma_key, tuple) and op.dma_key[0] == "W")])
    st.close()
    return nc


def alibi_slopes(n_heads):
    return [float(np.exp2(np.float32(-8.0) * np.float32(i) / np.float32(n_heads))) for i in range(1, n_heads + 1)]


def make_in_maps(cfg, inputs):
    D = cfg["D"]; S = cfg["S"]
    x = np.asarray(inputs["x"], dtype=np.float32)
    p = np.asarray(inputs["p"], dtype=np.float32)[0]
    B = x.shape[0]
    NBA = S // 128
    NOB = NBA // 4

    def w(name, shape=None):
        a = np.ascontiguousarray(np.asarray(inputs[name], dtype=np.float32))
        return a.reshape(shape) if shape is not None else a

    G = D // 256
    shared = {
        "g_mix": w("g_mix", (1, D)), "w_in": w("w_in", (D, 7 * D)), "b_gate": w("b_gate", (2, D)),
        "ln_v_g": w("ln_v_g", (1, D)), "ln_v_b": w("ln_v_b", (1, D)),
        "w_s": w("w_s", (G, 128, 128)), "b_s": w("b_s", (1, G * 128)),
        "lambda_q1": w("lambda_q1", (1, 128)), "lambda_k1": w("lambda_k1", (1, 128)),
        "lambda_q2": w("lambda_q2", (1, 128)), "lambda_k2": w("lambda_k2", (1, 128)),
        "subln_g": w("subln_g", (1, 256)),
        "w_br_a": w("w_br_a", (D, D)), "w_br_b": w("w_br_b", (D, D)), "w_o": w("w_o", (D, D)),
        "g_ffn": w("g_ffn", (1, D)), "w_gu": w("w_gu", (D, 2 * cfg["DFF"])), "w_down": w("w_down", (cfg["DFF"], D)),
        "g_ple": w("g_ple", (1, D)), "w_ple_gate": w("w_ple_gate", (D, D)), "w_ple_proj": w("w_ple_proj", (cfg["PLE"], D)),
        "g_final": w("g_final", (1, D)),
    }
    maps = []
    for c in range(4 * B):
        b, j = c // 4, c % 4
        xb = x[b].reshape(NBA, 128, D)
        pb = p[b].reshape(NBA, 128, -1)
        m = dict(shared)
        m["x_all"] = np.ascontiguousarray(x[b])
        m["x_own"] = np.ascontiguousarray(xb[j::4].reshape(NOB * 128, D))
        m["p_own"] = np.ascontiguousarray(pb[j::4].reshape(NOB * 128, -1))
        m["jv"] = np.full((128, 1), float(j), dtype=np.float32)
        maps.append(m)
    return maps


def assemble(cfg, results, B):
    D = cfg["D"]; S = cfg["S"]
    NBA = S // 128
    NOB = NBA // 4
    out = np.empty((B, NBA, 128, D), dtype=np.float32)
    for c in range(4 * B):
        b, j = c // 4, c % 4
        out[b, j::4] = np.asarray(results[c]["out"], dtype=np.float32).reshape(NOB, 128, D)
    return out.reshape(B, S, D)


FULL_CFG = {"D": 4096, "S": 8192, "DFF": 11008, "PLE": 256, "slopes": alibi_slopes(16)}


def kernel(**inputs):
    cfg = FULL_CFG
    nc = build(cfg)
    maps = make_in_maps(cfg, inputs)
    res = run_bass_kernel_spmd(nc, maps, core_ids=list(range(8)))
    return assemble(cfg, res.results, 2)
```

```python
import contextlib
import math
import numpy as np
import concourse.bass as bass
import concourse.mybir as mybir
from concourse.bass_utils import run_bass_kernel_spmd

F32 = mybir.dt.float32
BF16 = mybir.dt.bfloat16
I32 = mybir.dt.int32
AF = mybir.ActivationFunctionType
ALU = mybir.AluOpType
AX = mybir.AxisListType

RMS_EPS = 1e-6
LN_EPS = 1e-5
HEAD_DIM = 128
NEG_BIG = 30000.0


class Op:
    __slots__ = ("eng", "fn", "deps", "inc", "val", "dma_key", "dma_val")


class Prog:
    ENGS = ("pe", "act", "dve", "pool", "sp")

    def __init__(self, nc):
        self.nc = nc
        self.ops = {e: [] for e in self.ENGS}
        self.last_w = {}
        self.readers = {}
        self.dma_cnt = {}
        self.dma_last = {}
        self.last_real = {e: None for e in self.ENGS}

    def add(self, eng, fn, reads=(), writes=(), dma_key=None, extra_deps=()):
        op = Op()
        op.eng = eng; op.fn = fn
        op.inc = False; op.val = None; op.dma_key = dma_key; op.dma_val = None
        deps = []
        for r in reads:
            w = self.last_w.get(r)
            if w is not None:
                deps.append((w, True))
        for wk in writes:
            w = self.last_w.get(wk)
            if w is not None:
                deps.append((w, False))
            for rd in self.readers.get(wk, ()):
                deps.append((rd, False))
        for d in extra_deps:
            deps.append((d, True))
        if dma_key is not None:
            c = self.dma_cnt.get(dma_key, 0) + 1
            self.dma_cnt[dma_key] = c
            op.dma_val = 16 * c
            prev = self.dma_last.get(dma_key)
            if prev is not None:
                deps.append((prev, True))
            self.dma_last[dma_key] = op
        fd = []
        seen = set()
        for d, raw in deps:
            if d is op or id(d) in seen:
                continue
            if d.dma_key is None and d.eng == eng:
                if eng == "pe" or not raw:
                    continue
            seen.add(id(d))
            fd.append(d)
        op.deps = fd
        for d in fd:
            if d.dma_key is None:
                d.inc = True
        for r in reads:
            self.readers.setdefault(r, []).append(op)
        for wk in writes:
            self.last_w[wk] = op
            self.readers[wk] = []
        self.ops[eng].append(op)
        if fn is not None and dma_key is None:
            self.last_real[eng] = op
        return op

    def barrier(self, engs=("pe", "act", "dve", "sp"), keep=("W",)):
        lasts = [self.last_real[e] for e in self.ENGS if self.last_real[e] is not None and e != "pool"]
        dmas = [op for k, op in self.dma_last.items() if not (isinstance(k, tuple) and k[0] in keep)]
        for e in engs:
            self.add(e, None, extra_deps=lasts + dmas)
        for k in list(self.last_w.keys()):
            if not (isinstance(k, tuple) and k[0] in keep):
                del self.last_w[k]
        for k in list(self.readers.keys()):
            if not (isinstance(k, tuple) and k[0] in keep):
                del self.readers[k]

    def emit(self, final_waits=()):
        nc = self.nc
        for e in self.ENGS:
            c = 0
            for op in self.ops[e]:
                if op.dma_key is None and op.inc:
                    assert op.fn is not None
                    c += 1
                    op.val = c
        with contextlib.ExitStack() as st:
            esem = {e: st.enter_context(nc.semaphore("es_" + e)) for e in self.ENGS}
            dsem = {k: st.enter_context(nc.semaphore("ds_%d" % i)) for i, k in enumerate(self.dma_cnt)}
            block = st.enter_context(nc.Block())

            def run(e, engobj):
                waited = {}
                for op in self.ops[e]:
                    need = {}
                    for d in op.deps:
                        if d.dma_key is not None:
                            s, v = dsem[d.dma_key], d.dma_val
                        else:
                            s, v = esem[d.eng], d.val
                        key = id(s)
                        if waited.get(key, 0) >= v:
                            continue
                        if key not in need or need[key][1] < v:
                            need[key] = (s, v)
                    for key, (s, v) in need.items():
                        engobj.wait_ge(s, v)
                        waited[key] = v
                    if op.fn is None:
                        continue
                    ins = op.fn(engobj)
                    if op.dma_key is not None:
                        ins.then_inc(dsem[op.dma_key], 16)
                    elif op.inc:
                        ins.then_inc(esem[e], 1)
                if e == "sp":
                    for op in final_waits:
                        engobj.wait_ge(dsem[op.dma_key], op.dma_val)

            @block.tensor
            def _(t):
                run("pe", t)

            @block.scalar
            def _(t):
                run("act", t)

            @block.vector
            def _(t):
                run("dve", t)

            @block.gpsimd
            def _(t):
                run("pool", t)

            @block.sync
            def _(t):
                run("sp", t)


class Ring:
    def __init__(self, items):
        self.items = items
        self.i = 0

    def next(self):
        it = self.items[self.i % len(self.items)]
        self.i += 1
        return it


def build(cfg):
    D = cfg["D"]; S = cfg["S"]; DFF = cfg["DFF"]; PLE = cfg["PLE"]
    slopes = [float(s) for s in cfg["slopes"]]
    lam_init = 0.2
    H = D // 256; G = D // 256; KC = D // 128; FC = DFF // 128; PC = PLE // 128
    NBA = S // 128; NOB = NBA // 4; NO = NOB * 128
    TG = min(cfg.get("TG", 1024), NO)
    TGK = 512
    TGD = min(cfg.get("TGD", 512), TG)
    CH = min(16, NBA)
    NTAB = NBA + 3
    NTW = NBA + 16
    DSKIP = []
    for sl in slopes:
        d = 0
        while sl * (128 * d - 63) <= 128.0 and d <= NBA:
            d += 1
        DSKIP.append(d if (d <= NBA and cfg.get("skip", True)) else None)
    WIDE = [bool(cfg.get("wide", True)) and sl * 1663.0 <= 40.0 for sl in slopes]
    WIDX = {h: i for i, h in enumerate([h for h in range(len(slopes)) if WIDE[h]])}
    NW = max(1, len(WIDX))
    SCALE = HEAD_DIM ** -0.5
    assert TG % 512 == 0 and TGK % 512 == 0 and D % 512 == 0 and DFF % 256 == 0

    nc = bass.Bass("TRN2", target_bir_lowering=False)

    def din(name, shape):
        return nc.dram_tensor(name, list(shape), F32, kind="ExternalInput").ap()

    x_all = din("x_all", [S, D]); x_own = din("x_own", [NO, D]); p_own = din("p_own", [NO, PLE])
    jv_d = din("jv", [128, 1])
    g_mix = din("g_mix", [1, D]); w_in = din("w_in", [D, 7 * D]); b_gate = din("b_gate", [2, D])
    ln_v_g = din("ln_v_g", [1, D]); ln_v_b = din("ln_v_b", [1, D])
    w_s = din("w_s", [G, 128, 128]); b_s = din("b_s", [1, G * 128])
    lq1 = din("lambda_q1", [1, 128]); lk1 = din("lambda_k1", [1, 128])
    lq2 = din("lambda_q2", [1, 128]); lk2 = din("lambda_k2", [1, 128])
    subln_g = din("subln_g", [1, 256])
    w_br_a = din("w_br_a", [D, D]); w_br_b = din("w_br_b", [D, D]); w_o = din("w_o", [D, D])
    g_ffn = din("g_ffn", [1, D]); w_gu = din("w_gu", [D, 2 * DFF]); w_down = din("w_down", [DFF, D])
    g_ple = din("g_ple", [1, D]); w_pg = din("w_ple_gate", [D, D]); w_pp = din("w_ple_proj", [PLE, D])
    g_final = din("g_final", [1, D])
    out_d = nc.dram_tensor("out", [NO, D], F32, kind="ExternalOutput").ap()

    def dscr(name, shape, dt):
        if cfg.get("debug"):
            return nc.dram_tensor(name, list(shape), dt, kind="ExternalOutput").ap()
        return nc.dram_tensor(name, list(shape), dt).ap()

    KT = dscr("KT", [D, S], BF16); VV = dscr("VV", [S, D], BF16)
    QT = dscr("QT", [D, NO], BF16); MT = dscr("MT", [D, NO], BF16)
    YAT = dscr("YAT", [D, NO], BF16); YBT = dscr("YBT", [D, NO], BF16)
    GA = dscr("GA", [D, NO], BF16); GB = dscr("GB", [D, NO], BF16)
    T1 = dscr("T1", [D, NO], F32); MGT = dscr("MGT", [D, NO], BF16)
    X1 = dscr("X1", [NO, D], F32); X2 = dscr("X2", [NO, D], F32); X3 = dscr("X3", [NO, D], F32)
    FFA = dscr("FFA", [DFF, NO], BF16)

    P = Prog(nc)
    st = contextlib.ExitStack()
    ARENA_B = 128 * 1024
    arena = st.enter_context(nc.sbuf_tensor("arena", [128, ARENA_B // 2], BF16))
    wring_t = [st.enter_context(nc.sbuf_tensor("wr%d" % i, [128, 8192], BF16)) for i in range(3)]
    CONST_F = 512 + 15 + H * NTAB + NW * NTW + 4 * KC + KC * 128 + 256 + 2 + 3
    cst = st.enter_context(nc.sbuf_tensor("cst", [128, CONST_F], F32))
    cbf = st.enter_context(nc.sbuf_tensor("cbf", [128, 128 + 4 * 128 + G * 128 + 128], BF16))
    psf = [st.enter_context(nc.psum_tensor("psf%d" % i, [128, 512], F32)) for i in range(6)]
    psb = [st.enter_context(nc.psum_tensor("psb%d" % i, [128, 1024], BF16)) for i in range(2)]

    co = [0]

    def cf(n):
        a = cst[:, co[0]:co[0] + n]
        co[0] += n
        return a

    ident_f = cf(128); valf = cf(128); tri_f = cf(128); ones_f = cf(128)
    jv = cf(1); jv128 = cf(1); jsh = cf(4); neglam = cf(1); lamtmp = cf(8)
    tab = cf(H * NTAB).rearrange("p (h c) -> p h c", h=H)
    tabw = cf(NW * NTW).rearrange("p (h c) -> p h c", h=NW)
    cols = cf(4 * KC).rearrange("p (a c) -> p a c", a=4)
    B2 = cf(KC * 128).rearrange("p (c t) -> p c t", c=KC)
    subg = cf(256)
    epsc = cf(2)
    assert co[0] <= CONST_F
    bo = [0]

    def cb(n):
        a = cbf[:, bo[0]:bo[0] + n]
        bo[0] += n
        return a

    ident_b = cb(128)
    masks = cb(4 * 128).rearrange("p (r q) -> p r q", r=4)
    wT = cb(G * 128).rearrange("p (g t) -> p g t", g=G)
    ones_b = cb(128)

    def carve(off, nbytes, dt):
        assert off % 4 == 0 and off + nbytes <= ARENA_B, (off, nbytes)
        a = arena[:, off // 2:(off + nbytes) // 2]
        return a.bitcast(F32) if dt == F32 else a

    wr = Ring([(wring_t[i], ("W", i)) for i in range(3)])

    def wload(src_view, kc, n):
        t, key = wr.next()
        v = t[:].rearrange("p (c n) -> p c n", n=n)[:, 0:kc, :]
        P.add("pool", lambda e, v=v, s=src_view: e.dma_start(out=v, in_=s), writes=[key], dma_key=key)
        return v, key

    def wview_fm(w, c0, n):
        return w[:, c0:c0 + n].rearrange("(c p) n -> p c n", p=128)

    def wview_tm(w, k0, kp, c0, n):
        return w[k0 * 128:(k0 + kp) * 128, c0:c0 + n].rearrange("(c p) n -> p c n", p=128)

    gb = Ring([(psf[i], ("PS", i)) for i in range(4)])

    def gemm_fm(w, col0, ncols, AT, tg, epi, kcn=None, key_at="AT", hook=None):
        kcn = kcn or KC
        for ng in range(ncols // 256):
            wv, wkey = wload(wview_fm(w, col0 + ng * 256, 256), kcn, 256)
            for c2 in range(2):
                for ts in range(tg // 512):
                    ps, pkey = gb.next()
                    for k in range(kcn):
                        rk = [(key_at, ts * 4 + i, k // 8) for i in range(4)]
                        P.add("pe", lambda e, ps=ps, wv=wv, k=k, c2=c2, ts=ts: e.matmul(
                            ps[:], wv[:, k, c2 * 128:(c2 + 1) * 128], AT[:, k, ts * 512:(ts + 1) * 512],
                            start=(k == 0), stop=(k == kcn - 1)),
                            reads=[wkey] + rk, writes=[pkey])
                    epi(ng * 2 + c2, ts, ps, pkey)
            if hook is not None:
                hook(ng)

    def gemm_tm(w, kcn, col0, ncols, AT, tbs, epi, key_at="AT"):
        pieces = [(k0, min(16, kcn - k0)) for k0 in range(0, kcn, 16)]
        for nt in range(ncols // 512):
            for pi, (k0, kp) in enumerate(pieces):
                wv, wkey = wload(wview_tm(w, k0, kp, col0 + nt * 512, 512), kp, 512)
                for tb in tbs:
                    ps, pkey = gb.next()
                    for k in range(kp):
                        P.add("pe", lambda e, ps=ps, wv=wv, k=k, k0=k0, tb=tb: e.matmul(
                            ps[:], AT[:, k0 + k, tb * 128:(tb + 1) * 128], wv[:, k, :],
                            start=(k == 0), stop=(k == kp - 1)),
                            reads=[wkey, (key_at, tb, (k0 + k) // 8)], writes=[pkey])
                    epi(pi, len(pieces), tb, nt, ps, pkey)

    tpr = Ring([(psb[i], ("PB", i)) for i in range(2)])
    evtoggle = [0]

    def copy_any(out, in_, reads, writes):
        evtoggle[0] ^= 1
        if evtoggle[0]:
            P.add("act", lambda e: e.copy(out=out, in_=in_), reads=reads, writes=writes)
        else:
            P.add("dve", lambda e: e.tensor_copy(out=out, in_=in_), reads=reads, writes=writes)

    def load_bcast(dst, vec, key):
        P.add("sp", lambda e: e.dma_start(out=dst, in_=vec.broadcast_to([128, vec.shape[1]])),
              writes=[key], dma_key=("ld", key))

    def norm_block(src, r0, gbc, xb, xbkey, xn, ss, tbslot, AT, final_out=None, key_at="AT", xnkey="xn", defer=False):
        P.add("sp", lambda e: e.dma_start(out=xb, in_=src[r0:r0 + 128, :]), writes=[xbkey], dma_key=("ld", xbkey))
        P.add("act", lambda e: e.activation(out=xn, in_=xb, func=AF.Square, accum_out=ss[:, 0:1]),
              reads=[xbkey], writes=[xnkey, "ss0"])
        P.add("act", lambda e: e.activation(out=ss[:, 1:2], in_=ss[:, 0:1], func=AF.Sqrt, bias=epsc[:, 0:1], scale=1.0 / D),
              reads=["ss0"], writes=["ss1"])
        P.add("dve", lambda e: e.reciprocal(out=ss[:, 2:3], in_=ss[:, 1:2]), reads=["ss1"], writes=["ss2"])
        if final_out is not None:
            fo, fokey, dst = final_out
            P.add("dve", lambda e: e.scalar_tensor_tensor(out=fo, in0=xb, scalar=ss[:, 2:3], in1=gbc, op0=ALU.mult, op1=ALU.mult),
                  reads=[xbkey, "ss2", "gbc"], writes=[fokey])
            return P.add("sp", lambda e: e.dma_start(out=dst, in_=fo), reads=[fokey], dma_key=("st", fokey))
        P.add("dve", lambda e: e.scalar_tensor_tensor(out=xn, in0=xb, scalar=ss[:, 2:3], in1=gbc, op0=ALU.mult, op1=ALU.mult),
              reads=[xbkey, "ss2", "gbc"], writes=[xnkey])

        def back():
            norm_back(xn, xnkey, tbslot, AT, key_at)
        if defer:
            return back
        back()

    def norm_back(xn, xnkey, tbslot, AT, key_at):
        for c0 in range(0, KC, 8):
            nn = min(8, KC - c0)
            tp, tkey = tpr.next()
            for c in range(nn):
                P.add("pe", lambda e, tp=tp, c=c, c0=c0: e.transpose(out=tp[:, c * 128:(c + 1) * 128],
                                                                     in_=xn[:, (c0 + c) * 128:(c0 + c + 1) * 128], identity=ident_b),
                      reads=[xnkey], writes=[tkey])
            copy_any(AT[:, c0:c0 + nn, tbslot * 128:(tbslot + 1) * 128],
                     tp[:, 0:nn * 128].rearrange("p (c t) -> p c t", c=nn), [tkey], [(key_at, tbslot, c0 // 8)])

    def norm_phase(src, row0, ntb, gvec, AT, off):
        xbs = [(carve(off + i * 4 * D, 4 * D, F32), ("xb", i)) for i in range(2)]
        gbc = carve(off + 8 * D, 4 * D, F32)
        xn = carve(off + 12 * D, 2 * D, BF16)
        ss = carve(off + 14 * D, 16, F32)
        load_bcast(gbc, gvec, "gbc")
        for tb in range(ntb):
            xb, xbkey = xbs[tb % 2]
            norm_block(src, row0 + tb * 128, gbc, xb, xbkey, xn, ss, tb, AT)

    def load_AT(AT, src, kcn, c0, ntok, nparts=4):
        step = ((kcn + nparts - 1) // nparts + 7) // 8 * 8
        for i, k0 in enumerate(range(0, kcn, step)):
            k1 = min(kcn, k0 + step)
            P.add("sp", lambda e, k0=k0, k1=k1: e.dma_start(
                out=AT[:, k0:k1, :], in_=src[k0 * 128:k1 * 128, c0:c0 + ntok].rearrange("(c p) t -> p c t", p=128)),
                writes=[("AT", tb, cg) for tb in range(ntok // 128) for cg in range(k0 // 8, (k1 - 1) // 8 + 1)], dma_key=("ldat", i))

    scr_i = carve(0, 128 * 4, F32).bitcast(I32)
    scr_i2 = carve(512, NTAB * 4, F32).bitcast(I32)
    D1 = carve(2048, NTAB * 4, F32); DL = carve(4096, NTAB * 4, F32)
    t1 = carve(6144, NTAB * 4, F32); t2 = carve(8192, NTAB * 4, F32)
    P.add("pool", lambda e: e.iota(out=scr_i, pattern=[[1, 128]], base=0, channel_multiplier=-1), writes=["scr_i"])
    P.add("dve", lambda e: e.tensor_copy(out=valf, in_=scr_i), reads=["scr_i"], writes=["valf"])
    P.add("dve", lambda e: e.tensor_scalar(out=ident_f, in0=valf, scalar1=0.0, scalar2=None, op0=ALU.is_equal), reads=["valf"], writes=["ident_f"])
    P.add("dve", lambda e: e.tensor_scalar(out=tri_f, in0=valf, scalar1=0.0, scalar2=None, op0=ALU.is_ge), reads=["valf"], writes=["tri_f"])
    P.add("dve", lambda e: e.tensor_copy(out=ident_b, in_=ident_f), reads=["ident_f"], writes=["ident_b"])
    P.add("dve", lambda e: e.memset(ones_f, 1.0), writes=["ones_f"])
    P.add("dve", lambda e: e.memset(ones_b, 1.0), writes=["ones_b"])
    P.add("dve", lambda e: e.memset(epsc[:, 0:1], RMS_EPS), writes=["epsc"])
    P.add("dve", lambda e: e.memset(epsc[:, 1:2], LN_EPS), writes=["epsc"])
    P.add("sp", lambda e: e.dma_start(out=jv, in_=jv_d), writes=["jv"], dma_key="s_jv")
    P.add("dve", lambda e: e.tensor_scalar(out=jv128, in0=jv, scalar1=128.0, scalar2=None, op0=ALU.mult), reads=["jv"], writes=["jv128"])
    for r in range(4):
        P.add("dve", lambda e, r=r: e.tensor_scalar(out=jsh[:, r:r + 1], in0=jv, scalar1=128.0, scalar2=-128.0 * r, op0=ALU.mult, op1=ALU.add),
              reads=["jv"], writes=["jsh"])
    for r in range(4):
        P.add("dve", lambda e, r=r: e.tensor_scalar(out=masks[:, r, :], in0=valf, scalar1=jsh[:, r:r + 1], scalar2=0.0, op0=ALU.add, op1=ALU.is_ge),
              reads=["valf", "jsh"], writes=["masks"])
    P.add("pool", lambda e: e.iota(out=scr_i2, pattern=[[-128, NTAB]], base=320, channel_multiplier=1), writes=["scr_i2"])
    P.add("dve", lambda e: e.tensor_copy(out=D1, in_=scr_i2), reads=["scr_i2"], writes=["D1"])
    P.add("dve", lambda e: e.tensor_scalar(out=D1, in0=D1, scalar1=jv128, scalar2=None, op0=ALU.subtract), reads=["D1", "jv128"], writes=["D1"])
    P.add("pool", lambda e: e.iota(out=scr_i2, pattern=[[1, NTAB]], base=-3, channel_multiplier=0), reads=["D1"], writes=["scr_i2"])
    P.add("dve", lambda e: e.tensor_copy(out=DL, in_=scr_i2), reads=["scr_i2"], writes=["DL"])
    P.add("dve", lambda e: e.tensor_scalar(out=DL, in0=DL, scalar1=jv, scalar2=0.0, op0=ALU.add, op1=ALU.is_ge), reads=["DL", "jv"], writes=["DL"])
    P.add("dve", lambda e: e.tensor_tensor(out=t1, in0=D1, in1=DL, op=ALU.mult), reads=["D1", "DL"], writes=["t1"])
    P.add("dve", lambda e: e.tensor_scalar(out=t2, in0=DL, scalar1=1.0, scalar2=NEG_BIG, op0=ALU.subtract, op1=ALU.mult), reads=["DL"], writes=["t2"])
    for h in range(H):
        P.add("dve", lambda e, h=h: e.scalar_tensor_tensor(out=tab[:, h, :], in0=t1, scalar=slopes[h], in1=t2, op0=ALU.mult, op1=ALU.add),
              reads=["t1", "t2"], writes=["tab"])
    DW = carve(10240, NTW * 4, F32)
    scr_i3 = carve(12288, NTW * 4, F32).bitcast(I32)
    P.add("pool", lambda e: e.iota(out=scr_i3, pattern=[[-128, NTW]], base=-832 + 128 * 15, channel_multiplier=1), writes=["scr_i3"])
    P.add("dve", lambda e: e.tensor_copy(out=DW, in_=scr_i3), reads=["scr_i3"], writes=["DW"])
    P.add("dve", lambda e: e.tensor_scalar(out=DW, in0=DW, scalar1=jv128, scalar2=None, op0=ALU.subtract), reads=["DW", "jv128"], writes=["DW"])
    for h, hw in WIDX.items():
        P.add("dve", lambda e, h=h, hw=hw: e.tensor_scalar(out=tabw[:, hw, :], in0=DW, scalar1=slopes[h], scalar2=None, op0=ALU.mult), reads=["DW"], writes=["tabw"])
    lb = carve(16384, 4 * 128 * 4, F32).rearrange("p (a n) -> p a n", a=4)
    junk = carve(20480, 128 * 4, F32)
    for i, v in enumerate((lq1, lk1, lq2, lk2)):
        P.add("sp", lambda e, i=i, v=v: e.dma_start(out=lb[:, i, :], in_=v.broadcast_to([128, 128])), writes=[("lb", i)], dma_key=("s_lb", i))
    for i in range(2):
        P.add("dve", lambda e, i=i: e.tensor_tensor(out=junk, in0=lb[:, 2 * i, :], in1=lb[:, 2 * i + 1, :], op=ALU.mult),
              reads=[("lb", 2 * i), ("lb", 2 * i + 1)], writes=["junk"])
        P.add("dve", lambda e, i=i: e.reduce_sum(out=lamtmp[:, i:i + 1], in_=junk, axis=AX.X), reads=["junk"], writes=[("lt", i)])
        P.add("act", lambda e, i=i: e.activation(out=lamtmp[:, 2 + i:3 + i], in_=lamtmp[:, i:i + 1], func=AF.Exp), reads=[("lt", i)], writes=[("le", i)])
    P.add("dve", lambda e: e.tensor_tensor(out=lamtmp[:, 4:5], in0=lamtmp[:, 3:4], in1=lamtmp[:, 2:3], op=ALU.subtract), reads=[("le", 0), ("le", 1)], writes=["lt4"])
    P.add("dve", lambda e: e.tensor_scalar(out=neglam, in0=lamtmp[:, 4:5], scalar1=-lam_init, scalar2=None, op0=ALU.add), reads=["lt4"], writes=["neglam"])
    P.add("sp", lambda e: e.dma_start(out=subg, in_=subln_g.broadcast_to([128, 256])), writes=["subg"], dma_key="s_subg")
    P.add("dve", lambda e: e.tensor_scalar(out=subg, in0=subg, scalar1=1.0 - lam_init, scalar2=None, op0=ALU.mult), reads=["subg"], writes=["subg"])
    vrows = carve(24576, 128 * 4, F32)
    for a, v in enumerate((ln_v_g, ln_v_b, b_gate[0:1, :], b_gate[1:2, :])):
        P.add("sp", lambda e, v=v: e.dma_start(out=vrows[0:KC, :], in_=v.rearrange("o (c p) -> (o c) p", p=128)), writes=["vrows"], dma_key="s_vr")
        P.add("pe", lambda e: e.transpose(out=psf[4][:, 0:KC], in_=vrows[0:KC, :], identity=ident_f[0:KC, 0:KC]),
              reads=["vrows", "ident_f"], writes=[("PS", 4)])
        P.add("dve", lambda e, a=a: e.tensor_copy(out=cols[:, a, :], in_=psf[4][:, 0:KC]), reads=[("PS", 4)], writes=["cols"])
    wsl = carve(28672, 128 * 4, F32); wtf = carve(32768, 128 * 4, F32)
    bsbc = carve(36864, G * 128 * 4, F32).rearrange("p (g t) -> p g t", g=G)
    P.add("sp", lambda e: e.dma_start(out=bsbc.rearrange("p g t -> p (g t)"), in_=b_s.broadcast_to([128, G * 128])), writes=["bsbc"], dma_key="s_bs")
    for g in range(G):
        P.add("sp", lambda e, g=g: e.dma_start(out=wsl, in_=w_s[g]), writes=["wsl"], dma_key="s_ws")
        P.add("pe", lambda e: e.transpose(out=psf[4][:, 0:128], in_=wsl, identity=ident_f), reads=["wsl", "ident_f"], writes=[("PS", 4)])
        P.add("dve", lambda e: e.tensor_tensor(out=wtf, in0=psf[4][:, 0:128], in1=tri_f, op=ALU.mult), reads=[("PS", 4), "tri_f"], writes=["wtf"])
        P.add("dve", lambda e, g=g: e.tensor_copy(out=wT[:, g, :], in_=wtf), reads=["wtf"], writes=["wT"])
        P.add("pe", lambda e: e.matmul(psf[5][:, 0:128], ones_f, wtf, start=True, stop=True), reads=["wtf", "ones_f"], writes=[("PS", 5)])
        for cc in range(2):
            c = 2 * g + cc
            P.add("dve", lambda e, c=c, g=g: e.scalar_tensor_tensor(out=B2[:, c, :], in0=psf[5][:, 0:128], scalar=cols[:, 1, c:c + 1], in1=bsbc[:, g, :],
                                                                     op0=ALU.mult, op1=ALU.add), reads=[("PS", 5), "cols", "bsbc"], writes=["B2"])
    P.barrier()

    TGK = 512
    NGK = S // TGK
    ATs = [carve(i * KC * TGK * 2, KC * TGK * 2, BF16).rearrange("p (c t) -> p c t", c=KC) for i in range(2)]
    off_n = 2 * KC * TGK * 2
    xb_k = carve(off_n, 4 * D, F32); gbc_k = carve(off_n + 4 * D, 4 * D, F32)
    xn_k = [carve(off_n + 8 * D + i * 2 * D, 2 * D, BF16) for i in range(2)]
    ss_k = carve(off_n + 12 * D, 16, F32)
    soff = off_n + 12 * D + 64
    kst = [(carve(soff + i * TGK * 2, TGK * 2, BF16), ("kst", i)) for i in range(2)]
    vst = [(carve(soff + 2 * TGK * 2 + i * 1024, 1024, BF16), ("vst", i)) for i in range(2)]
    acc_off = soff + 2 * TGK * 2 + 2048
    accs_k = [(carve(acc_off + i * 2048, 2048, F32), ("acc", i)) for i in range(TGK // 128)]
    load_bcast(gbc_k, g_mix, "gbc")
    kring = Ring(kst); vring = Ring(vst)

    nbi = [0]
    backs = []

    def kv_norm(kg, tb, defer=False):
        i = nbi[0] % 2
        nbi[0] += 1
        return norm_block(x_all, kg * TGK + tb * 128, gbc_k, xb_k, ("xb", 0), xn_k[i], ss_k, tb, ATs[kg % 2], key_at=("ATK", kg % 2),
                          xnkey=("xn", i), defer=defer)

    def kv_step(pending):
        if backs:
            backs.pop(0)()
        if pending:
            backs.append(kv_norm(*pending.pop(0), defer=True))

    for tb in range(TGK // 128):
        kv_norm(0, tb)
    for kg in range(NGK):
        pending = [(kg + 1, tb) for tb in range(TGK // 128)] if kg + 1 < NGK else []

        def epi_k(ch, ts, ps, pkey, kg=kg, state={}):
            if ts == 0:
                state["cur"] = kring.next()
            stg, skey = state["cur"]
            copy_any(stg[:, ts * 512:(ts + 1) * 512], ps[:], [pkey], [(skey, ts)])
            if ts == TGK // 512 - 1:
                P.add("sp", lambda e: e.dma_start(out=KT[ch * 128:(ch + 1) * 128, kg * TGK:(kg + 1) * TGK], in_=stg),
                      reads=[(skey, t_) for t_ in range(TGK // 512)], dma_key=("st", skey))

        def hook(ng, pending=pending):
            if ng % 3 == 1:
                kv_step(pending)

        gemm_fm(w_in, 3 * D, D, ATs[kg % 2], TGK, epi_k, key_at=("ATK", kg % 2), hook=hook)
        while pending or backs:
            kv_step(pending)

        def epi_v(pi, npc, tb, nt, ps, pkey, kg=kg):
            acc, akey = accs_k[tb]
            if pi == 0 and npc > 1:
                copy_any(acc, ps[:], [pkey], [akey])
                return
            stg, skey = vring.next()
            if npc > 1:
                P.add("dve", lambda e: e.tensor_tensor(out=stg, in0=ps[:], in1=acc, op=ALU.add), reads=[pkey, akey], writes=[skey])
            else:
                copy_any(stg, ps[:], [pkey], [skey])
            r0 = kg * TGK + tb * 128
            P.add("sp", lambda e: e.dma_start(out=VV[r0:r0 + 128, nt * 512:(nt + 1) * 512], in_=stg), reads=[skey], dma_key=("st", skey))

        gemm_tm(w_in, KC, 4 * D, D, ATs[kg % 2], list(range(TGK // 128)), epi_v, key_at=("ATK", kg % 2))
    P.barrier()

    NTB = TG // 128
    def do_group(tg):
        tok0 = tg * TG
        AT = carve(0, KC * TG * 2, BF16).rearrange("p (c t) -> p c t", c=KC)
        off_n = KC * TG * 2
        norm_phase(x_own, tok0, NTB, g_mix, AT, off_n)
        P.barrier()
        gvs = [carve(off_n + i * 4 * D, 4 * D, F32) for i in range(2)]
        nbf = carve(off_n + 8 * D, 2 * D, BF16)
        mts = carve(off_n + 10 * D, 2 * D, BF16).rearrange("p (c t) -> p c t", c=KC)
        soff = off_n + 12 * D
        stat = carve(soff, 256, F32)
        bnst = carve(soff + 256, (D // 512) * 6 * 4, F32).rearrange("p (n s) -> p n s", s=6)
        soff2 = soff + 256 + (D // 512) * 24
        soff2 = (soff2 + 63) // 64 * 64
        accs = [(carve(soff2 + i * 2048, 2048, F32), ("acc", i)) for i in range(2)]
        soff3 = off_n
        for sg in range(NTB // 2):
            def epi_gv(pi, npc, tb, nt, ps, pkey, sg=sg):
                li = tb - 2 * sg
                acc, akey = accs[li]
                if pi == 0 and npc > 1:
                    copy_any(acc, ps[:], [pkey], [akey])
                    return
                src = ps[:]
                rd = [pkey]
                if npc > 1:
                    P.add("dve", lambda e: e.tensor_tensor(out=acc, in0=ps[:], in1=acc, op=ALU.add), reads=[pkey, akey], writes=[akey])
                    src = acc
                    rd = [akey]
                P.add("act", lambda e: e.activation(out=gvs[li][:, nt * 512:(nt + 1) * 512], in_=src, func=AF.Gelu), reads=rd, writes=[("gv", li, nt)])

            gemm_tm(w_in, KC, D, D, AT, [2 * sg, 2 * sg + 1], epi_gv)
            for li in range(2):
                tb = 2 * sg + li
                gv = gvs[li]
                for n_ in range(D // 512):
                    P.add("dve", lambda e, n_=n_, gv=gv: e.bn_stats(out=bnst[:, n_, :], in_=gv[:, n_ * 512:(n_ + 1) * 512]),
                          reads=[("gv", li, n_)], writes=["bnst"])
                P.add("dve", lambda e: e.bn_aggr(out=stat[:, 0:2], in_=bnst.rearrange("p n s -> p (n s)")), reads=["bnst"], writes=["st01"])
                P.add("act", lambda e: e.activation(out=stat[:, 2:3], in_=stat[:, 1:2], func=AF.Sqrt, bias=epsc[:, 1:2], scale=1.0), reads=["st01"], writes=["st2"])
                P.add("dve", lambda e: e.reciprocal(out=stat[:, 3:4], in_=stat[:, 2:3]), reads=["st2"], writes=["st3"])
                P.add("dve", lambda e, gv=gv: e.tensor_scalar(out=nbf, in0=gv, scalar1=stat[:, 0:1], scalar2=stat[:, 3:4], op0=ALU.subtract, op1=ALU.mult),
                      reads=[("gv", li, n_) for n_ in range(D // 512)] + ["st01", "st3"], writes=["nbf"])
                for c0 in range(0, KC, 4):
                    ps, pkey = gb.next()
                    for c in range(4):
                        cc = c0 + c
                        P.add("pe", lambda e, ps=ps, c=c, cc=cc: e.matmul(ps[:, c * 128:(c + 1) * 128], nbf[:, cc * 128:(cc + 1) * 128], wT[:, cc // 2, :], start=True, stop=True),
                              reads=["nbf", "wT"], writes=[pkey])
                    for c in range(4):
                        cc = c0 + c
                        P.add("dve", lambda e, ps=ps, c=c, cc=cc: e.scalar_tensor_tensor(out=mts[:, cc, :], in0=ps[:, c * 128:(c + 1) * 128], scalar=cols[:, 0, cc:cc + 1], in1=B2[:, cc, :],
                                                                                      op0=ALU.mult, op1=ALU.add), reads=[pkey, "cols", "B2"], writes=["mts"])
                c0t = tok0 + tb * 128
                P.add("sp", lambda e, c0t=c0t: e.dma_start(out=MT[:, c0t:c0t + 128].rearrange("(c p) t -> p c t", p=128), in_=mts), reads=["mts"], dma_key=("st", "mts"))
        P.barrier()
        fst = [(carve(soff3 + i * TG * 2, TG * 2, BF16), ("fst", i)) for i in range(3)]
        mtl = [(carve(soff3 + 3 * TG * 2 + i * 1024, 1024, BF16), ("mtl", i)) for i in range(3)]
        gtmp = [(carve(soff3 + 3 * TG * 2 + 3072 + i * 2048, 2048, F32), ("gtmp", i)) for i in range(2)]
        fring = Ring(fst); mring = Ring(mtl); tring = Ring(gtmp)

        def fm_store(dst):
            state = {}

            def put(ch, ts, producer):
                if ts == 0:
                    state["cur"] = fring.next()
                stg, skey = state["cur"]
                producer(stg[:, ts * 512:(ts + 1) * 512], (skey, ts))
                if ts == TG // 512 - 1:
                    P.add("sp", lambda e: e.dma_start(out=dst[ch * 128:(ch + 1) * 128, tok0:tok0 + TG], in_=stg),
                          reads=[(skey, t_) for t_ in range(TG // 512)], dma_key=("st", skey))
            return put

        put_u = fm_store(YAT)

        def epi_u(ch, ts, ps, pkey):
            ml, mkey = mring.next()
            tt, tkey = tring.next()
            c0t = tok0 + ts * 512
            P.add("sp", lambda e: e.dma_start(out=ml, in_=MT[ch * 128:(ch + 1) * 128, c0t:c0t + 512]), writes=[mkey], dma_key=("ld", mkey))
            P.add("act", lambda e: e.activation(out=tt, in_=ps[:], func=AF.Gelu), reads=[pkey], writes=[tkey])
            put_u(ch, ts, lambda o, skey: P.add("dve", lambda e: e.tensor_tensor(out=o, in0=tt, in1=ml, op=ALU.mult), reads=[tkey, mkey], writes=[skey]))

        gemm_fm(w_in, 0, D, AT, TG, epi_u)
        put_q = fm_store(QT)

        def epi_q(ch, ts, ps, pkey):
            put_q(ch, ts, lambda o, skey: copy_any(o, ps[:], [pkey], [skey]))

        gemm_fm(w_in, 2 * D, D, AT, TG, epi_q)
        for a, dst in ((0, GA), (1, GB)):
            put_g = fm_store(dst)

            def epi_g(ch, ts, ps, pkey, a=a, put_g=put_g):
                put_g(ch, ts, lambda o, skey: P.add("act", lambda e: e.activation(out=o, in_=ps[:], func=AF.Sigmoid, bias=cols[:, 2 + a, ch:ch + 1], scale=1.0),
                                                    reads=[pkey, "cols"], writes=[skey]))

            gemm_fm(w_in, (5 + a) * D, D, AT, TG, epi_g)
        P.barrier()

        o = [0]

        def al(nbytes, dt):
            a = carve(o[0], nbytes, dt)
            o[0] += (nbytes + 63) // 64 * 64
            return a

        qbuf = [(al(2 * TG * 2, BF16).rearrange("p (m t) -> p m t", m=2), ("qb", i)) for i in range(2)]
        kbuf = [(al(CH * 128 * 2, BF16), ("kb", i)) for i in range(3)]
        vbuf = [(al(CH * 258 * 2, BF16).rearrange("p (c e) -> p c e", e=258), ("vb", i)) for i in range(3)]
        pbuf = [(al(1024, BF16), ("pt", i)) for i in range(4)]
        o1n = al(4 * 256 * 4, F32).rearrange("p (i e) -> p i e", i=4)
        ofin = [(al(1024, F32), ("of", i)) for i in range(2)]
        ybf = [(al(512, BF16), ("yb", i)) for i in range(2)]
        ybT = [(al(2 * 512 * 2, BF16).rearrange("p (a t) -> p a t", a=2), ("ybT", i)) for i in range(2)]
        sm = al(64 * 4, F32)
        sq = al(1024, F32)
        for vb, vkey in vbuf:
            P.add("dve", lambda e, vb=vb: e.memset(vb[:, :, 256:257], 1.0), writes=[vkey])
        qring = Ring(qbuf); kring = Ring(kbuf); vring = Ring(vbuf); pring = Ring(pbuf)
        ofr = Ring(ofin); ybr = Ring(ybf); ybTr = Ring(ybT)
        sring = Ring([(psf[4], ("PS", 4)), (psf[5], ("PS", 5))])
        smi = [0]
        rows = []
        for h in range(H):
            for qt in range(TG // 512):
                m0 = tg * NTB + 4 * qt
                nkb = 4 * m0 + 16
                for mp in range(2):
                    for kb in range(nkb):
                        i0 = max(0, (kb - 4 * m0) // 4)
                        i1 = 4
                        if DSKIP[h] is not None:
                            i1 = min(4, max(0, (DSKIP[h] + kb - 4 * m0 + 3) // 4))
                        if i1 <= i0:
                            continue
                        rows.append(dict(h=h, qt=qt, mp=mp, kb=kb, m0=m0, nkb=nkb, i0=i0, i1=i1))
        cur = {}

        def emit_S(row):
            h, qt, mp, kb, m0, nkb = row["h"], row["qt"], row["mp"], row["kb"], row["m0"], row["nkb"]
            if cur.get("qh") != h:
                cur["qh"] = h
                qb, qkey = qring.next()
                P.add("sp", lambda e: e.dma_start(out=qb, in_=QT[h * 256:(h + 1) * 256, tok0:tok0 + TG].rearrange("(m p) t -> p m t", p=128)),
                      writes=[qkey], dma_key=("ld", qkey))
                cur["q"] = (qb, qkey)
            if cur.get("unit") != (h, qt):
                cur["unit"] = (h, qt)
                cur["yT"] = ybTr.next()
            if cur.get("chunk") != (h, qt, mp, kb // CH):
                cur["chunk"] = (h, qt, mp, kb // CH)
                kc0 = (kb // CH) * CH
                l0 = kb - kc0
                nk = min(CH, nkb - kc0)
                kb_, kkey = kring.next()
                P.add("sp", lambda e: e.dma_start(
                    out=kb_[:, l0 * 128:nk * 128], in_=KT[h * 256 + mp * 128:h * 256 + (mp + 1) * 128, kb * 128:(kc0 + nk) * 128]),
                    writes=[kkey], dma_key=("ld", kkey))
                vb, vkey = vring.next()
                P.add("sp", lambda e: e.dma_start(
                    out=vb[:, l0:nk, 0:256], in_=VV[kb * 128:(kc0 + nk) * 128, h * 256:(h + 1) * 256].rearrange("(c p) e -> p c e", p=128)),
                    writes=[vkey], dma_key=("ld", vkey))
                cur["k"] = (kb_, kkey)
                cur["v"] = (vb, vkey)
            row["q"] = cur["q"]; row["k"] = cur["k"]; row["v"] = cur["v"]; row["yT"] = cur["yT"]
            qb, qkey = row["q"]
            kb_, kkey = row["k"]
            kl = kb % CH
            i0, i1 = row["i0"], row["i1"]
            sps, skey = sring.next()
            row["s"] = (sps, skey)
            P.add("pe", lambda e: e.matmul(
                sps[:, i0 * 128:i1 * 128], kb_[:, kl * 128:(kl + 1) * 128], qb[:, mp, qt * 512 + i0 * 128:qt * 512 + i1 * 128], start=True, stop=True),
                reads=[kkey, qkey], writes=[skey])

        def emit_rest(row):
            h, qt, mp, kb, m0, nkb = row["h"], row["qt"], row["mp"], row["kb"], row["m0"], row["nkb"]
            sps, skey = row["s"]
            vb, vkey = row["v"]
            i0, i1 = row["i0"], row["i1"]
            kl = kb % CH
            pt, pkey_ = pring.next()
            if WIDE[h]:
                cw = 4 * m0 - kb + 15
                P.add("act", lambda e: e.activation(
                    out=pt[:, i0 * 128:i1 * 128], in_=sps[:, i0 * 128:i1 * 128], func=AF.Exp, bias=tabw[:, WIDX[h], cw:cw + 1], scale=SCALE),
                    reads=[skey], writes=[(pkey_, i) for i in range(i0, i1)])
            for i in range(i0, i1):
                dl = 4 * (m0 + i) - kb
                if not WIDE[h]:
                    P.add("act", lambda e, i=i, dl=dl: e.activation(
                        out=pt[:, i * 128:(i + 1) * 128], in_=sps[:, i * 128:(i + 1) * 128], func=AF.Exp, bias=tab[:, h, dl + 3:dl + 4], scale=SCALE),
                        reads=[skey], writes=[(pkey_, i)])
                r = -dl
                if 0 <= r <= 3:
                    P.add("dve", lambda e, i=i, r=r: e.tensor_tensor(out=pt[:, i * 128:(i + 1) * 128], in0=pt[:, i * 128:(i + 1) * 128], in1=masks[:, r, :], op=ALU.mult),
                          reads=[(pkey_, i)], writes=[(pkey_, i)])
                kfirst = 0 if DSKIP[h] is None else max(0, 4 * (m0 + i) - DSKIP[h] + 1)
                P.add("pe", lambda e, i=i, first=(kb == kfirst), last=(kb == 4 * (m0 + i) + 3): e.matmul(
                    psf[i][:, 0:257], pt[:, i * 128:(i + 1) * 128], vb[:, kl, 0:257], start=first, stop=last),
                    reads=[(pkey_, i), vkey], writes=[("PS", i)])
            if kb != nkb - 1:
                return
            yT, yTkey = row["yT"]
            for i in range(4):
                s0 = (smi[0] % 8) * 8
                smi[0] += 1
                rd = sm[:, s0:s0 + 1]
                P.add("dve", lambda e, i=i, rd=rd: e.reciprocal(out=rd, in_=psf[i][:, 256:257]), reads=[("PS", i)], writes=[("sm", s0)])
                if mp == 0:
                    P.add("dve", lambda e, i=i, rd=rd: e.tensor_scalar(out=o1n[:, i, :], in0=psf[i][:, 0:256], scalar1=rd, scalar2=None, op0=ALU.mult),
                          reads=[("PS", i), ("sm", s0)], writes=[("o1n", i)])
                    continue
                of, okey = ofr.next()
                yb, ykey = ybr.next()
                P.add("dve", lambda e, rd=rd, s0=s0: e.tensor_scalar(out=sm[:, s0 + 1:s0 + 2], in0=rd, scalar1=neglam, scalar2=None, op0=ALU.mult),
                      reads=[("sm", s0), "neglam"], writes=[("sm", s0 + 1)])
                P.add("dve", lambda e, i=i, of=of, s0=s0: e.scalar_tensor_tensor(out=of, in0=psf[i][:, 0:256], scalar=sm[:, s0 + 1:s0 + 2], in1=o1n[:, i, :],
                                                                              op0=ALU.mult, op1=ALU.add), reads=[("PS", i), ("sm", s0 + 1), ("o1n", i)], writes=[okey])
                P.add("act", lambda e, of=of, s0=s0: e.activation(out=sq, in_=of, func=AF.Square, accum_out=sm[:, s0 + 2:s0 + 3]), reads=[okey], writes=["sq", ("sm", s0 + 2)])
                P.add("act", lambda e, s0=s0: e.activation(out=sm[:, s0 + 3:s0 + 4], in_=sm[:, s0 + 2:s0 + 3], func=AF.Sqrt, bias=epsc[:, 1:2], scale=1.0 / 256),
                      reads=[("sm", s0 + 2)], writes=[("sm", s0 + 3)])
                P.add("dve", lambda e, s0=s0: e.reciprocal(out=sm[:, s0 + 4:s0 + 5], in_=sm[:, s0 + 3:s0 + 4]), reads=[("sm", s0 + 3)], writes=[("sm", s0 + 4)])
                P.add("dve", lambda e, of=of, yb=yb, s0=s0: e.scalar_tensor_tensor(out=yb, in0=of, scalar=sm[:, s0 + 4:s0 + 5], in1=subg, op0=ALU.mult, op1=ALU.mult),
                      reads=[okey, ("sm", s0 + 4), "subg"], writes=[ykey])
                tp, tkey = tpr.next()
                for a in range(2):
                    P.add("pe", lambda e, tp=tp, a=a, yb=yb: e.transpose(out=tp[:, a * 128:(a + 1) * 128], in_=yb[:, a * 128:(a + 1) * 128], identity=ident_b),
                          reads=[ykey], writes=[tkey])
                copy_any(yT[:, :, i * 128:(i + 1) * 128], tp[:, 0:256].rearrange("p (a t) -> p a t", a=2), [tkey], [(yTkey, i)])
            if mp == 1:
                c0t = tok0 + qt * 512
                P.add("sp", lambda e: e.dma_start(out=YBT[h * 256:(h + 1) * 256, c0t:c0t + 512].rearrange("(a p) t -> p a t", p=128), in_=yT),
                      reads=[(yTkey, i_) for i_ in range(4)], dma_key=("st", yTkey))

        emit_S(rows[0])
        for ri, row in enumerate(rows):
            if ri + 1 < len(rows):
                emit_S(rows[ri + 1])
            emit_rest(row)
        P.barrier()

        AT = carve(0, KC * TG * 2, BF16).rearrange("p (c t) -> p c t", c=KC)
        soff = KC * TG * 2
        gl = [(carve(soff + i * 1024, 1024, BF16), ("gl", i)) for i in range(3)]
        tl = [(carve(soff + 3072 + i * 2048, 2048, F32), ("tl", i)) for i in range(3)]
        tmpm = [(carve(soff + 3072 + 6144 + i * 2048, 2048, F32), ("tmpm", i)) for i in range(2)]
        fst = [(carve(soff + 3072 + 6144 + 4096 + i * TG * 2, TG * 2, BF16), ("fst", i)) for i in range(3)]
        glr = Ring(gl); tlr = Ring(tl); tmr = Ring(tmpm); fring = Ring(fst)
        load_AT(AT, YAT, KC, tok0, TG)

        def epi_a(ch, ts, ps, pkey):
            g_, gkey = glr.next()
            t_, tkey = tlr.next()
            c0t = tok0 + ts * 512
            P.add("sp", lambda e: e.dma_start(out=g_, in_=GA[ch * 128:(ch + 1) * 128, c0t:c0t + 512]), writes=[gkey], dma_key=("ld", gkey))
            P.add("dve", lambda e: e.tensor_tensor(out=t_, in0=ps[:], in1=g_, op=ALU.mult), reads=[pkey, gkey], writes=[tkey])
            P.add("sp", lambda e: e.dma_start(out=T1[ch * 128:(ch + 1) * 128, c0t:c0t + 512], in_=t_), reads=[tkey], dma_key=("st", tkey))

        gemm_fm(w_br_a, 0, D, AT, TG, epi_a)
        P.barrier()
        load_AT(AT, YBT, KC, tok0, TG)
        put_m = fm_store(MGT)

        def epi_b(ch, ts, ps, pkey):
            g_, gkey = glr.next()
            t_, tkey = tlr.next()
            m_, mkey = tmr.next()
            c0t = tok0 + ts * 512
            P.add("sp", lambda e: e.dma_start(out=g_, in_=GB[ch * 128:(ch + 1) * 128, c0t:c0t + 512]), writes=[gkey], dma_key=("ld", gkey))
            P.add("sp", lambda e: e.dma_start(out=t_, in_=T1[ch * 128:(ch + 1) * 128, c0t:c0t + 512]), writes=[tkey], dma_key=("ld", tkey))
            P.add("dve", lambda e: e.tensor_tensor(out=m_, in0=ps[:], in1=g_, op=ALU.mult), reads=[pkey, gkey], writes=[mkey])
            put_m(ch, ts, lambda o_, skey: P.add("dve", lambda e: e.tensor_tensor(out=o_, in0=m_, in1=t_, op=ALU.add), reads=[mkey, tkey], writes=[skey]))

        gemm_fm(w_br_b, 0, D, AT, TG, epi_b)
        P.barrier()
        load_AT(AT, MGT, KC, tok0, TG)

        def tm_residual(res_src, dst, accs, xl, ost, post=None):
            xlr = Ring(xl); osr = Ring(ost)

            def epi(pi, npc, tb, nt, ps, pkey):
                acc, akey = accs[tb % len(accs)]
                r0 = tok0_cur[0] + tb * 128
                if pi == 0:
                    x_, xkey = xlr.next()
                    P.add("sp", lambda e: e.dma_start(out=x_, in_=res_src[r0:r0 + 128, nt * 512:(nt + 1) * 512]), writes=[xkey], dma_key=("ld", xkey))
                    tgt, tk = (acc, akey) if npc > 1 else osr.next()
                    P.add("dve", lambda e: e.tensor_tensor(out=tgt, in0=ps[:], in1=x_, op=ALU.add), reads=[pkey, xkey], writes=[tk])
                    if npc > 1:
                        return
                elif pi < npc - 1:
                    P.add("dve", lambda e: e.tensor_tensor(out=acc, in0=ps[:], in1=acc, op=ALU.add), reads=[pkey, akey], writes=[akey])
                    return
                else:
                    tgt, tk = osr.next()
                    P.add("dve", lambda e: e.tensor_tensor(out=tgt, in0=ps[:], in1=acc, op=ALU.add), reads=[pkey, akey], writes=[tk])
                P.add("sp", lambda e: e.dma_start(out=dst[r0:r0 + 128, nt * 512:(nt + 1) * 512], in_=tgt), reads=[tk], dma_key=("st", tk))
            return epi

        tok0_cur = [tok0]
        a0 = soff
        accs = [(carve(a0 + i * 2048, 2048, F32), ("acc", i)) for i in range(NTB)]
        xl = [(carve(a0 + NTB * 2048 + i * 2048, 2048, F32), ("xl", i)) for i in range(3)]
        ost = [(carve(a0 + NTB * 2048 + 6144 + i * 2048, 2048, F32), ("ost", i)) for i in range(3)]
        gemm_tm(w_o, KC, 0, D, AT, list(range(NTB)), tm_residual(x_own, X1, accs, xl, ost))
        P.barrier()

        norm_phase(X1, tok0, NTB, g_ffn, AT, off_n)
        P.barrier()
        fst = [(carve(off_n + i * TG * 2, TG * 2, BF16), ("fst", i)) for i in range(3)]
        stmp = [(carve(off_n + 3 * TG * 2 + i * 2048, 2048, F32), ("stmp", i)) for i in range(3)]
        fring = Ring(fst); sring2 = Ring(stmp)
        put_f = fm_store(FFA)
        for pg in range(DFF // 256):
            wg, wgk = wload(wview_fm(w_gu, pg * 256, 256), KC, 256)
            wu, wuk = wload(wview_fm(w_gu, DFF + pg * 256, 256), KC, 256)
            for c2 in range(2):
                for ts in range(TG // 512):
                    pg_, pgk = gb.next()
                    pu_, puk = gb.next()
                    for (ps, pk, wv, wk) in ((pg_, pgk, wg, wgk), (pu_, puk, wu, wuk)):
                        for k in range(KC):
                            rk = [("AT", ts * 4 + i, k // 8) for i in range(4)]
                            P.add("pe", lambda e, ps=ps, wv=wv, k=k, c2=c2, ts=ts: e.matmul(
                                ps[:], wv[:, k, c2 * 128:(c2 + 1) * 128], AT[:, k, ts * 512:(ts + 1) * 512], start=(k == 0), stop=(k == KC - 1)),
                                reads=[wk] + rk, writes=[pk])
                    s_, sk = sring2.next()
                    P.add("act", lambda e, s_=s_, pg_=pg_: e.activation(out=s_, in_=pg_[:], func=AF.Silu), reads=[pgk], writes=[sk])
                    put_f(pg * 2 + c2, ts, lambda o_, skey, s_=s_, sk=sk, pu_=pu_, puk=puk: P.add(
                        "dve", lambda e: e.tensor_tensor(out=o_, in0=pu_[:], in1=s_, op=ALU.mult), reads=[puk, sk], writes=[skey]))
        P.barrier()

        ATd = carve(0, FC * TGD * 2, BF16).rearrange("p (c t) -> p c t", c=FC)
        a0 = FC * TGD * 2
        nd = TGD // 128
        accs = [(carve(a0 + i * 2048, 2048, F32), ("acc", i)) for i in range(nd)]
        xl = [(carve(a0 + nd * 2048 + i * 2048, 2048, F32), ("xl", i)) for i in range(3)]
        ost = [(carve(a0 + nd * 2048 + 6144 + i * 2048, 2048, F32), ("ost", i)) for i in range(3)]
        for sd in range(TG // TGD):
            tok0_cur[0] = tok0 + sd * TGD
            load_AT(ATd, FFA, FC, tok0_cur[0], TGD, nparts=6)
            gemm_tm(w_down, FC, 0, D, ATd, list(range(nd)), tm_residual(X1, X2, accs, xl, ost))
            P.barrier()
        tok0_cur[0] = tok0

        norm_phase(X2, tok0, NTB, g_ple, AT, off_n)
        pT = carve(off_n, PC * TG * 2, BF16).rearrange("p (c t) -> p c t", c=PC)
        pl = carve(off_n + PC * TG * 2, PLE * 4, F32)
        plb = carve(off_n + PC * TG * 2 + PLE * 4, PLE * 2, BF16)
        P.barrier()
        for tb in range(NTB):
            r0 = tok0 + tb * 128
            P.add("sp", lambda e, r0=r0: e.dma_start(out=pl, in_=p_own[r0:r0 + 128, :]), writes=["pl"], dma_key=("ld", "pl"))
            P.add("dve", lambda e: e.tensor_copy(out=plb, in_=pl), reads=["pl"], writes=["plb"])
            tp, tkey = tpr.next()
            for c in range(PC):
                P.add("pe", lambda e, tp=tp, c=c: e.transpose(out=tp[:, c * 128:(c + 1) * 128], in_=plb[:, c * 128:(c + 1) * 128], identity=ident_b),
                      reads=["plb"], writes=[tkey])
            copy_any(pT[:, :, tb * 128:(tb + 1) * 128], tp[:, 0:PC * 128].rearrange("p (c t) -> p c t", c=PC), [tkey], [("pT", tb)])
        a0 = off_n + PC * TG * 2 + PLE * 6
        a0 = (a0 + 63) // 64 * 64
        accs = [(carve(a0 + i * 2048, 2048, F32), ("acc", i)) for i in range(NTB)]
        xl = [(carve(a0 + NTB * 2048 + i * 2048, 2048, F32), ("xl", i)) for i in range(3)]
        ost = [(carve(a0 + NTB * 2048 + 6144 + i * 2048, 2048, F32), ("ost", i)) for i in range(3)]
        sg_ = [(carve(a0 + NTB * 2048 + 12288 + i * 2048, 2048, F32), ("sg", i)) for i in range(2)]
        xlr = Ring(xl); osr = Ring(ost); sgr = Ring(sg_)
        ppr = Ring([(psf[4], ("PS", 4)), (psf[5], ("PS", 5))])
        wpp_cur = {}

        def epi_p(pi, npc, tb, nt, ps, pkey):
            acc, akey = accs[tb]
            if pi == 0 and npc > 1:
                copy_any(acc, ps[:], [pkey], [akey])
                return
            src = ps[:]
            rd = [pkey]
            if npc > 1:
                P.add("dve", lambda e: e.tensor_tensor(out=acc, in0=ps[:], in1=acc, op=ALU.add), reads=[pkey, akey], writes=[akey])
                src = acc
                rd = [akey]
            s_, sk = sgr.next()
            P.add("act", lambda e: e.activation(out=s_, in_=src, func=AF.Sigmoid), reads=rd, writes=[sk])
            if wpp_cur.get("nt") != nt:
                wpp_cur["nt"] = nt
                wpp_cur["w"] = wload(wview_tm(w_pp, 0, PC, nt * 512, 512), PC, 512)
            wv, wkey = wpp_cur["w"]
            pp, ppk = ppr.next()
            for c in range(PC):
                P.add("pe", lambda e, pp=pp, c=c, wv=wv: e.matmul(pp[:], pT[:, c, tb * 128:(tb + 1) * 128], wv[:, c, :], start=(c == 0), stop=(c == PC - 1)),
                      reads=[wkey, ("pT", tb)], writes=[ppk])
            x_, xkey = xlr.next()
            r0 = tok0 + tb * 128
            P.add("sp", lambda e: e.dma_start(out=x_, in_=X2[r0:r0 + 128, nt * 512:(nt + 1) * 512]), writes=[xkey], dma_key=("ld", xkey))
            P.add("dve", lambda e: e.tensor_tensor(out=s_, in0=pp[:], in1=s_, op=ALU.mult), reads=[ppk, sk], writes=[sk])
            o_, ok = osr.next()
            P.add("dve", lambda e: e.tensor_tensor(out=o_, in0=s_, in1=x_, op=ALU.add), reads=[sk, xkey], writes=[ok])
            P.add("sp", lambda e: e.dma_start(out=X3[r0:r0 + 128, nt * 512:(nt + 1) * 512], in_=o_), reads=[ok], dma_key=("st", ok))

        gemm_tm(w_pg, KC, 0, D, AT, list(range(NTB)), epi_p)
        P.barrier()

        xbs = [(carve(i * 4 * D, 4 * D, F32), ("xb", i)) for i in range(2)]
        gbc = carve(8 * D, 4 * D, F32)
        xn = carve(12 * D, 2 * D, BF16)
        ss = carve(14 * D, 16, F32)
        fos = [(carve(14 * D + 64 + i * 4 * D, 4 * D, F32), ("fo", i)) for i in range(2)]
        load_bcast(gbc, g_final, "gbc")
        finals = []
        for tb in range(NTB):
            r0 = tok0 + tb * 128
            xb, xbkey = xbs[tb % 2]
            fo, fokey = fos[tb % 2]
            finals.append(norm_block(X3, r0, gbc, xb, xbkey, xn, ss, tb, None, final_out=(fo, fokey, out_d[r0:r0 + 128, :])))
        P.barrier()

    for tg_ in range(NO // TG):
        do_group(tg_)

    P.emit(final_waits=[op for op in P.dma_last.values() if not (isinstance(op.dma_key, tuple) and op.dma_key[0] == "W")])
    st.close()
    return nc


def alibi_slopes(n_heads):
    return [float(np.exp2(np.float32(-8.0) * np.float32(i) / np.float32(n_heads))) for i in range(1, n_heads + 1)]


def make_in_maps(cfg, inputs):
    D = cfg["D"]; S = cfg["S"]
    x = np.asarray(inputs["x"], dtype=np.float32)
    p = np.asarray(inputs["p"], dtype=np.float32)[0]
    B = x.shape[0]
    NBA = S // 128
    NOB = NBA // 4

    def w(name, shape=None):
        a = np.ascontiguousarray(np.asarray(inputs[name], dtype=np.float32))
        return a.reshape(shape) if shape is not None else a

    G = D // 256
    shared = {
        "g_mix": w("g_mix", (1, D)), "w_in": w("w_in", (D, 7 * D)), "b_gate": w("b_gate", (2, D)),
        "ln_v_g": w("ln_v_g", (1, D)), "ln_v_b": w("ln_v_b", (1, D)),
        "w_s": w("w_s", (G, 128, 128)), "b_s": w("b_s", (1, G * 128)),
        "lambda_q1": w("lambda_q1", (1, 128)), "lambda_k1": w("lambda_k1", (1, 128)),
        "lambda_q2": w("lambda_q2", (1, 128)), "lambda_k2": w("lambda_k2", (1, 128)),
        "subln_g": w("subln_g", (1, 256)),
        "w_br_a": w("w_br_a", (D, D)), "w_br_b": w("w_br_b", (D, D)), "w_o": w("w_o", (D, D)),
        "g_ffn": w("g_ffn", (1, D)), "w_gu": w("w_gu", (D, 2 * cfg["DFF"])), "w_down": w("w_down", (cfg["DFF"], D)),
        "g_ple": w("g_ple", (1, D)), "w_ple_gate": w("w_ple_gate", (D, D)), "w_ple_proj": w("w_ple_proj", (cfg["PLE"], D)),
        "g_final": w("g_final", (1, D)),
    }
    maps = []
    for c in range(4 * B):
        b, j = c // 4, c % 4
        xb = x[b].reshape(NBA, 128, D)
        pb = p[b].reshape(NBA, 128, -1)
        m = dict(shared)
        m["x_all"] = np.ascontiguousarray(x[b])
        m["x_own"] = np.ascontiguousarray(xb[j::4].reshape(NOB * 128, D))
        m["p_own"] = np.ascontiguousarray(pb[j::4].reshape(NOB * 128, -1))
        m["jv"] = np.full((128, 1), float(j), dtype=np.float32)
        maps.append(m)
    return maps


def assemble(cfg, results, B):
    D = cfg["D"]; S = cfg["S"]
    NBA = S // 128
    NOB = NBA // 4
    out = np.empty((B, NBA, 128, D), dtype=np.float32)
    for c in range(4 * B):
        b, j = c // 4, c % 4
        out[b, j::4] = np.asarray(results[c]["out"], dtype=np.float32).reshape(NOB, 128, D)
    return out.reshape(B, S, D)


FULL_CFG = {"D": 4096, "S": 8192, "DFF": 11008, "PLE": 256, "slopes": alibi_slopes(16)}


def kernel(**inputs):
    cfg = FULL_CFG
    nc = build(cfg)
    maps = make_in_maps(cfg, inputs)
    res = run_bass_kernel_spmd(nc, maps, core_ids=list(range(8)))
    return assemble(cfg, res.results, 2)
```

```python
import contextlib
import math
import numpy as np
import concourse.bass as bass
import concourse.mybir as mybir
from concourse.bass_utils import run_bass_kernel_spmd

F32 = mybir.dt.float32
BF16 = mybir.dt.bfloat16
I32 = mybir.dt.int32
AF = mybir.ActivationFunctionType
ALU = mybir.AluOpType
AX = mybir.AxisListType

RMS_EPS = 1e-6
LN_EPS = 1e-5
HEAD_DIM = 128
NEG_BIG = 30000.0


class Op:
    __slots__ = ("eng", "fn", "deps", "inc", "val", "dma_key", "dma_val")


class Prog:
    ENGS = ("pe", "act", "dve", "pool", "sp")

    def __init__(self, nc):
        self.nc = nc
        self.ops = {e: [] for e in self.ENGS}
        self.last_w = {}
        self.readers = {}
        self.dma_cnt = {}
        self.dma_last = {}
        self.last_real = {e: None for e in self.ENGS}

    def add(self, eng, fn, reads=(), writes=(), dma_key=None, extra_deps=()):
        op = Op()
        op.eng = eng; op.fn = fn
        op.inc = False; op.val = None; op.dma_key = dma_key; op.dma_val = None
        deps = []
        for r in reads:
            w = self.last_w.get(r)
            if w is not None:
                deps.append((w, True))
        for wk in writes:
            w = self.last_w.get(wk)
            if w is not None:
                deps.append((w, False))
            for rd in self.readers.get(wk, ()):
                deps.append((rd, False))
        for d in extra_deps:
            deps.append((d, True))
        if dma_key is not None:
            c = self.dma_cnt.get(dma_key, 0) + 1
            self.dma_cnt[dma_key] = c
            op.dma_val = 16 * c
            prev = self.dma_last.get(dma_key)
            if prev is not None:
                deps.append((prev, True))
            self.dma_last[dma_key] = op
        fd = []
        seen = set()
        for d, raw in deps:
            if d is op or id(d) in seen:
                continue
            if d.dma_key is None and d.eng == eng:
                if eng == "pe" or not raw:
                    continue
            seen.add(id(d))
            fd.append(d)
        op.deps = fd
        for d in fd:
            if d.dma_key is None:
                d.inc = True
        for r in reads:
            self.readers.setdefault(r, []).append(op)
        for wk in writes:
            self.last_w[wk] = op
            self.readers[wk] = []
        self.ops[eng].append(op)
        if fn is not None and dma_key is None:
            self.last_real[eng] = op
        return op

    def barrier(self, engs=("pe", "act", "dve", "sp"), keep=("W",)):
        lasts = [self.last_real[e] for e in self.ENGS if self.last_real[e] is not None and e != "pool"]
        dmas = [op for k, op in self.dma_last.items() if not (isinstance(k, tuple) and k[0] in keep)]
        for e in engs:
            self.add(e, None, extra_deps=lasts + dmas)
        for k in list(self.last_w.keys()):
            if not (isinstance(k, tuple) and k[0] in keep):
                del self.last_w[k]
        for k in list(self.readers.keys()):
            if not (isinstance(k, tuple) and k[0] in keep):
                del self.readers[k]

    def emit(self, final_waits=()):
        nc = self.nc
        for e in self.ENGS:
            c = 0
            for op in self.ops[e]:
                if op.dma_key is None and op.inc:
                    assert op.fn is not None
                    c += 1
                    op.val = c
        with contextlib.ExitStack() as st:
            esem = {e: st.enter_context(nc.semaphore("es_" + e)) for e in self.ENGS}
            dsem = {k: st.enter_context(nc.semaphore("ds_%d" % i)) for i, k in enumerate(self.dma_cnt)}
            block = st.enter_context(nc.Block())

            def run(e, engobj):
                waited = {}
                for op in self.ops[e]:
                    need = {}
                    for d in op.deps:
                        if d.dma_key is not None:
                            s, v = dsem[d.dma_key], d.dma_val
                        else:
                            s, v = esem[d.eng], d.val
                        key = id(s)
                        if waited.get(key, 0) >= v:
                            continue
                        if key not in need or need[key][1] < v:
                            need[key] = (s, v)
                    for key, (s, v) in need.items():
                        engobj.wait_ge(s, v)
                        waited[key] = v
                    if op.fn is None:
                        continue
                    ins = op.fn(engobj)
                    if op.dma_key is not None:
                        ins.then_inc(dsem[op.dma_key], 16)
                    elif op.inc:
                        ins.then_inc(esem[e], 1)
                if e == "sp":
                    for op in final_waits:
                        engobj.wait_ge(dsem[op.dma_key], op.dma_val)

            @block.tensor
            def _(t):
                run("pe", t)

            @block.scalar
            def _(t):
                run("act", t)

            @block.vector
            def _(t):
                run("dve", t)

            @block.gpsimd
            def _(t):
                run("pool", t)

            @block.sync
            def _(t):
                run("sp", t)


class Ring:
    def __init__(self, items):
        self.items = items
        self.i = 0

    def next(self):
        it = self.items[self.i % len(self.items)]
        self.i += 1
        return it


def build(cfg):
    D = cfg["D"]; S = cfg["S"]; DFF = cfg["DFF"]; PLE = cfg["PLE"]
    slopes = [float(s) for s in cfg["slopes"]]
    lam_init = 0.2
    H = D // 256; G = D // 256; KC = D // 128; FC = DFF // 128; PC = PLE // 128
    NBA = S // 128; NOB = NBA // 4; NO = NOB * 128
    TG = min(cfg.get("TG", 1024), NO)
    TGK = 512
    TGD = min(cfg.get("TGD", 512), TG)
    CH = min(16, NBA)
    NTAB = NBA + 3
    NTW = NBA + 8
    DSKIP = []
    for sl in slopes:
        d = 0
        while sl * (128 * d - 63) <= 128.0 and d <= NBA:
            d += 1
        DSKIP.append(d if (d <= NBA and cfg.get("skip", True)) else None)
    WIDE = [bool(cfg.get("wide", True)) and sl * 639.0 <= 40.5 for sl in slopes]
    WIDX = {h: i for i, h in enumerate([h for h in range(len(slopes)) if WIDE[h]])}
    NW = max(1, len(WIDX))
    SCALE = HEAD_DIM ** -0.5
    assert TG % 512 == 0 and TGK % 512 == 0 and D % 512 == 0 and DFF % 256 == 0

    nc = bass.Bass("TRN2", target_bir_lowering=False)

    def din(name, shape):
        return nc.dram_tensor(name, list(shape), F32, kind="ExternalInput").ap()

    x_all = din("x_all", [S, D]); x_own = din("x_own", [NO, D]); p_own = din("p_own", [NO, PLE])
    jv_d = din("jv", [128, 1])
    g_mix = din("g_mix", [1, D]); w_in = din("w_in", [D, 7 * D]); b_gate = din("b_gate", [2, D])
    ln_v_g = din("ln_v_g", [1, D]); ln_v_b = din("ln_v_b", [1, D])
    w_s = din("w_s", [G, 128, 128]); b_s = din("b_s", [1, G * 128])
    lq1 = din("lambda_q1", [1, 128]); lk1 = din("lambda_k1", [1, 128])
    lq2 = din("lambda_q2", [1, 128]); lk2 = din("lambda_k2", [1, 128])
    subln_g = din("subln_g", [1, 256])
    w_br_a = din("w_br_a", [D, D]); w_br_b = din("w_br_b", [D, D]); w_o = din("w_o", [D, D])
    g_ffn = din("g_ffn", [1, D]); w_gu = din("w_gu", [D, 2 * DFF]); w_down = din("w_down", [DFF, D])
    g_ple = din("g_ple", [1, D]); w_pg = din("w_ple_gate", [D, D]); w_pp = din("w_ple_proj", [PLE, D])
    g_final = din("g_final", [1, D])
    out_d = nc.dram_tensor("out", [NO, D], F32, kind="ExternalOutput").ap()

    def dscr(name, shape, dt):
        if cfg.get("debug"):
            return nc.dram_tensor(name, list(shape), dt, kind="ExternalOutput").ap()
        return nc.dram_tensor(name, list(shape), dt).ap()

    KT = dscr("KT", [D, S], BF16); VV = dscr("VV", [S, D], BF16)
    QT = dscr("QT", [D, NO], BF16); MT = dscr("MT", [D, NO], BF16)
    YAT = dscr("YAT", [D, NO], BF16); YBT = dscr("YBT", [D, NO], BF16)
    GA = dscr("GA", [D, NO], BF16); GB = dscr("GB", [D, NO], BF16)
    T1 = dscr("T1", [D, NO], F32); MGT = dscr("MGT", [D, NO], BF16)
    X1 = dscr("X1", [NO, D], F32); X2 = dscr("X2", [NO, D], F32); X3 = dscr("X3", [NO, D], F32)
    FFA = dscr("FFA", [DFF, NO], BF16)

    P = Prog(nc)
    st = contextlib.ExitStack()
    ARENA_B = 128 * 1024
    arena = st.enter_context(nc.sbuf_tensor("arena", [128, ARENA_B // 2], BF16))
    wring_t = [st.enter_context(nc.sbuf_tensor("wr%d" % i, [128, 8192], BF16)) for i in range(3)]
    CONST_F = 512 + 15 + H * NTAB + NW * NTW + 4 * KC + KC * 128 + 256 + 2 + 3
    cst = st.enter_context(nc.sbuf_tensor("cst", [128, CONST_F], F32))
    cbf = st.enter_context(nc.sbuf_tensor("cbf", [128, 128 + 4 * 128 + G * 128], BF16))
    psf = [st.enter_context(nc.psum_tensor("psf%d" % i, [128, 512], F32)) for i in range(6)]
    psb = [st.enter_context(nc.psum_tensor("psb%d" % i, [128, 1024], BF16)) for i in range(2)]

    co = [0]

    def cf(n):
        a = cst[:, co[0]:co[0] + n]
        co[0] += n
        return a

    ident_f = cf(128); valf = cf(128); tri_f = cf(128); ones_f = cf(128)
    jv = cf(1); jv128 = cf(1); jsh = cf(4); neglam = cf(1); lamtmp = cf(8)
    tab = cf(H * NTAB).rearrange("p (h c) -> p h c", h=H)
    tabw = cf(NW * NTW).rearrange("p (h c) -> p h c", h=NW)
    cols = cf(4 * KC).rearrange("p (a c) -> p a c", a=4)
    B2 = cf(KC * 128).rearrange("p (c t) -> p c t", c=KC)
    subg = cf(256)
    epsc = cf(2)
    assert co[0] <= CONST_F
    bo = [0]

    def cb(n):
        a = cbf[:, bo[0]:bo[0] + n]
        bo[0] += n
        return a

    ident_b = cb(128)
    masks = cb(4 * 128).rearrange("p (r q) -> p r q", r=4)
    wT = cb(G * 128).rearrange("p (g t) -> p g t", g=G)

    def carve(off, nbytes, dt):
        assert off % 4 == 0 and off + nbytes <= ARENA_B, (off, nbytes)
        a = arena[:, off // 2:(off + nbytes) // 2]
        return a.bitcast(F32) if dt == F32 else a

    wr = Ring([(wring_t[i], ("W", i)) for i in range(3)])

    def wload(src_view, kc, n):
        t, key = wr.next()
        v = t[:].rearrange("p (c n) -> p c n", n=n)[:, 0:kc, :]
        P.add("pool", lambda e, v=v, s=src_view: e.dma_start(out=v, in_=s), writes=[key], dma_key=key)
        return v, key

    def wview_fm(w, c0, n):
        return w[:, c0:c0 + n].rearrange("(c p) n -> p c n", p=128)

    def wview_tm(w, k0, kp, c0, n):
        return w[k0 * 128:(k0 + kp) * 128, c0:c0 + n].rearrange("(c p) n -> p c n", p=128)

    gb = Ring([(psf[i], ("PS", i)) for i in range(4)])

    def gemm_fm(w, col0, ncols, AT, tg, epi, kcn=None, key_at="AT", hook=None):
        kcn = kcn or KC
        for ng in range(ncols // 256):
            wv, wkey = wload(wview_fm(w, col0 + ng * 256, 256), kcn, 256)
            for c2 in range(2):
                for ts in range(tg // 512):
                    ps, pkey = gb.next()
                    for k in range(kcn):
                        rk = [(key_at, ts * 4 + i, k // 8) for i in range(4)]
                        P.add("pe", lambda e, ps=ps, wv=wv, k=k, c2=c2, ts=ts: e.matmul(
                            ps[:], wv[:, k, c2 * 128:(c2 + 1) * 128], AT[:, k, ts * 512:(ts + 1) * 512],
                            start=(k == 0), stop=(k == kcn - 1)),
                            reads=[wkey] + rk, writes=[pkey])
                    epi(ng * 2 + c2, ts, ps, pkey)
            if hook is not None:
                hook(ng)

    def gemm_tm(w, kcn, col0, ncols, AT, tbs, epi, key_at="AT"):
        pieces = [(k0, min(16, kcn - k0)) for k0 in range(0, kcn, 16)]
        for nt in range(ncols // 512):
            for pi, (k0, kp) in enumerate(pieces):
                wv, wkey = wload(wview_tm(w, k0, kp, col0 + nt * 512, 512), kp, 512)
                for tb in tbs:
                    ps, pkey = gb.next()
                    for k in range(kp):
                        P.add("pe", lambda e, ps=ps, wv=wv, k=k, k0=k0, tb=tb: e.matmul(
                            ps[:], AT[:, k0 + k, tb * 128:(tb + 1) * 128], wv[:, k, :],
                            start=(k == 0), stop=(k == kp - 1)),
                            reads=[wkey, (key_at, tb, (k0 + k) // 8)], writes=[pkey])
                    epi(pi, len(pieces), tb, nt, ps, pkey)

    tpr = Ring([(psb[i], ("PB", i)) for i in range(2)])
    evtoggle = [0]

    def copy_any(out, in_, reads, writes):
        evtoggle[0] ^= 1
        if evtoggle[0]:
            P.add("act", lambda e: e.copy(out=out, in_=in_), reads=reads, writes=writes)
        else:
            P.add("dve", lambda e: e.tensor_copy(out=out, in_=in_), reads=reads, writes=writes)

    def load_bcast(dst, vec, key):
        P.add("sp", lambda e: e.dma_start(out=dst, in_=vec.broadcast_to([128, vec.shape[1]])),
              writes=[key], dma_key=("ld", key))

    def norm_block(src, r0, gbc, xb, xbkey, xn, ss, tbslot, AT, final_out=None, key_at="AT", xnkey="xn", defer=False):
        P.add("sp", lambda e: e.dma_start(out=xb, in_=src[r0:r0 + 128, :]), writes=[xbkey], dma_key=("ld", xbkey))
        P.add("act", lambda e: e.activation(out=xn, in_=xb, func=AF.Square, accum_out=ss[:, 0:1]),
              reads=[xbkey], writes=[xnkey, "ss0"])
        P.add("act", lambda e: e.activation(out=ss[:, 1:2], in_=ss[:, 0:1], func=AF.Sqrt, bias=epsc[:, 0:1], scale=1.0 / D),
              reads=["ss0"], writes=["ss1"])
        P.add("dve", lambda e: e.reciprocal(out=ss[:, 2:3], in_=ss[:, 1:2]), reads=["ss1"], writes=["ss2"])
        if final_out is not None:
            fo, fokey, dst = final_out
            P.add("dve", lambda e: e.scalar_tensor_tensor(out=fo, in0=xb, scalar=ss[:, 2:3], in1=gbc, op0=ALU.mult, op1=ALU.mult),
                  reads=[xbkey, "ss2", "gbc"], writes=[fokey])
            return P.add("sp", lambda e: e.dma_start(out=dst, in_=fo), reads=[fokey], dma_key=("st", fokey))
        P.add("dve", lambda e: e.scalar_tensor_tensor(out=xn, in0=xb, scalar=ss[:, 2:3], in1=gbc, op0=ALU.mult, op1=ALU.mult),
              reads=[xbkey, "ss2", "gbc"], writes=[xnkey])

        def back():
            norm_back(xn, xnkey, tbslot, AT, key_at)
        if defer:
            return back
        back()

    def norm_back(xn, xnkey, tbslot, AT, key_at):
        for c0 in range(0, KC, 8):
            nn = min(8, KC - c0)
            tp, tkey = tpr.next()
            for c in range(nn):
                P.add("pe", lambda e, tp=tp, c=c, c0=c0: e.transpose(out=tp[:, c * 128:(c + 1) * 128],
                                                                     in_=xn[:, (c0 + c) * 128:(c0 + c + 1) * 128], identity=ident_b),
                      reads=[xnkey], writes=[tkey])
            copy_any(AT[:, c0:c0 + nn, tbslot * 128:(tbslot + 1) * 128],
                     tp[:, 0:nn * 128].rearrange("p (c t) -> p c t", c=nn), [tkey], [(key_at, tbslot, c0 // 8)])

    def norm_phase(src, row0, ntb, gvec, AT, off):
        xbs = [(carve(off + i * 4 * D, 4 * D, F32), ("xb", i)) for i in range(2)]
        gbc = carve(off + 8 * D, 4 * D, F32)
        xn = carve(off + 12 * D, 2 * D, BF16)
        ss = carve(off + 14 * D, 16, F32)
        load_bcast(gbc, gvec, "gbc")
        for tb in range(ntb):
            xb, xbkey = xbs[tb % 2]
            norm_block(src, row0 + tb * 128, gbc, xb, xbkey, xn, ss, tb, AT)

    def load_AT(AT, src, kcn, c0, ntok, nparts=4):
        step = ((kcn + nparts - 1) // nparts + 7) // 8 * 8
        for i, k0 in enumerate(range(0, kcn, step)):
            k1 = min(kcn, k0 + step)
            P.add("sp", lambda e, k0=k0, k1=k1: e.dma_start(
                out=AT[:, k0:k1, :], in_=src[k0 * 128:k1 * 128, c0:c0 + ntok].rearrange("(c p) t -> p c t", p=128)),
                writes=[("AT", tb, cg) for tb in range(ntok // 128) for cg in range(k0 // 8, (k1 - 1) // 8 + 1)], dma_key=("ldat", i))

    scr_i = carve(0, 128 * 4, F32).bitcast(I32)
    scr_i2 = carve(512, NTAB * 4, F32).bitcast(I32)
    D1 = carve(2048, NTAB * 4, F32); DL = carve(4096, NTAB * 4, F32)
    t1 = carve(6144, NTAB * 4, F32); t2 = carve(8192, NTAB * 4, F32)
    P.add("pool", lambda e: e.iota(out=scr_i, pattern=[[1, 128]], base=0, channel_multiplier=-1), writes=["scr_i"])
    P.add("dve", lambda e: e.tensor_copy(out=valf, in_=scr_i), reads=["scr_i"], writes=["valf"])
    P.add("dve", lambda e: e.tensor_scalar(out=ident_f, in0=valf, scalar1=0.0, scalar2=None, op0=ALU.is_equal), reads=["valf"], writes=["ident_f"])
    P.add("dve", lambda e: e.tensor_scalar(out=tri_f, in0=valf, scalar1=0.0, scalar2=None, op0=ALU.is_ge), reads=["valf"], writes=["tri_f"])
    P.add("dve", lambda e: e.tensor_copy(out=ident_b, in_=ident_f), reads=["ident_f"], writes=["ident_b"])
    P.add("dve", lambda e: e.memset(ones_f, 1.0), writes=["ones_f"])
    P.add("dve", lambda e: e.memset(epsc[:, 0:1], RMS_EPS), writes=["epsc"])
    P.add("dve", lambda e: e.memset(epsc[:, 1:2], LN_EPS), writes=["epsc"])
    P.add("sp", lambda e: e.dma_start(out=jv, in_=jv_d), writes=["jv"], dma_key="s_jv")
    P.add("dve", lambda e: e.tensor_scalar(out=jv128, in0=jv, scalar1=128.0, scalar2=None, op0=ALU.mult), reads=["jv"], writes=["jv128"])
    for r in range(4):
        P.add("dve", lambda e, r=r: e.tensor_scalar(out=jsh[:, r:r + 1], in0=jv, scalar1=128.0, scalar2=-128.0 * r, op0=ALU.mult, op1=ALU.add),
              reads=["jv"], writes=["jsh"])
    for r in range(4):
        P.add("dve", lambda e, r=r: e.tensor_scalar(out=masks[:, r, :], in0=valf, scalar1=jsh[:, r:r + 1], scalar2=0.0, op0=ALU.add, op1=ALU.is_ge),
              reads=["valf", "jsh"], writes=["masks"])
    P.add("pool", lambda e: e.iota(out=scr_i2, pattern=[[-128, NTAB]], base=320, channel_multiplier=1), writes=["scr_i2"])
    P.add("dve", lambda e: e.tensor_copy(out=D1, in_=scr_i2), reads=["scr_i2"], writes=["D1"])
    P.add("dve", lambda e: e.tensor_scalar(out=D1, in0=D1, scalar1=jv128, scalar2=None, op0=ALU.subtract), reads=["D1", "jv128"], writes=["D1"])
    P.add("pool", lambda e: e.iota(out=scr_i2, pattern=[[1, NTAB]], base=-3, channel_multiplier=0), reads=["D1"], writes=["scr_i2"])
    P.add("dve", lambda e: e.tensor_copy(out=DL, in_=scr_i2), reads=["scr_i2"], writes=["DL"])
    P.add("dve", lambda e: e.tensor_scalar(out=DL, in0=DL, scalar1=jv, scalar2=0.0, op0=ALU.add, op1=ALU.is_ge), reads=["DL", "jv"], writes=["DL"])
    P.add("dve", lambda e: e.tensor_tensor(out=t1, in0=D1, in1=DL, op=ALU.mult), reads=["D1", "DL"], writes=["t1"])
    P.add("dve", lambda e: e.tensor_scalar(out=t2, in0=DL, scalar1=1.0, scalar2=NEG_BIG, op0=ALU.subtract, op1=ALU.mult), reads=["DL"], writes=["t2"])
    for h in range(H):
        P.add("dve", lambda e, h=h: e.scalar_tensor_tensor(out=tab[:, h, :], in0=t1, scalar=slopes[h], in1=t2, op0=ALU.mult, op1=ALU.add),
              reads=["t1", "t2"], writes=["tab"])
    DW = carve(10240, NTW * 4, F32)
    scr_i3 = carve(12288, NTW * 4, F32).bitcast(I32)
    P.add("pool", lambda e: e.iota(out=scr_i3, pattern=[[-128, NTW]], base=-320 + 128 * 15, channel_multiplier=1), writes=["scr_i3"])
    P.add("dve", lambda e: e.tensor_copy(out=DW, in_=scr_i3), reads=["scr_i3"], writes=["DW"])
    P.add("dve", lambda e: e.tensor_scalar(out=DW, in0=DW, scalar1=jv128, scalar2=None, op0=ALU.subtract), reads=["DW", "jv128"], writes=["DW"])
    for h, hw in WIDX.items():
        P.add("dve", lambda e, h=h, hw=hw: e.tensor_scalar(out=tabw[:, hw, :], in0=DW, scalar1=slopes[h], scalar2=None, op0=ALU.mult), reads=["DW"], writes=["tabw"])
    lb = carve(16384, 4 * 128 * 4, F32).rearrange("p (a n) -> p a n", a=4)
    junk = carve(20480, 128 * 4, F32)
    for i, v in enumerate((lq1, lk1, lq2, lk2)):
        P.add("sp", lambda e, i=i, v=v: e.dma_start(out=lb[:, i, :], in_=v.broadcast_to([128, 128])), writes=[("lb", i)], dma_key=("s_lb", i))
    for i in range(2):
        P.add("dve", lambda e, i=i: e.tensor_tensor(out=junk, in0=lb[:, 2 * i, :], in1=lb[:, 2 * i + 1, :], op=ALU.mult),
              reads=[("lb", 2 * i), ("lb", 2 * i + 1)], writes=["junk"])
        P.add("dve", lambda e, i=i: e.reduce_sum(out=lamtmp[:, i:i + 1], in_=junk, axis=AX.X), reads=["junk"], writes=[("lt", i)])
        P.add("act", lambda e, i=i: e.activation(out=lamtmp[:, 2 + i:3 + i], in_=lamtmp[:, i:i + 1], func=AF.Exp), reads=[("lt", i)], writes=[("le", i)])
    P.add("dve", lambda e: e.tensor_tensor(out=lamtmp[:, 4:5], in0=lamtmp[:, 3:4], in1=lamtmp[:, 2:3], op=ALU.subtract), reads=[("le", 0), ("le", 1)], writes=["lt4"])
    P.add("dve", lambda e: e.tensor_scalar(out=neglam, in0=lamtmp[:, 4:5], scalar1=-lam_init, scalar2=None, op0=ALU.add), reads=["lt4"], writes=["neglam"])
    P.add("sp", lambda e: e.dma_start(out=subg, in_=subln_g.broadcast_to([128, 256])), writes=["subg"], dma_key="s_subg")
    P.add("dve", lambda e: e.tensor_scalar(out=subg, in0=subg, scalar1=1.0 - lam_init, scalar2=None, op0=ALU.mult), reads=["subg"], writes=["subg"])
    vrows = carve(24576, 128 * 4, F32)
    for a, v in enumerate((ln_v_g, ln_v_b, b_gate[0:1, :], b_gate[1:2, :])):
        P.add("sp", lambda e, v=v: e.dma_start(out=vrows[0:KC, :], in_=v.rearrange("o (c p) -> (o c) p", p=128)), writes=["vrows"], dma_key="s_vr")
        P.add("pe", lambda e: e.transpose(out=psf[4][:, 0:KC], in_=vrows[0:KC, :], identity=ident_f[0:KC, 0:KC]),
              reads=["vrows", "ident_f"], writes=[("PS", 4)])
        P.add("dve", lambda e, a=a: e.tensor_copy(out=cols[:, a, :], in_=psf[4][:, 0:KC]), reads=[("PS", 4)], writes=["cols"])
    wsl = carve(28672, 128 * 4, F32); wtf = carve(32768, 128 * 4, F32)
    bsbc = carve(36864, G * 128 * 4, F32).rearrange("p (g t) -> p g t", g=G)
    P.add("sp", lambda e: e.dma_start(out=bsbc.rearrange("p g t -> p (g t)"), in_=b_s.broadcast_to([128, G * 128])), writes=["bsbc"], dma_key="s_bs")
    for g in range(G):
        P.add("sp", lambda e, g=g: e.dma_start(out=wsl, in_=w_s[g]), writes=["wsl"], dma_key="s_ws")
        P.add("pe", lambda e: e.transpose(out=psf[4][:, 0:128], in_=wsl, identity=ident_f), reads=["wsl", "ident_f"], writes=[("PS", 4)])
        P.add("dve", lambda e: e.tensor_tensor(out=wtf, in0=psf[4][:, 0:128], in1=tri_f, op=ALU.mult), reads=[("PS", 4), "tri_f"], writes=["wtf"])
        P.add("dve", lambda e, g=g: e.tensor_copy(out=wT[:, g, :], in_=wtf), reads=["wtf"], writes=["wT"])
        P.add("pe", lambda e: e.matmul(psf[5][:, 0:128], ones_f, wtf, start=True, stop=True), reads=["wtf", "ones_f"], writes=[("PS", 5)])
        for cc in range(2):
            c = 2 * g + cc
            P.add("dve", lambda e, c=c, g=g: e.scalar_tensor_tensor(out=B2[:, c, :], in0=psf[5][:, 0:128], scalar=cols[:, 1, c:c + 1], in1=bsbc[:, g, :],
                                                                     op0=ALU.mult, op1=ALU.add), reads=[("PS", 5), "cols", "bsbc"], writes=["B2"])
    P.barrier()

    TGK = 512
    NGK = S // TGK
    ATs = [carve(i * KC * TGK * 2, KC * TGK * 2, BF16).rearrange("p (c t) -> p c t", c=KC) for i in range(2)]
    off_n = 2 * KC * TGK * 2
    xb_k = carve(off_n, 4 * D, F32); gbc_k = carve(off_n + 4 * D, 4 * D, F32)
    xn_k = [carve(off_n + 8 * D + i * 2 * D, 2 * D, BF16) for i in range(2)]
    ss_k = carve(off_n + 12 * D, 16, F32)
    soff = off_n + 12 * D + 64
    kst = [(carve(soff + i * TGK * 2, TGK * 2, BF16), ("kst", i)) for i in range(2)]
    vst = [(carve(soff + 2 * TGK * 2 + i * 1024, 1024, BF16), ("vst", i)) for i in range(2)]
    acc_off = soff + 2 * TGK * 2 + 2048
    accs_k = [(carve(acc_off + i * 2048, 2048, F32), ("acc", i)) for i in range(TGK // 128)]
    load_bcast(gbc_k, g_mix, "gbc")
    kring = Ring(kst); vring = Ring(vst)

    nbi = [0]
    backs = []

    def kv_norm(kg, tb, defer=False):
        i = nbi[0] % 2
        nbi[0] += 1
        return norm_block(x_all, kg * TGK + tb * 128, gbc_k, xb_k, ("xb", 0), xn_k[i], ss_k, tb, ATs[kg % 2], key_at=("ATK", kg % 2),
                          xnkey=("xn", i), defer=defer)

    def kv_step(pending):
        if backs:
            backs.pop(0)()
        if pending:
            backs.append(kv_norm(*pending.pop(0), defer=True))

    for tb in range(TGK // 128):
        kv_norm(0, tb)
    for kg in range(NGK):
        pending = [(kg + 1, tb) for tb in range(TGK // 128)] if kg + 1 < NGK else []

        def epi_k(ch, ts, ps, pkey, kg=kg, state={}):
            if ts == 0:
                state["cur"] = kring.next()
            stg, skey = state["cur"]
            copy_any(stg[:, ts * 512:(ts + 1) * 512], ps[:], [pkey], [(skey, ts)])
            if ts == TGK // 512 - 1:
                P.add("sp", lambda e: e.dma_start(out=KT[ch * 128:(ch + 1) * 128, kg * TGK:(kg + 1) * TGK], in_=stg),
                      reads=[(skey, t_) for t_ in range(TGK // 512)], dma_key=("st", skey))

        def hook(ng, pending=pending):
            if ng % 3 == 1:
                kv_step(pending)

        gemm_fm(w_in, 3 * D, D, ATs[kg % 2], TGK, epi_k, key_at=("ATK", kg % 2), hook=hook)
        while pending or backs:
            kv_step(pending)

        def epi_v(pi, npc, tb, nt, ps, pkey, kg=kg):
            acc, akey = accs_k[tb]
            if pi == 0 and npc > 1:
                copy_any(acc, ps[:], [pkey], [akey])
                return
            stg, skey = vring.next()
            if npc > 1:
                P.add("dve", lambda e: e.tensor_tensor(out=stg, in0=ps[:], in1=acc, op=ALU.add), reads=[pkey, akey], writes=[skey])
            else:
                copy_any(stg, ps[:], [pkey], [skey])
            r0 = kg * TGK + tb * 128
            P.add("sp", lambda e: e.dma_start(out=VV[r0:r0 + 128, nt * 512:(nt + 1) * 512], in_=stg), reads=[skey], dma_key=("st", skey))

        gemm_tm(w_in, KC, 4 * D, D, ATs[kg % 2], list(range(TGK // 128)), epi_v, key_at=("ATK", kg % 2))
    P.barrier()

    NTB = TG // 128
    def do_group(tg):
        tok0 = tg * TG
        AT = carve(0, KC * TG * 2, BF16).rearrange("p (c t) -> p c t", c=KC)
        off_n = KC * TG * 2
        norm_phase(x_own, tok0, NTB, g_mix, AT, off_n)
        P.barrier()
        gvs = [carve(off_n + i * 4 * D, 4 * D, F32) for i in range(2)]
        nbf = carve(off_n + 8 * D, 2 * D, BF16)
        mts = carve(off_n + 10 * D, 2 * D, BF16).rearrange("p (c t) -> p c t", c=KC)
        soff = off_n + 12 * D
        stat = carve(soff, 256, F32)
        bnst = carve(soff + 256, (D // 512) * 6 * 4, F32).rearrange("p (n s) -> p n s", s=6)
        soff2 = soff + 256 + (D // 512) * 24
        soff2 = (soff2 + 63) // 64 * 64
        accs = [(carve(soff2 + i * 2048, 2048, F32), ("acc", i)) for i in range(2)]
        soff3 = off_n
        for sg in range(NTB // 2):
            def epi_gv(pi, npc, tb, nt, ps, pkey, sg=sg):
                li = tb - 2 * sg
                acc, akey = accs[li]
                if pi == 0 and npc > 1:
                    copy_any(acc, ps[:], [pkey], [akey])
                    return
                src = ps[:]
                rd = [pkey]
                if npc > 1:
                    P.add("dve", lambda e: e.tensor_tensor(out=acc, in0=ps[:], in1=acc, op=ALU.add), reads=[pkey, akey], writes=[akey])
                    src = acc
                    rd = [akey]
                P.add("act", lambda e: e.activation(out=gvs[li][:, nt * 512:(nt + 1) * 512], in_=src, func=AF.Gelu), reads=rd, writes=[("gv", li, nt)])

            gemm_tm(w_in, KC, D, D, AT, [2 * sg, 2 * sg + 1], epi_gv)
            for li in range(2):
                tb = 2 * sg + li
                gv = gvs[li]
                for n_ in range(D // 512):
                    P.add("dve", lambda e, n_=n_, gv=gv: e.bn_stats(out=bnst[:, n_, :], in_=gv[:, n_ * 512:(n_ + 1) * 512]),
                          reads=[("gv", li, n_)], writes=["bnst"])
                P.add("dve", lambda e: e.bn_aggr(out=stat[:, 0:2], in_=bnst.rearrange("p n s -> p (n s)")), reads=["bnst"], writes=["st01"])
                P.add("act", lambda e: e.activation(out=stat[:, 2:3], in_=stat[:, 1:2], func=AF.Sqrt, bias=epsc[:, 1:2], scale=1.0), reads=["st01"], writes=["st2"])
                P.add("dve", lambda e: e.reciprocal(out=stat[:, 3:4], in_=stat[:, 2:3]), reads=["st2"], writes=["st3"])
                P.add("dve", lambda e, gv=gv: e.tensor_scalar(out=nbf, in0=gv, scalar1=stat[:, 0:1], scalar2=stat[:, 3:4], op0=ALU.subtract, op1=ALU.mult),
                      reads=[("gv", li, n_) for n_ in range(D // 512)] + ["st01", "st3"], writes=["nbf"])
                for c0 in range(0, KC, 4):
                    ps, pkey = gb.next()
                    for c in range(4):
                        cc = c0 + c
                        P.add("pe", lambda e, ps=ps, c=c, cc=cc: e.matmul(ps[:, c * 128:(c + 1) * 128], nbf[:, cc * 128:(cc + 1) * 128], wT[:, cc // 2, :], start=True, stop=True),
                              reads=["nbf", "wT"], writes=[pkey])
                    for c in range(4):
                        cc = c0 + c
                        P.add("dve", lambda e, ps=ps, c=c, cc=cc: e.scalar_tensor_tensor(out=mts[:, cc, :], in0=ps[:, c * 128:(c + 1) * 128], scalar=cols[:, 0, cc:cc + 1], in1=B2[:, cc, :],
                                                                                      op0=ALU.mult, op1=ALU.add), reads=[pkey, "cols", "B2"], writes=["mts"])
                c0t = tok0 + tb * 128
                P.add("sp", lambda e, c0t=c0t: e.dma_start(out=MT[:, c0t:c0t + 128].rearrange("(c p) t -> p c t", p=128), in_=mts), reads=["mts"], dma_key=("st", "mts"))
        P.barrier()
        fst = [(carve(soff3 + i * TG * 2, TG * 2, BF16), ("fst", i)) for i in range(3)]
        mtl = [(carve(soff3 + 3 * TG * 2 + i * 1024, 1024, BF16), ("mtl", i)) for i in range(3)]
        gtmp = [(carve(soff3 + 3 * TG * 2 + 3072 + i * 2048, 2048, F32), ("gtmp", i)) for i in range(2)]
        fring = Ring(fst); mring = Ring(mtl); tring = Ring(gtmp)

        def fm_store(dst):
            state = {}

            def put(ch, ts, producer):
                if ts == 0:
                    state["cur"] = fring.next()
                stg, skey = state["cur"]
                producer(stg[:, ts * 512:(ts + 1) * 512], (skey, ts))
                if ts == TG // 512 - 1:
                    P.add("sp", lambda e: e.dma_start(out=dst[ch * 128:(ch + 1) * 128, tok0:tok0 + TG], in_=stg),
                          reads=[(skey, t_) for t_ in range(TG // 512)], dma_key=("st", skey))
            return put

        put_u = fm_store(YAT)

        def epi_u(ch, ts, ps, pkey):
            ml, mkey = mring.next()
            tt, tkey = tring.next()
            c0t = tok0 + ts * 512
            P.add("sp", lambda e: e.dma_start(out=ml, in_=MT[ch * 128:(ch + 1) * 128, c0t:c0t + 512]), writes=[mkey], dma_key=("ld", mkey))
            P.add("act", lambda e: e.activation(out=tt, in_=ps[:], func=AF.Gelu), reads=[pkey], writes=[tkey])
            put_u(ch, ts, lambda o, skey: P.add("dve", lambda e: e.tensor_tensor(out=o, in0=tt, in1=ml, op=ALU.mult), reads=[tkey, mkey], writes=[skey]))

        gemm_fm(w_in, 0, D, AT, TG, epi_u)
        put_q = fm_store(QT)

        def epi_q(ch, ts, ps, pkey):
            put_q(ch, ts, lambda o, skey: copy_any(o, ps[:], [pkey], [skey]))

        gemm_fm(w_in, 2 * D, D, AT, TG, epi_q)
        for a, dst in ((0, GA), (1, GB)):
            put_g = fm_store(dst)

            def epi_g(ch, ts, ps, pkey, a=a, put_g=put_g):
                put_g(ch, ts, lambda o, skey: P.add("act", lambda e: e.activation(out=o, in_=ps[:], func=AF.Sigmoid, bias=cols[:, 2 + a, ch:ch + 1], scale=1.0),
                                                    reads=[pkey, "cols"], writes=[skey]))

            gemm_fm(w_in, (5 + a) * D, D, AT, TG, epi_g)
        P.barrier()

        o = [0]

        def al(nbytes, dt):
            a = carve(o[0], nbytes, dt)
            o[0] += (nbytes + 63) // 64 * 64
            return a

        qbuf = [(al(2 * TG * 2, BF16).rearrange("p (m t) -> p m t", m=2), ("qb", i)) for i in range(2)]
        kbuf = [(al(2 * CH * 128 * 2, BF16).rearrange("p (m t) -> p m t", m=2), ("kb", i)) for i in range(3)]
        vbuf = [(al(CH * 258 * 2, BF16).rearrange("p (c e) -> p c e", e=258), ("vb", i)) for i in range(3)]
        pbuf = [(al(1024, BF16).rearrange("p (m q) -> p m q", m=2), ("pt", i)) for i in range(4)]
        o1n = al(2 * 256 * 4, F32).rearrange("p (i e) -> p i e", i=2)
        ofin = [(al(1024, F32), ("of", i)) for i in range(2)]
        ybf = [(al(512, BF16), ("yb", i)) for i in range(2)]
        ybT = [(al(2 * 256 * 2, BF16).rearrange("p (a t) -> p a t", a=2), ("ybT", i)) for i in range(2)]
        sm = al(64 * 4, F32)
        sq = al(1024, F32)
        for vb, vkey in vbuf:
            P.add("dve", lambda e, vb=vb: e.memset(vb[:, :, 256:257], 1.0), writes=[vkey])
        qring = Ring(qbuf); kring = Ring(kbuf); vring = Ring(vbuf); pring = Ring(pbuf)
        ofr = Ring(ofin); ybr = Ring(ybf); ybTr = Ring(ybT)
        sring = Ring([(psf[4], ("PS", 4)), (psf[5], ("PS", 5))])
        smi = [0]
        rows = []
        for h in range(H):
            for qt in range(TG // 256):
                m0 = tg * NTB + 2 * qt
                nkb = 4 * m0 + 8
                for kb in range(nkb):
                    i0 = max(0, (kb - 4 * m0) // 4)
                    i1 = 2
                    if DSKIP[h] is not None:
                        i1 = min(2, max(0, (DSKIP[h] + kb - 4 * m0 + 3) // 4))
                    if i1 <= i0:
                        continue
                    rows.append(dict(h=h, qt=qt, kb=kb, m0=m0, nkb=nkb, i0=i0, i1=i1))
        cur = {}

        def emit_S(row):
            h, qt, kb, m0, nkb = row["h"], row["qt"], row["kb"], row["m0"], row["nkb"]
            if cur.get("qh") != h:
                cur["qh"] = h
                qb, qkey = qring.next()
                P.add("sp", lambda e: e.dma_start(out=qb, in_=QT[h * 256:(h + 1) * 256, tok0:tok0 + TG].rearrange("(m p) t -> p m t", p=128)),
                      writes=[qkey], dma_key=("ld", qkey))
                cur["q"] = (qb, qkey)
            if cur.get("unit") != (h, qt):
                cur["unit"] = (h, qt)
                cur["yT"] = ybTr.next()
            if cur.get("chunk") != (h, qt, kb // CH):
                cur["chunk"] = (h, qt, kb // CH)
                kc0 = (kb // CH) * CH
                l0 = kb - kc0
                nk = min(CH, nkb - kc0)
                kb_, kkey = kring.next()
                P.add("sp", lambda e: e.dma_start(
                    out=kb_[:, :, l0 * 128:nk * 128], in_=KT[h * 256:(h + 1) * 256, kb * 128:(kc0 + nk) * 128].rearrange("(m p) t -> p m t", p=128)),
                    writes=[kkey], dma_key=("ld", kkey))
                vb, vkey = vring.next()
                P.add("sp", lambda e: e.dma_start(
                    out=vb[:, l0:nk, 0:256], in_=VV[kb * 128:(kc0 + nk) * 128, h * 256:(h + 1) * 256].rearrange("(c p) e -> p c e", p=128)),
                    writes=[vkey], dma_key=("ld", vkey))
                cur["k"] = (kb_, kkey)
                cur["v"] = (vb, vkey)
            row["q"] = cur["q"]; row["k"] = cur["k"]; row["v"] = cur["v"]; row["yT"] = cur["yT"]
            qb, qkey = row["q"]
            kb_, kkey = row["k"]
            kl = kb % CH
            i0, i1 = row["i0"], row["i1"]
            sps, skey = sring.next()
            row["s"] = (sps, skey)
            for mp in range(2):
                P.add("pe", lambda e, mp=mp: e.matmul(
                    sps[:, mp * 256 + i0 * 128:mp * 256 + i1 * 128], kb_[:, mp, kl * 128:(kl + 1) * 128],
                    qb[:, mp, qt * 256 + i0 * 128:qt * 256 + i1 * 128], start=True, stop=True),
                    reads=[kkey, qkey], writes=[(skey, mp)])

        def emit_rest(row):
            h, qt, kb, m0, nkb = row["h"], row["qt"], row["kb"], row["m0"], row["nkb"]
            sps, skey = row["s"]
            s3 = sps[:].rearrange("p (m q) -> p m q", m=2)
            vb, vkey = row["v"]
            i0, i1 = row["i0"], row["i1"]
            kl = kb % CH
            pt, pkey_ = pring.next()
            if WIDE[h]:
                cw = 4 * m0 - kb + 15
                P.add("act", lambda e: e.activation(
                    out=pt[:, :, i0 * 128:i1 * 128], in_=s3[:, :, i0 * 128:i1 * 128], func=AF.Exp, bias=tabw[:, WIDX[h], cw:cw + 1], scale=SCALE),
                    reads=[(skey, 0), (skey, 1)], writes=[(pkey_, i) for i in range(i0, i1)])
            for i in range(i0, i1):
                dl = 4 * (m0 + i) - kb
                if not WIDE[h]:
                    P.add("act", lambda e, i=i, dl=dl: e.activation(
                        out=pt[:, :, i * 128:(i + 1) * 128], in_=s3[:, :, i * 128:(i + 1) * 128], func=AF.Exp, bias=tab[:, h, dl + 3:dl + 4], scale=SCALE),
                        reads=[(skey, 0), (skey, 1)], writes=[(pkey_, i)])
                r = -dl
                if 0 <= r <= 3:
                    for mp in range(2):
                        P.add("dve", lambda e, i=i, r=r, mp=mp: e.tensor_tensor(out=pt[:, mp, i * 128:(i + 1) * 128], in0=pt[:, mp, i * 128:(i + 1) * 128], in1=masks[:, r, :], op=ALU.mult),
                              reads=[(pkey_, i)], writes=[(pkey_, i)])
                kfirst = 0 if DSKIP[h] is None else max(0, 4 * (m0 + i) - DSKIP[h] + 1)
                for mp in range(2):
                    P.add("pe", lambda e, i=i, mp=mp, first=(kb == kfirst), last=(kb == 4 * (m0 + i) + 3): e.matmul(
                        psf[mp * 2 + i][:, 0:257], pt[:, mp, i * 128:(i + 1) * 128], vb[:, kl, 0:257], start=first, stop=last),
                        reads=[(pkey_, i), vkey], writes=[("PS", mp * 2 + i)])
            if kb != nkb - 1:
                return
            yT, yTkey = row["yT"]
            for i in range(2):
                s0 = (smi[0] % 8) * 8
                smi[0] += 1
                P.add("dve", lambda e, i=i, s0=s0: e.reciprocal(out=sm[:, s0:s0 + 1], in_=psf[i][:, 256:257]), reads=[("PS", i)], writes=[("sm", s0)])
                P.add("dve", lambda e, i=i, s0=s0: e.reciprocal(out=sm[:, s0 + 1:s0 + 2], in_=psf[2 + i][:, 256:257]), reads=[("PS", 2 + i)], writes=[("sm", s0 + 1)])
                P.add("dve", lambda e, i=i, s0=s0: e.tensor_scalar(out=o1n[:, i, :], in0=psf[i][:, 0:256], scalar1=sm[:, s0:s0 + 1], scalar2=None, op0=ALU.mult),
                      reads=[("PS", i), ("sm", s0)], writes=[("o1n", i)])
                of, okey = ofr.next()
                yb, ykey = ybr.next()
                P.add("dve", lambda e, s0=s0: e.tensor_scalar(out=sm[:, s0 + 5:s0 + 6], in0=sm[:, s0 + 1:s0 + 2], scalar1=neglam, scalar2=None, op0=ALU.mult),
                      reads=[("sm", s0 + 1), "neglam"], writes=[("sm", s0 + 5)])
                P.add("dve", lambda e, i=i, of=of, s0=s0: e.scalar_tensor_tensor(out=of, in0=psf[2 + i][:, 0:256], scalar=sm[:, s0 + 5:s0 + 6], in1=o1n[:, i, :],
                                                                              op0=ALU.mult, op1=ALU.add), reads=[("PS", 2 + i), ("sm", s0 + 5), ("o1n", i)], writes=[okey])
                P.add("act", lambda e, of=of, s0=s0: e.activation(out=sq, in_=of, func=AF.Square, accum_out=sm[:, s0 + 2:s0 + 3]), reads=[okey], writes=["sq", ("sm", s0 + 2)])
                P.add("act", lambda e, s0=s0: e.activation(out=sm[:, s0 + 3:s0 + 4], in_=sm[:, s0 + 2:s0 + 3], func=AF.Sqrt, bias=epsc[:, 1:2], scale=1.0 / 256),
                      reads=[("sm", s0 + 2)], writes=[("sm", s0 + 3)])
                P.add("dve", lambda e, s0=s0: e.reciprocal(out=sm[:, s0 + 4:s0 + 5], in_=sm[:, s0 + 3:s0 + 4]), reads=[("sm", s0 + 3)], writes=[("sm", s0 + 4)])
                P.add("dve", lambda e, of=of, yb=yb, s0=s0: e.scalar_tensor_tensor(out=yb, in0=of, scalar=sm[:, s0 + 4:s0 + 5], in1=subg, op0=ALU.mult, op1=ALU.mult),
                      reads=[okey, ("sm", s0 + 4), "subg"], writes=[ykey])
                tp, tkey = tpr.next()
                for a in range(2):
                    P.add("pe", lambda e, tp=tp, a=a, yb=yb: e.transpose(out=tp[:, a * 128:(a + 1) * 128], in_=yb[:, a * 128:(a + 1) * 128], identity=ident_b),
                          reads=[ykey], writes=[tkey])
                copy_any(yT[:, :, i * 128:(i + 1) * 128], tp[:, 0:256].rearrange("p (a t) -> p a t", a=2), [tkey], [(yTkey, i)])
            c0t = tok0 + qt * 256
            P.add("sp", lambda e: e.dma_start(out=YBT[h * 256:(h + 1) * 256, c0t:c0t + 256].rearrange("(a p) t -> p a t", p=128), in_=yT),
                  reads=[(yTkey, i_) for i_ in range(2)], dma_key=("st", yTkey))

        emit_S(rows[0])
        for ri, row in enumerate(rows):
            if ri + 1 < len(rows):
                emit_S(rows[ri + 1])
            emit_rest(row)
        P.barrier()

        AT = carve(0, KC * TG * 2, BF16).rearrange("p (c t) -> p c t", c=KC)
        soff = KC * TG * 2
        gl = [(carve(soff + i * 1024, 1024, BF16), ("gl", i)) for i in range(3)]
        tl = [(carve(soff + 3072 + i * 2048, 2048, F32), ("tl", i)) for i in range(3)]
        tmpm = [(carve(soff + 3072 + 6144 + i * 2048, 2048, F32), ("tmpm", i)) for i in range(2)]
        fst = [(carve(soff + 3072 + 6144 + 4096 + i * TG * 2, TG * 2, BF16), ("fst", i)) for i in range(3)]
        glr = Ring(gl); tlr = Ring(tl); tmr = Ring(tmpm); fring = Ring(fst)
        load_AT(AT, YAT, KC, tok0, TG)

        def epi_a(ch, ts, ps, pkey):
            g_, gkey = glr.next()
            t_, tkey = tlr.next()
            c0t = tok0 + ts * 512
            P.add("sp", lambda e: e.dma_start(out=g_, in_=GA[ch * 128:(ch + 1) * 128, c0t:c0t + 512]), writes=[gkey], dma_key=("ld", gkey))
            P.add("dve", lambda e: e.tensor_tensor(out=t_, in0=ps[:], in1=g_, op=ALU.mult), reads=[pkey, gkey], writes=[tkey])
            P.add("sp", lambda e: e.dma_start(out=T1[ch * 128:(ch + 1) * 128, c0t:c0t + 512], in_=t_), reads=[tkey], dma_key=("st", tkey))

        gemm_fm(w_br_a, 0, D, AT, TG, epi_a)
        P.barrier()
        load_AT(AT, YBT, KC, tok0, TG)
        put_m = fm_store(MGT)

        def epi_b(ch, ts, ps, pkey):
            g_, gkey = glr.next()
            t_, tkey = tlr.next()
            m_, mkey = tmr.next()
            c0t = tok0 + ts * 512
            P.add("sp", lambda e: e.dma_start(out=g_, in_=GB[ch * 128:(ch + 1) * 128, c0t:c0t + 512]), writes=[gkey], dma_key=("ld", gkey))
            P.add("sp", lambda e: e.dma_start(out=t_, in_=T1[ch * 128:(ch + 1) * 128, c0t:c0t + 512]), writes=[tkey], dma_key=("ld", tkey))
            P.add("dve", lambda e: e.tensor_tensor(out=m_, in0=ps[:], in1=g_, op=ALU.mult), reads=[pkey, gkey], writes=[mkey])
            put_m(ch, ts, lambda o_, skey: P.add("dve", lambda e: e.tensor_tensor(out=o_, in0=m_, in1=t_, op=ALU.add), reads=[mkey, tkey], writes=[skey]))

        gemm_fm(w_br_b, 0, D, AT, TG, epi_b)
        P.barrier()
        load_AT(AT, MGT, KC, tok0, TG)

        def tm_residual(res_src, dst, accs, xl, ost, post=None):
            xlr = Ring(xl); osr = Ring(ost)

            def epi(pi, npc, tb, nt, ps, pkey):
                acc, akey = accs[tb % len(accs)]
                r0 = tok0_cur[0] + tb * 128
                if pi == 0:
                    x_, xkey = xlr.next()
                    P.add("sp", lambda e: e.dma_start(out=x_, in_=res_src[r0:r0 + 128, nt * 512:(nt + 1) * 512]), writes=[xkey], dma_key=("ld", xkey))
                    tgt, tk = (acc, akey) if npc > 1 else osr.next()
                    P.add("dve", lambda e: e.tensor_tensor(out=tgt, in0=ps[:], in1=x_, op=ALU.add), reads=[pkey, xkey], writes=[tk])
                    if npc > 1:
                        return
                elif pi < npc - 1:
                    P.add("dve", lambda e: e.tensor_tensor(out=acc, in0=ps[:], in1=acc, op=ALU.add), reads=[pkey, akey], writes=[akey])
                    return
                else:
                    tgt, tk = osr.next()
                    P.add("dve", lambda e: e.tensor_tensor(out=tgt, in0=ps[:], in1=acc, op=ALU.add), reads=[pkey, akey], writes=[tk])
                P.add("sp", lambda e: e.dma_start(out=dst[r0:r0 + 128, nt * 512:(nt + 1) * 512], in_=tgt), reads=[tk], dma_key=("st", tk))
            return epi

        tok0_cur = [tok0]
        a0 = soff
        accs = [(carve(a0 + i * 2048, 2048, F32), ("acc", i)) for i in range(NTB)]
        xl = [(carve(a0 + NTB * 2048 + i * 2048, 2048, F32), ("xl", i)) for i in range(3)]
        ost = [(carve(a0 + NTB * 2048 + 6144 + i * 2048, 2048, F32), ("ost", i)) for i in range(3)]
        gemm_tm(w_o, KC, 0, D, AT, list(range(NTB)), tm_residual(x_own, X1, accs, xl, ost))
        P.barrier()

        norm_phase(X1, tok0, NTB, g_ffn, AT, off_n)
        P.barrier()
        fst = [(carve(off_n + i * TG * 2, TG * 2, BF16), ("fst", i)) for i in range(3)]
        stmp = [(carve(off_n + 3 * TG * 2 + i * 2048, 2048, F32), ("stmp", i)) for i in range(3)]
        fring = Ring(fst); sring2 = Ring(stmp)
        put_f = fm_store(FFA)
        for pg in range(DFF // 256):
            wg, wgk = wload(wview_fm(w_gu, pg * 256, 256), KC, 256)
            wu, wuk = wload(wview_fm(w_gu, DFF + pg * 256, 256), KC, 256)
            for c2 in range(2):
                for ts in range(TG // 512):
                    pg_, pgk = gb.next()
                    pu_, puk = gb.next()
                    for (ps, pk, wv, wk) in ((pg_, pgk, wg, wgk), (pu_, puk, wu, wuk)):
                        for k in range(KC):
                            rk = [("AT", ts * 4 + i, k // 8) for i in range(4)]
                            P.add("pe", lambda e, ps=ps, wv=wv, k=k, c2=c2, ts=ts: e.matmul(
                                ps[:], wv[:, k, c2 * 128:(c2 + 1) * 128], AT[:, k, ts * 512:(ts + 1) * 512], start=(k == 0), stop=(k == KC - 1)),
                                reads=[wk] + rk, writes=[pk])
                    s_, sk = sring2.next()
                    P.add("act", lambda e, s_=s_, pg_=pg_: e.activation(out=s_, in_=pg_[:], func=AF.Silu), reads=[pgk], writes=[sk])
                    put_f(pg * 2 + c2, ts, lambda o_, skey, s_=s_, sk=sk, pu_=pu_, puk=puk: P.add(
                        "dve", lambda e: e.tensor_tensor(out=o_, in0=pu_[:], in1=s_, op=ALU.mult), reads=[puk, sk], writes=[skey]))
        P.barrier()

        ATd = carve(0, FC * TGD * 2, BF16).rearrange("p (c t) -> p c t", c=FC)
        a0 = FC * TGD * 2
        nd = TGD // 128
        accs = [(carve(a0 + i * 2048, 2048, F32), ("acc", i)) for i in range(nd)]
        xl = [(carve(a0 + nd * 2048 + i * 2048, 2048, F32), ("xl", i)) for i in range(3)]
        ost = [(carve(a0 + nd * 2048 + 6144 + i * 2048, 2048, F32), ("ost", i)) for i in range(3)]
        for sd in range(TG // TGD):
            tok0_cur[0] = tok0 + sd * TGD
            load_AT(ATd, FFA, FC, tok0_cur[0], TGD, nparts=6)
            gemm_tm(w_down, FC, 0, D, ATd, list(range(nd)), tm_residual(X1, X2, accs, xl, ost))
            P.barrier()
        tok0_cur[0] = tok0

        norm_phase(X2, tok0, NTB, g_ple, AT, off_n)
        pT = carve(off_n, PC * TG * 2, BF16).rearrange("p (c t) -> p c t", c=PC)
        pl = carve(off_n + PC * TG * 2, PLE * 4, F32)
        plb = carve(off_n + PC * TG * 2 + PLE * 4, PLE * 2, BF16)
        P.barrier()
        for tb in range(NTB):
            r0 = tok0 + tb * 128
            P.add("sp", lambda e, r0=r0: e.dma_start(out=pl, in_=p_own[r0:r0 + 128, :]), writes=["pl"], dma_key=("ld", "pl"))
            P.add("dve", lambda e: e.tensor_copy(out=plb, in_=pl), reads=["pl"], writes=["plb"])
            tp, tkey = tpr.next()
            for c in range(PC):
                P.add("pe", lambda e, tp=tp, c=c: e.transpose(out=tp[:, c * 128:(c + 1) * 128], in_=plb[:, c * 128:(c + 1) * 128], identity=ident_b),
                      reads=["plb"], writes=[tkey])
            copy_any(pT[:, :, tb * 128:(tb + 1) * 128], tp[:, 0:PC * 128].rearrange("p (c t) -> p c t", c=PC), [tkey], [("pT", tb)])
        a0 = off_n + PC * TG * 2 + PLE * 6
        a0 = (a0 + 63) // 64 * 64
        accs = [(carve(a0 + i * 2048, 2048, F32), ("acc", i)) for i in range(NTB)]
        xl = [(carve(a0 + NTB * 2048 + i * 2048, 2048, F32), ("xl", i)) for i in range(3)]
        ost = [(carve(a0 + NTB * 2048 + 6144 + i * 2048, 2048, F32), ("ost", i)) for i in range(3)]
        sg_ = [(carve(a0 + NTB * 2048 + 12288 + i * 2048, 2048, F32), ("sg", i)) for i in range(2)]
        xlr = Ring(xl); osr = Ring(ost); sgr = Ring(sg_)
        ppr = Ring([(psf[4], ("PS", 4)), (psf[5], ("PS", 5))])
        wpp_cur = {}

        def epi_p(pi, npc, tb, nt, ps, pkey):
            acc, akey = accs[tb]
            if pi == 0 and npc > 1:
                copy_any(acc, ps[:], [pkey], [akey])
                return
            src = ps[:]
            rd = [pkey]
            if npc > 1:
                P.add("dve", lambda e: e.tensor_tensor(out=acc, in0=ps[:], in1=acc, op=ALU.add), reads=[pkey, akey], writes=[akey])
                src = acc
                rd = [akey]
            s_, sk = sgr.next()
            P.add("act", lambda e: e.activation(out=s_, in_=src, func=AF.Sigmoid), reads=rd, writes=[sk])
            if wpp_cur.get("nt") != nt:
                wpp_cur["nt"] = nt
                wpp_cur["w"] = wload(wview_tm(w_pp, 0, PC, nt * 512, 512), PC, 512)
            wv, wkey = wpp_cur["w"]
            pp, ppk = ppr.next()
            for c in range(PC):
                P.add("pe", lambda e, pp=pp, c=c, wv=wv: e.matmul(pp[:], pT[:, c, tb * 128:(tb + 1) * 128], wv[:, c, :], start=(c == 0), stop=(c == PC - 1)),
                      reads=[wkey, ("pT", tb)], writes=[ppk])
            x_, xkey = xlr.next()
            r0 = tok0 + tb * 128
            P.add("sp", lambda e: e.dma_start(out=x_, in_=X2[r0:r0 + 128, nt * 512:(nt + 1) * 512]), writes=[xkey], dma_key=("ld", xkey))
            P.add("dve", lambda e: e.tensor_tensor(out=s_, in0=pp[:], in1=s_, op=ALU.mult), reads=[ppk, sk], writes=[sk])
            o_, ok = osr.next()
            P.add("dve", lambda e: e.tensor_tensor(out=o_, in0=s_, in1=x_, op=ALU.add), reads=[sk, xkey], writes=[ok])
            P.add("sp", lambda e: e.dma_start(out=X3[r0:r0 + 128, nt * 512:(nt + 1) * 512], in_=o_), reads=[ok], dma_key=("st", ok))

        gemm_tm(w_pg, KC, 0, D, AT, list(range(NTB)), epi_p)
        P.barrier()

        xbs = [(carve(i * 4 * D, 4 * D, F32), ("xb", i)) for i in range(2)]
        gbc = carve(8 * D, 4 * D, F32)
        xn = carve(12 * D, 2 * D, BF16)
        ss = carve(14 * D, 16, F32)
        fos = [(carve(14 * D + 64 + i * 4 * D, 4 * D, F32), ("fo", i)) for i in range(2)]
        load_bcast(gbc, g_final, "gbc")
        finals = []
        for tb in range(NTB):
            r0 = tok0 + tb * 128
            xb, xbkey = xbs[tb % 2]
            fo, fokey = fos[tb % 2]
            finals.append(norm_block(X3, r0, gbc, xb, xbkey, xn, ss, tb, None, final_out=(fo, fokey, out_d[r0:r0 + 128, :])))
        P.barrier()

    for tg_ in range(NO // TG):
        do_group(tg_)

    P.emit(final_waits=[op for op in P.dma_last.values() if not (isinstance(op.dma_key, tuple) and op.dma_key[0] == "W")])
    st.close()
    return nc


def alibi_slopes(n_heads):
    return [float(np.exp2(np.float32(-8.0) * np.float32(i) / np.float32(n_heads))) for i in range(1, n_heads + 1)]


def make_in_maps(cfg, inputs):
    D = cfg["D"]; S = cfg["S"]
    x = np.asarray(inputs["x"], dtype=np.float32)
    p = np.asarray(inputs["p"], dtype=np.float32)[0]
    B = x.shape[0]
    NBA = S // 128
    NOB = NBA // 4

    def w(name, shape=None):
        a = np.ascontiguousarray(np.asarray(inputs[name], dtype=np.float32))
        return a.reshape(shape) if shape is not None else a

    G = D // 256
    shared = {
        "g_mix": w("g_mix", (1, D)), "w_in": w("w_in", (D, 7 * D)), "b_gate": w("b_gate", (2, D)),
        "ln_v_g": w("ln_v_g", (1, D)), "ln_v_b": w("ln_v_b", (1, D)),
        "w_s": w("w_s", (G, 128, 128)), "b_s": w("b_s", (1, G * 128)),
        "lambda_q1": w("lambda_q1", (1, 128)), "lambda_k1": w("lambda_k1", (1, 128)),
        "lambda_q2": w("lambda_q2", (1, 128)), "lambda_k2": w("lambda_k2", (1, 128)),
        "subln_g": w("subln_g", (1, 256)),
        "w_br_a": w("w_br_a", (D, D)), "w_br_b": w("w_br_b", (D, D)), "w_o": w("w_o", (D, D)),
        "g_ffn": w("g_ffn", (1, D)), "w_gu": w("w_gu", (D, 2 * cfg["DFF"])), "w_down": w("w_down", (cfg["DFF"], D)),
        "g_ple": w("g_ple", (1, D)), "w_ple_gate": w("w_ple_gate", (D, D)), "w_ple_proj": w("w_ple_proj", (cfg["PLE"], D)),
        "g_final": w("g_final", (1, D)),
    }
    maps = []
    for c in range(4 * B):
        b, j = c // 4, c % 4
        xb = x[b].reshape(NBA, 128, D)
        pb = p[b].reshape(NBA, 128, -1)
        m = dict(shared)
        m["x_all"] = np.ascontiguousarray(x[b])
        m["x_own"] = np.ascontiguousarray(xb[j::4].reshape(NOB * 128, D))
        m["p_own"] = np.ascontiguousarray(pb[j::4].reshape(NOB * 128, -1))
        m["jv"] = np.full((128, 1), float(j), dtype=np.float32)
        maps.append(m)
    return maps


def assemble(cfg, results, B):
    D = cfg["D"]; S = cfg["S"]
    NBA = S // 128
    NOB = NBA // 4
    out = np.empty((B, NBA, 128, D), dtype=np.float32)
    for c in range(4 * B):
        b, j = c // 4, c % 4
        out[b, j::4] = np.asarray(results[c]["out"], dtype=np.float32).reshape(NOB, 128, D)
    return out.reshape(B, S, D)


FULL_CFG = {"D": 4096, "S": 8192, "DFF": 11008, "PLE": 256, "slopes": alibi_slopes(16)}


def kernel(**inputs):
    cfg = FULL_CFG
    nc = build(cfg)
    maps = make_in_maps(cfg, inputs)
    res = run_bass_kernel_spmd(nc, maps, core_ids=list(range(8)))
    return assemble(cfg, res.results, 2)
```

```python
import contextlib
import math
import numpy as np
import concourse.bass as bass
import concourse.mybir as mybir
from concourse.bass_utils import run_bass_kernel_spmd

F32 = mybir.dt.float32
BF16 = mybir.dt.bfloat16
I32 = mybir.dt.int32
AF = mybir.ActivationFunctionType
ALU = mybir.AluOpType
AX = mybir.AxisListType

RMS_EPS = 1e-6
LN_EPS = 1e-5
HEAD_DIM = 128
NEG_BIG = 30000.0


class Op:
    __slots__ = ("eng", "fn", "deps", "inc", "val", "dma_key", "dma_val")


class Prog:
    ENGS = ("pe", "act", "dve", "pool", "sp")

    def __init__(self, nc):
        self.nc = nc
        self.ops = {e: [] for e in self.ENGS}
        self.last_w = {}
        self.readers = {}
        self.dma_cnt = {}
        self.dma_last = {}
        self.last_real = {e: None for e in self.ENGS}

    def add(self, eng, fn, reads=(), writes=(), dma_key=None, extra_deps=()):
        op = Op()
        op.eng = eng; op.fn = fn
        op.inc = False; op.val = None; op.dma_key = dma_key; op.dma_val = None
        deps = []
        for r in reads:
            w = self.last_w.get(r)
            if w is not None:
                deps.append((w, True))
        for wk in writes:
            w = self.last_w.get(wk)
            if w is not None:
                deps.append((w, False))
            for rd in self.readers.get(wk, ()):
                deps.append((rd, False))
        for d in extra_deps:
            deps.append((d, True))
        if dma_key is not None:
            c = self.dma_cnt.get(dma_key, 0) + 1
            self.dma_cnt[dma_key] = c
            op.dma_val = 16 * c
            prev = self.dma_last.get(dma_key)
            if prev is not None:
                deps.append((prev, True))
            self.dma_last[dma_key] = op
        fd = []
        seen = set()
        for d, raw in deps:
            if d is op or id(d) in seen:
                continue
            if d.dma_key is None and d.eng == eng:
                if eng == "pe" or not raw:
                    continue
            seen.add(id(d))
            fd.append(d)
        op.deps = fd
        for d in fd:
            if d.dma_key is None:
                d.inc = True
        for r in reads:
            self.readers.setdefault(r, []).append(op)
        for wk in writes:
            self.last_w[wk] = op
            self.readers[wk] = []
        self.ops[eng].append(op)
        if fn is not None and dma_key is None:
            self.last_real[eng] = op
        return op

    def barrier(self, engs=("pe", "act", "dve", "sp"), keep=("W",)):
        lasts = [self.last_real[e] for e in self.ENGS if self.last_real[e] is not None and e != "pool"]
        dmas = [op for k, op in self.dma_last.items() if not (isinstance(k, tuple) and k[0] in keep)]
        for e in engs:
            self.add(e, None, extra_deps=lasts + dmas)
        for k in list(self.last_w.keys()):
            if not (isinstance(k, tuple) and k[0] in keep):
                del self.last_w[k]
        for k in list(self.readers.keys()):
            if not (isinstance(k, tuple) and k[0] in keep):
                del self.readers[k]

    def emit(self, final_waits=()):
        nc = self.nc
        for e in self.ENGS:
            c = 0
            for op in self.ops[e]:
                if op.dma_key is None and op.inc:
                    assert op.fn is not None
                    c += 1
                    op.val = c
        with contextlib.ExitStack() as st:
            esem = {e: st.enter_context(nc.semaphore("es_" + e)) for e in self.ENGS}
            dsem = {k: st.enter_context(nc.semaphore("ds_%d" % i)) for i, k in enumerate(self.dma_cnt)}
            block = st.enter_context(nc.Block())

            def run(e, engobj):
                waited = {}
                for op in self.ops[e]:
                    need = {}
                    for d in op.deps:
                        if d.dma_key is not None:
                            s, v = dsem[d.dma_key], d.dma_val
                        else:
                            s, v = esem[d.eng], d.val
                        key = id(s)
                        if waited.get(key, 0) >= v:
                            continue
                        if key not in need or need[key][1] < v:
                            need[key] = (s, v)
                    for key, (s, v) in need.items():
                        engobj.wait_ge(s, v)
                        waited[key] = v
                    if op.fn is None:
                        continue
                    ins = op.fn(engobj)
                    if op.dma_key is not None:
                        ins.then_inc(dsem[op.dma_key], 16)
                    elif op.inc:
                        ins.then_inc(esem[e], 1)
                if e == "sp":
                    for op in final_waits:
                        engobj.wait_ge(dsem[op.dma_key], op.dma_val)

            @block.tensor
            def _(t):
                run("pe", t)

            @block.scalar
            def _(t):
                run("act", t)

            @block.vector
            def _(t):
                run("dve", t)

            @block.gpsimd
            def _(t):
                run("pool", t)

            @block.sync
            def _(t):
                run("sp", t)


class Ring:
    def __init__(self, items):
        self.items = items
        self.i = 0

    def next(self):
        it = self.items[self.i % len(self.items)]
        self.i += 1
        return it


def build(cfg):
    D = cfg["D"]; S = cfg["S"]; DFF = cfg["DFF"]; PLE = cfg["PLE"]
    slopes = [float(s) for s in cfg["slopes"]]
    lam_init = 0.2
    H = D // 256; G = D // 256; KC = D // 128; FC = DFF // 128; PC = PLE // 128
    NBA = S // 128; NOB = NBA // 4; NO = NOB * 128
    TG = min(cfg.get("TG", 1024), NO)
    TGK = 512
    TGD = min(cfg.get("TGD", 512), TG)
    CH = min(16, NBA)
    NTAB = NBA + 3
    NTW = NBA + 8
    DSKIP = []
    for sl in slopes:
        d = 0
        while sl * (128 * d - 63) <= 128.0 and d <= NBA:
            d += 1
        DSKIP.append(d if (d <= NBA and cfg.get("skip", True)) else None)
    WIDE = [bool(cfg.get("wide", True)) and sl * 639.0 <= 40.5 for sl in slopes]
    WIDX = {h: i for i, h in enumerate([h for h in range(len(slopes)) if WIDE[h]])}
    NW = max(1, len(WIDX))
    SCALE = HEAD_DIM ** -0.5
    assert TG % 512 == 0 and TGK % 512 == 0 and D % 512 == 0 and DFF % 256 == 0

    nc = bass.Bass("TRN2", target_bir_lowering=False)

    def din(name, shape):
        return nc.dram_tensor(name, list(shape), F32, kind="ExternalInput").ap()

    x_all = din("x_all", [S, D]); x_own = din("x_own", [NO, D]); p_own = din("p_own", [NO, PLE])
    jv_d = din("jv", [128, 1])
    g_mix = din("g_mix", [1, D]); w_in = din("w_in", [D, 7 * D]); b_gate = din("b_gate", [2, D])
    ln_v_g = din("ln_v_g", [1, D]); ln_v_b = din("ln_v_b", [1, D])
    w_s = din("w_s", [G, 128, 128]); b_s = din("b_s", [1, G * 128])
    lq1 = din("lambda_q1", [1, 128]); lk1 = din("lambda_k1", [1, 128])
    lq2 = din("lambda_q2", [1, 128]); lk2 = din("lambda_k2", [1, 128])
    subln_g = din("subln_g", [1, 256])
    w_br_a = din("w_br_a", [D, D]); w_br_b = din("w_br_b", [D, D]); w_o = din("w_o", [D, D])
    g_ffn = din("g_ffn", [1, D]); w_gu = din("w_gu", [D, 2 * DFF]); w_down = din("w_down", [DFF, D])
    g_ple = din("g_ple", [1, D]); w_pg = din("w_ple_gate", [D, D]); w_pp = din("w_ple_proj", [PLE, D])
    g_final = din("g_final", [1, D])
    out_d = nc.dram_tensor("out", [NO, D], F32, kind="ExternalOutput").ap()

    def dscr(name, shape, dt):
        if cfg.get("debug"):
            return nc.dram_tensor(name, list(shape), dt, kind="ExternalOutput").ap()
        return nc.dram_tensor(name, list(shape), dt).ap()

    KT = dscr("KT", [D, S], BF16); VV = dscr("VV", [S, D], BF16)
    QT = dscr("QT", [D, NO], BF16); MT = dscr("MT", [D, NO], BF16)
    YAT = dscr("YAT", [D, NO], BF16); YBT = dscr("YBT", [D, NO], BF16)
    GA = dscr("GA", [D, NO], BF16); GB = dscr("GB", [D, NO], BF16)
    T1 = dscr("T1", [D, NO], F32); MGT = dscr("MGT", [D, NO], BF16)
    X1 = dscr("X1", [NO, D], F32); X2 = dscr("X2", [NO, D], F32); X3 = dscr("X3", [NO, D], F32)
    FFA = dscr("FFA", [DFF, NO], BF16)

    P = Prog(nc)
    st = contextlib.ExitStack()
    ARENA_B = 128 * 1024
    arena = st.enter_context(nc.sbuf_tensor("arena", [128, ARENA_B // 2], BF16))
    wring_t = [st.enter_context(nc.sbuf_tensor("wr%d" % i, [128, 8192], BF16)) for i in range(3)]
    CONST_F = 512 + 15 + H * NTAB + NW * NTW + 4 * KC + KC * 128 + 256 + 2 + 3
    cst = st.enter_context(nc.sbuf_tensor("cst", [128, CONST_F], F32))
    cbf = st.enter_context(nc.sbuf_tensor("cbf", [128, 128 + 4 * 128 + G * 128], BF16))
    psf = [st.enter_context(nc.psum_tensor("psf%d" % i, [128, 512], F32)) for i in range(6)]
    psb = [st.enter_context(nc.psum_tensor("psb%d" % i, [128, 1024], BF16)) for i in range(2)]

    co = [0]

    def cf(n):
        a = cst[:, co[0]:co[0] + n]
        co[0] += n
        return a

    ident_f = cf(128); valf = cf(128); tri_f = cf(128); ones_f = cf(128)
    jv = cf(1); jv128 = cf(1); jsh = cf(4); neglam = cf(1); lamtmp = cf(8)
    tab = cf(H * NTAB).rearrange("p (h c) -> p h c", h=H)
    tabw = cf(NW * NTW).rearrange("p (h c) -> p h c", h=NW)
    cols = cf(4 * KC).rearrange("p (a c) -> p a c", a=4)
    B2 = cf(KC * 128).rearrange("p (c t) -> p c t", c=KC)
    subg = cf(256)
    epsc = cf(2)
    assert co[0] <= CONST_F
    bo = [0]

    def cb(n):
        a = cbf[:, bo[0]:bo[0] + n]
        bo[0] += n
        return a

    ident_b = cb(128)
    masks = cb(4 * 128).rearrange("p (r q) -> p r q", r=4)
    wT = cb(G * 128).rearrange("p (g t) -> p g t", g=G)

    def carve(off, nbytes, dt):
        assert off % 4 == 0 and off + nbytes <= ARENA_B, (off, nbytes)
        a = arena[:, off // 2:(off + nbytes) // 2]
        return a.bitcast(F32) if dt == F32 else a

    wr = Ring([(wring_t[i], ("W", i)) for i in range(3)])

    def wload(src_view, kc, n):
        t, key = wr.next()
        v = t[:].rearrange("p (c n) -> p c n", n=n)[:, 0:kc, :]
        P.add("pool", lambda e, v=v, s=src_view: e.dma_start(out=v, in_=s), writes=[key], dma_key=key)
        return v, key

    def wview_fm(w, c0, n):
        return w[:, c0:c0 + n].rearrange("(c p) n -> p c n", p=128)

    def wview_tm(w, k0, kp, c0, n):
        return w[k0 * 128:(k0 + kp) * 128, c0:c0 + n].rearrange("(c p) n -> p c n", p=128)

    gb = Ring([(psf[i], ("PS", i)) for i in range(4)])

    def gemm_fm(w, col0, ncols, AT, tg, epi, kcn=None, key_at="AT", hook=None):
        kcn = kcn or KC
        for ng in range(ncols // 256):
            wv, wkey = wload(wview_fm(w, col0 + ng * 256, 256), kcn, 256)
            for c2 in range(2):
                for ts in range(tg // 512):
                    ps, pkey = gb.next()
                    for k in range(kcn):
                        rk = [(key_at, ts * 4 + i, k // 8) for i in range(4)]
                        P.add("pe", lambda e, ps=ps, wv=wv, k=k, c2=c2, ts=ts: e.matmul(
                            ps[:], wv[:, k, c2 * 128:(c2 + 1) * 128], AT[:, k, ts * 512:(ts + 1) * 512],
                            start=(k == 0), stop=(k == kcn - 1)),
                            reads=[wkey] + rk, writes=[pkey])
                    epi(ng * 2 + c2, ts, ps, pkey)
            if hook is not None:
                hook(ng)

    def gemm_tm(w, kcn, col0, ncols, AT, tbs, epi, key_at="AT"):
        pieces = [(k0, min(16, kcn - k0)) for k0 in range(0, kcn, 16)]
        for nt in range(ncols // 512):
            for pi, (k0, kp) in enumerate(pieces):
                wv, wkey = wload(wview_tm(w, k0, kp, col0 + nt * 512, 512), kp, 512)
                for tb in tbs:
                    ps, pkey = gb.next()
                    for k in range(kp):
                        P.add("pe", lambda e, ps=ps, wv=wv, k=k, k0=k0, tb=tb: e.matmul(
                            ps[:], AT[:, k0 + k, tb * 128:(tb + 1) * 128], wv[:, k, :],
                            start=(k == 0), stop=(k == kp - 1)),
                            reads=[wkey, (key_at, tb, (k0 + k) // 8)], writes=[pkey])
                    epi(pi, len(pieces), tb, nt, ps, pkey)

    tpr = Ring([(psb[i], ("PB", i)) for i in range(2)])
    evtoggle = [0]

    def copy_any(out, in_, reads, writes):
        evtoggle[0] ^= 1
        if evtoggle[0]:
            P.add("act", lambda e: e.copy(out=out, in_=in_), reads=reads, writes=writes)
        else:
            P.add("dve", lambda e: e.tensor_copy(out=out, in_=in_), reads=reads, writes=writes)

    def load_bcast(dst, vec, key):
        P.add("sp", lambda e: e.dma_start(out=dst, in_=vec.broadcast_to([128, vec.shape[1]])),
              writes=[key], dma_key=("ld", key))

    def norm_block(src, r0, gbc, xb, xbkey, xn, ss, tbslot, AT, final_out=None, key_at="AT", xnkey="xn", defer=False):
        P.add("sp", lambda e: e.dma_start(out=xb, in_=src[r0:r0 + 128, :]), writes=[xbkey], dma_key=("ld", xbkey))
        P.add("act", lambda e: e.activation(out=xn, in_=xb, func=AF.Square, accum_out=ss[:, 0:1]),
              reads=[xbkey], writes=[xnkey, "ss0"])
        P.add("act", lambda e: e.activation(out=ss[:, 1:2], in_=ss[:, 0:1], func=AF.Sqrt, bias=epsc[:, 0:1], scale=1.0 / D),
              reads=["ss0"], writes=["ss1"])
        P.add("dve", lambda e: e.reciprocal(out=ss[:, 2:3], in_=ss[:, 1:2]), reads=["ss1"], writes=["ss2"])
        if final_out is not None:
            fo, fokey, dst = final_out
            P.add("dve", lambda e: e.scalar_tensor_tensor(out=fo, in0=xb, scalar=ss[:, 2:3], in1=gbc, op0=ALU.mult, op1=ALU.mult),
                  reads=[xbkey, "ss2", "gbc"], writes=[fokey])
            return P.add("sp", lambda e: e.dma_start(out=dst, in_=fo), reads=[fokey], dma_key=("st", fokey))
        P.add("dve", lambda e: e.scalar_tensor_tensor(out=xn, in0=xb, scalar=ss[:, 2:3], in1=gbc, op0=ALU.mult, op1=ALU.mult),
              reads=[xbkey, "ss2", "gbc"], writes=[xnkey])

        def back():
            norm_back(xn, xnkey, tbslot, AT, key_at)
        if defer:
            return back
        back()

    def norm_back(xn, xnkey, tbslot, AT, key_at):
        for c0 in range(0, KC, 8):
            nn = min(8, KC - c0)
            tp, tkey = tpr.next()
            for c in range(nn):
                P.add("pe", lambda e, tp=tp, c=c, c0=c0: e.transpose(out=tp[:, c * 128:(c + 1) * 128],
                                                                     in_=xn[:, (c0 + c) * 128:(c0 + c + 1) * 128], identity=ident_b),
                      reads=[xnkey], writes=[tkey])
            copy_any(AT[:, c0:c0 + nn, tbslot * 128:(tbslot + 1) * 128],
                     tp[:, 0:nn * 128].rearrange("p (c t) -> p c t", c=nn), [tkey], [(key_at, tbslot, c0 // 8)])

    def norm_phase(src, row0, ntb, gvec, AT, off):
        xbs = [(carve(off + i * 4 * D, 4 * D, F32), ("xb", i)) for i in range(2)]
        gbc = carve(off + 8 * D, 4 * D, F32)
        xn = carve(off + 12 * D, 2 * D, BF16)
        ss = carve(off + 14 * D, 16, F32)
        load_bcast(gbc, gvec, "gbc")
        for tb in range(ntb):
            xb, xbkey = xbs[tb % 2]
            norm_block(src, row0 + tb * 128, gbc, xb, xbkey, xn, ss, tb, AT)

    def load_AT(AT, src, kcn, c0, ntok, nparts=4):
        step = ((kcn + nparts - 1) // nparts + 7) // 8 * 8
        for i, k0 in enumerate(range(0, kcn, step)):
            k1 = min(kcn, k0 + step)
            P.add("sp", lambda e, k0=k0, k1=k1: e.dma_start(
                out=AT[:, k0:k1, :], in_=src[k0 * 128:k1 * 128, c0:c0 + ntok].rearrange("(c p) t -> p c t", p=128)),
                writes=[("AT", tb, cg) for tb in range(ntok // 128) for cg in range(k0 // 8, (k1 - 1) // 8 + 1)], dma_key=("ldat", i))

    scr_i = carve(0, 128 * 4, F32).bitcast(I32)
    scr_i2 = carve(512, NTAB * 4, F32).bitcast(I32)
    D1 = carve(2048, NTAB * 4, F32); DL = carve(4096, NTAB * 4, F32)
    t1 = carve(6144, NTAB * 4, F32); t2 = carve(8192, NTAB * 4, F32)
    P.add("pool", lambda e: e.iota(out=scr_i, pattern=[[1, 128]], base=0, channel_multiplier=-1), writes=["scr_i"])
    P.add("dve", lambda e: e.tensor_copy(out=valf, in_=scr_i), reads=["scr_i"], writes=["valf"])
    P.add("dve", lambda e: e.tensor_scalar(out=ident_f, in0=valf, scalar1=0.0, scalar2=None, op0=ALU.is_equal), reads=["valf"], writes=["ident_f"])
    P.add("dve", lambda e: e.tensor_scalar(out=tri_f, in0=valf, scalar1=0.0, scalar2=None, op0=ALU.is_ge), reads=["valf"], writes=["tri_f"])
    P.add("dve", lambda e: e.tensor_copy(out=ident_b, in_=ident_f), reads=["ident_f"], writes=["ident_b"])
    P.add("dve", lambda e: e.memset(ones_f, 1.0), writes=["ones_f"])
    P.add("dve", lambda e: e.memset(epsc[:, 0:1], RMS_EPS), writes=["epsc"])
    P.add("dve", lambda e: e.memset(epsc[:, 1:2], LN_EPS), writes=["epsc"])
    P.add("sp", lambda e: e.dma_start(out=jv, in_=jv_d), writes=["jv"], dma_key="s_jv")
    P.add("dve", lambda e: e.tensor_scalar(out=jv128, in0=jv, scalar1=128.0, scalar2=None, op0=ALU.mult), reads=["jv"], writes=["jv128"])
    for r in range(4):
        P.add("dve", lambda e, r=r: e.tensor_scalar(out=jsh[:, r:r + 1], in0=jv, scalar1=128.0, scalar2=-128.0 * r, op0=ALU.mult, op1=ALU.add),
              reads=["jv"], writes=["jsh"])
    for r in range(4):
        P.add("dve", lambda e, r=r: e.tensor_scalar(out=masks[:, r, :], in0=valf, scalar1=jsh[:, r:r + 1], scalar2=0.0, op0=ALU.add, op1=ALU.is_ge),
              reads=["valf", "jsh"], writes=["masks"])
    P.add("pool", lambda e: e.iota(out=scr_i2, pattern=[[-128, NTAB]], base=320, channel_multiplier=1), writes=["scr_i2"])
    P.add("dve", lambda e: e.tensor_copy(out=D1, in_=scr_i2), reads=["scr_i2"], writes=["D1"])
    P.add("dve", lambda e: e.tensor_scalar(out=D1, in0=D1, scalar1=jv128, scalar2=None, op0=ALU.subtract), reads=["D1", "jv128"], writes=["D1"])
    P.add("pool", lambda e: e.iota(out=scr_i2, pattern=[[1, NTAB]], base=-3, channel_multiplier=0), reads=["D1"], writes=["scr_i2"])
    P.add("dve", lambda e: e.tensor_copy(out=DL, in_=scr_i2), reads=["scr_i2"], writes=["DL"])
    P.add("dve", lambda e: e.tensor_scalar(out=DL, in0=DL, scalar1=jv, scalar2=0.0, op0=ALU.add, op1=ALU.is_ge), reads=["DL", "jv"], writes=["DL"])
    P.add("dve", lambda e: e.tensor_tensor(out=t1, in0=D1, in1=DL, op=ALU.mult), reads=["D1", "DL"], writes=["t1"])
    P.add("dve", lambda e: e.tensor_scalar(out=t2, in0=DL, scalar1=1.0, scalar2=NEG_BIG, op0=ALU.subtract, op1=ALU.mult), reads=["DL"], writes=["t2"])
    for h in range(H):
        P.add("dve", lambda e, h=h: e.scalar_tensor_tensor(out=tab[:, h, :], in0=t1, scalar=slopes[h], in1=t2, op0=ALU.mult, op1=ALU.add),
              reads=["t1", "t2"], writes=["tab"])
    DW = carve(10240, NTW * 4, F32)
    scr_i3 = carve(12288, NTW * 4, F32).bitcast(I32)
    P.add("pool", lambda e: e.iota(out=scr_i3, pattern=[[-128, NTW]], base=-320 + 128 * 15, channel_multiplier=1), writes=["scr_i3"])
    P.add("dve", lambda e: e.tensor_copy(out=DW, in_=scr_i3), reads=["scr_i3"], writes=["DW"])
    P.add("dve", lambda e: e.tensor_scalar(out=DW, in0=DW, scalar1=jv128, scalar2=None, op0=ALU.subtract), reads=["DW", "jv128"], writes=["DW"])
    for h, hw in WIDX.items():
        P.add("dve", lambda e, h=h, hw=hw: e.tensor_scalar(out=tabw[:, hw, :], in0=DW, scalar1=slopes[h], scalar2=None, op0=ALU.mult), reads=["DW"], writes=["tabw"])
    lb = carve(16384, 4 * 128 * 4, F32).rearrange("p (a n) -> p a n", a=4)
    junk = carve(20480, 128 * 4, F32)
    for i, v in enumerate((lq1, lk1, lq2, lk2)):
        P.add("sp", lambda e, i=i, v=v: e.dma_start(out=lb[:, i, :], in_=v.broadcast_to([128, 128])), writes=[("lb", i)], dma_key=("s_lb", i))
    for i in range(2):
        P.add("dve", lambda e, i=i: e.tensor_tensor(out=junk, in0=lb[:, 2 * i, :], in1=lb[:, 2 * i + 1, :], op=ALU.mult),
              reads=[("lb", 2 * i), ("lb", 2 * i + 1)], writes=["junk"])
        P.add("dve", lambda e, i=i: e.reduce_sum(out=lamtmp[:, i:i + 1], in_=junk, axis=AX.X), reads=["junk"], writes=[("lt", i)])
        P.add("act", lambda e, i=i: e.activation(out=lamtmp[:, 2 + i:3 + i], in_=lamtmp[:, i:i + 1], func=AF.Exp), reads=[("lt", i)], writes=[("le", i)])
    P.add("dve", lambda e: e.tensor_tensor(out=lamtmp[:, 4:5], in0=lamtmp[:, 3:4], in1=lamtmp[:, 2:3], op=ALU.subtract), reads=[("le", 0), ("le", 1)], writes=["lt4"])
    P.add("dve", lambda e: e.tensor_scalar(out=neglam, in0=lamtmp[:, 4:5], scalar1=-lam_init, scalar2=None, op0=ALU.add), reads=["lt4"], writes=["neglam"])
    P.add("sp", lambda e: e.dma_start(out=subg, in_=subln_g.broadcast_to([128, 256])), writes=["subg"], dma_key="s_subg")
    P.add("dve", lambda e: e.tensor_scalar(out=subg, in0=subg, scalar1=1.0 - lam_init, scalar2=None, op0=ALU.mult), reads=["subg"], writes=["subg"])
    vrows = carve(24576, 128 * 4, F32)
    for a, v in enumerate((ln_v_g, ln_v_b, b_gate[0:1, :], b_gate[1:2, :])):
        P.add("sp", lambda e, v=v: e.dma_start(out=vrows[0:KC, :], in_=v.rearrange("o (c p) -> (o c) p", p=128)), writes=["vrows"], dma_key="s_vr")
        P.add("pe", lambda e: e.transpose(out=psf[4][:, 0:KC], in_=vrows[0:KC, :], identity=ident_f[0:KC, 0:KC]),
              reads=["vrows", "ident_f"], writes=[("PS", 4)])
        P.add("dve", lambda e, a=a: e.tensor_copy(out=cols[:, a, :], in_=psf[4][:, 0:KC]), reads=[("PS", 4)], writes=["cols"])
    wsl = carve(28672, 128 * 4, F32); wtf = carve(32768, 128 * 4, F32)
    bsbc = carve(36864, G * 128 * 4, F32).rearrange("p (g t) -> p g t", g=G)
    P.add("sp", lambda e: e.dma_start(out=bsbc.rearrange("p g t -> p (g t)"), in_=b_s.broadcast_to([128, G * 128])), writes=["bsbc"], dma_key="s_bs")
    for g in range(G):
        P.add("sp", lambda e, g=g: e.dma_start(out=wsl, in_=w_s[g]), writes=["wsl"], dma_key="s_ws")
        P.add("pe", lambda e: e.transpose(out=psf[4][:, 0:128], in_=wsl, identity=ident_f), reads=["wsl", "ident_f"], writes=[("PS", 4)])
        P.add("dve", lambda e: e.tensor_tensor(out=wtf, in0=psf[4][:, 0:128], in1=tri_f, op=ALU.mult), reads=[("PS", 4), "tri_f"], writes=["wtf"])
        P.add("dve", lambda e, g=g: e.tensor_copy(out=wT[:, g, :], in_=wtf), reads=["wtf"], writes=["wT"])
        P.add("pe", lambda e: e.matmul(psf[5][:, 0:128], ones_f, wtf, start=True, stop=True), reads=["wtf", "ones_f"], writes=[("PS", 5)])
        for cc in range(2):
            c = 2 * g + cc
            P.add("dve", lambda e, c=c, g=g: e.scalar_tensor_tensor(out=B2[:, c, :], in0=psf[5][:, 0:128], scalar=cols[:, 1, c:c + 1], in1=bsbc[:, g, :],
                                                                     op0=ALU.mult, op1=ALU.add), reads=[("PS", 5), "cols", "bsbc"], writes=["B2"])
    P.barrier()

    TGK = 512
    NGK = S // TGK
    ATs = [carve(i * KC * TGK * 2, KC * TGK * 2, BF16).rearrange("p (c t) -> p c t", c=KC) for i in range(2)]
    off_n = 2 * KC * TGK * 2
    xb_k = carve(off_n, 4 * D, F32); gbc_k = carve(off_n + 4 * D, 4 * D, F32)
    xn_k = [carve(off_n + 8 * D + i * 2 * D, 2 * D, BF16) for i in range(2)]
    ss_k = carve(off_n + 12 * D, 16, F32)
    soff = off_n + 12 * D + 64
    kst = [(carve(soff + i * TGK * 2, TGK * 2, BF16), ("kst", i)) for i in range(2)]
    vst = [(carve(soff + 2 * TGK * 2 + i * 1024, 1024, BF16), ("vst", i)) for i in range(2)]
    acc_off = soff + 2 * TGK * 2 + 2048
    accs_k = [(carve(acc_off + i * 2048, 2048, F32), ("acc", i)) for i in range(TGK // 128)]
    load_bcast(gbc_k, g_mix, "gbc")
    kring = Ring(kst); vring = Ring(vst)

    nbi = [0]
    backs = []

    def kv_norm(kg, tb, defer=False):
        i = nbi[0] % 2
        nbi[0] += 1
        return norm_block(x_all, kg * TGK + tb * 128, gbc_k, xb_k, ("xb", 0), xn_k[i], ss_k, tb, ATs[kg % 2], key_at=("ATK", kg % 2),
                          xnkey=("xn", i), defer=defer)

    def kv_step(pending):
        if backs:
            backs.pop(0)()
        if pending:
            backs.append(kv_norm(*pending.pop(0), defer=True))

    for tb in range(TGK // 128):
        kv_norm(0, tb)
    for kg in range(NGK):
        pending = [(kg + 1, tb) for tb in range(TGK // 128)] if kg + 1 < NGK else []

        def epi_k(ch, ts, ps, pkey, kg=kg, state={}):
            if ts == 0:
                state["cur"] = kring.next()
            stg, skey = state["cur"]
            copy_any(stg[:, ts * 512:(ts + 1) * 512], ps[:], [pkey], [(skey, ts)])
            if ts == TGK // 512 - 1:
                P.add("sp", lambda e: e.dma_start(out=KT[ch * 128:(ch + 1) * 128, kg * TGK:(kg + 1) * TGK], in_=stg),
                      reads=[(skey, t_) for t_ in range(TGK // 512)], dma_key=("st", skey))

        def hook(ng, pending=pending):
            if ng % 3 == 1:
                kv_step(pending)

        gemm_fm(w_in, 3 * D, D, ATs[kg % 2], TGK, epi_k, key_at=("ATK", kg % 2), hook=hook)
        while pending or backs:
            kv_step(pending)

        def epi_v(pi, npc, tb, nt, ps, pkey, kg=kg):
            acc, akey = accs_k[tb]
            if pi == 0 and npc > 1:
                copy_any(acc, ps[:], [pkey], [akey])
                return
            stg, skey = vring.next()
            if npc > 1:
                P.add("dve", lambda e: e.tensor_tensor(out=stg, in0=ps[:], in1=acc, op=ALU.add), reads=[pkey, akey], writes=[skey])
            else:
                copy_any(stg, ps[:], [pkey], [skey])
            r0 = kg * TGK + tb * 128
            P.add("sp", lambda e: e.dma_start(out=VV[r0:r0 + 128, nt * 512:(nt + 1) * 512], in_=stg), reads=[skey], dma_key=("st", skey))

        gemm_tm(w_in, KC, 4 * D, D, ATs[kg % 2], list(range(TGK // 128)), epi_v, key_at=("ATK", kg % 2))
    P.barrier()

    NTB = TG // 128
    def do_group(tg):
        tok0 = tg * TG
        AT = carve(0, KC * TG * 2, BF16).rearrange("p (c t) -> p c t", c=KC)
        off_n = KC * TG * 2
        norm_phase(x_own, tok0, NTB, g_mix, AT, off_n)
        P.barrier()
        gvs = [carve(off_n + i * 4 * D, 4 * D, F32) for i in range(2)]
        nbf = carve(off_n + 8 * D, 2 * D, BF16)
        mts = carve(off_n + 10 * D, 2 * D, BF16).rearrange("p (c t) -> p c t", c=KC)
        soff = off_n + 12 * D
        stat = carve(soff, 256, F32)
        bnst = carve(soff + 256, (D // 512) * 6 * 4, F32).rearrange("p (n s) -> p n s", s=6)
        soff2 = soff + 256 + (D // 512) * 24
        soff2 = (soff2 + 63) // 64 * 64
        accs = [(carve(soff2 + i * 2048, 2048, F32), ("acc", i)) for i in range(2)]
        soff3 = off_n
        for sg in range(NTB // 2):
            def epi_gv(pi, npc, tb, nt, ps, pkey, sg=sg):
                li = tb - 2 * sg
                acc, akey = accs[li]
                if pi == 0 and npc > 1:
                    copy_any(acc, ps[:], [pkey], [akey])
                    return
                src = ps[:]
                rd = [pkey]
                if npc > 1:
                    P.add("dve", lambda e: e.tensor_tensor(out=acc, in0=ps[:], in1=acc, op=ALU.add), reads=[pkey, akey], writes=[akey])
                    src = acc
                    rd = [akey]
                P.add("act", lambda e: e.activation(out=gvs[li][:, nt * 512:(nt + 1) * 512], in_=src, func=AF.Gelu), reads=rd, writes=[("gv", li, nt)])

            gemm_tm(w_in, KC, D, D, AT, [2 * sg, 2 * sg + 1], epi_gv)
            for li in range(2):
                tb = 2 * sg + li
                gv = gvs[li]
                for n_ in range(D // 512):
                    P.add("dve", lambda e, n_=n_, gv=gv: e.bn_stats(out=bnst[:, n_, :], in_=gv[:, n_ * 512:(n_ + 1) * 512]),
                          reads=[("gv", li, n_)], writes=["bnst"])
                P.add("dve", lambda e: e.bn_aggr(out=stat[:, 0:2], in_=bnst.rearrange("p n s -> p (n s)")), reads=["bnst"], writes=["st01"])
                P.add("act", lambda e: e.activation(out=stat[:, 2:3], in_=stat[:, 1:2], func=AF.Sqrt, bias=epsc[:, 1:2], scale=1.0), reads=["st01"], writes=["st2"])
                P.add("dve", lambda e: e.reciprocal(out=stat[:, 3:4], in_=stat[:, 2:3]), reads=["st2"], writes=["st3"])
                P.add("dve", lambda e, gv=gv: e.tensor_scalar(out=nbf, in0=gv, scalar1=stat[:, 0:1], scalar2=stat[:, 3:4], op0=ALU.subtract, op1=ALU.mult),
                      reads=[("gv", li, n_) for n_ in range(D // 512)] + ["st01", "st3"], writes=["nbf"])
                for c0 in range(0, KC, 4):
                    ps, pkey = gb.next()
                    for c in range(4):
                        cc = c0 + c
                        P.add("pe", lambda e, ps=ps, c=c, cc=cc: e.matmul(ps[:, c * 128:(c + 1) * 128], nbf[:, cc * 128:(cc + 1) * 128], wT[:, cc // 2, :], start=True, stop=True),
                              reads=["nbf", "wT"], writes=[pkey])
                    for c in range(4):
                        cc = c0 + c
                        P.add("dve", lambda e, ps=ps, c=c, cc=cc: e.scalar_tensor_tensor(out=mts[:, cc, :], in0=ps[:, c * 128:(c + 1) * 128], scalar=cols[:, 0, cc:cc + 1], in1=B2[:, cc, :],
                                                                                      op0=ALU.mult, op1=ALU.add), reads=[pkey, "cols", "B2"], writes=["mts"])
                c0t = tok0 + tb * 128
                P.add("sp", lambda e, c0t=c0t: e.dma_start(out=MT[:, c0t:c0t + 128].rearrange("(c p) t -> p c t", p=128), in_=mts), reads=["mts"], dma_key=("st", "mts"))
        P.barrier()
        fst = [(carve(soff3 + i * TG * 2, TG * 2, BF16), ("fst", i)) for i in range(3)]
        mtl = [(carve(soff3 + 3 * TG * 2 + i * 1024, 1024, BF16), ("mtl", i)) for i in range(3)]
        gtmp = [(carve(soff3 + 3 * TG * 2 + 3072 + i * 2048, 2048, F32), ("gtmp", i)) for i in range(2)]
        fring = Ring(fst); mring = Ring(mtl); tring = Ring(gtmp)

        def fm_store(dst):
            state = {}

            def put(ch, ts, producer):
                if ts == 0:
                    state["cur"] = fring.next()
                stg, skey = state["cur"]
                producer(stg[:, ts * 512:(ts + 1) * 512], (skey, ts))
                if ts == TG // 512 - 1:
                    P.add("sp", lambda e: e.dma_start(out=dst[ch * 128:(ch + 1) * 128, tok0:tok0 + TG], in_=stg),
                          reads=[(skey, t_) for t_ in range(TG // 512)], dma_key=("st", skey))
            return put

        put_u = fm_store(YAT)

        def epi_u(ch, ts, ps, pkey):
            ml, mkey = mring.next()
            tt, tkey = tring.next()
            c0t = tok0 + ts * 512
            P.add("sp", lambda e: e.dma_start(out=ml, in_=MT[ch * 128:(ch + 1) * 128, c0t:c0t + 512]), writes=[mkey], dma_key=("ld", mkey))
            P.add("act", lambda e: e.activation(out=tt, in_=ps[:], func=AF.Gelu), reads=[pkey], writes=[tkey])
            put_u(ch, ts, lambda o, skey: P.add("dve", lambda e: e.tensor_tensor(out=o, in0=tt, in1=ml, op=ALU.mult), reads=[tkey, mkey], writes=[skey]))

        gemm_fm(w_in, 0, D, AT, TG, epi_u)
        put_q = fm_store(QT)

        def epi_q(ch, ts, ps, pkey):
            put_q(ch, ts, lambda o, skey: copy_any(o, ps[:], [pkey], [skey]))

        gemm_fm(w_in, 2 * D, D, AT, TG, epi_q)
        for a, dst in ((0, GA), (1, GB)):
            put_g = fm_store(dst)

            def epi_g(ch, ts, ps, pkey, a=a, put_g=put_g):
                put_g(ch, ts, lambda o, skey: P.add("act", lambda e: e.activation(out=o, in_=ps[:], func=AF.Sigmoid, bias=cols[:, 2 + a, ch:ch + 1], scale=1.0),
                                                    reads=[pkey, "cols"], writes=[skey]))

            gemm_fm(w_in, (5 + a) * D, D, AT, TG, epi_g)
        P.barrier()

        o = [0]

        def al(nbytes, dt):
            a = carve(o[0], nbytes, dt)
            o[0] += (nbytes + 63) // 64 * 64
            return a

        qbuf = [(al(2 * TG * 2, BF16).rearrange("p (m t) -> p m t", m=2), ("qb", i)) for i in range(2)]
        kbuf = [(al(2 * CH * 128 * 2, BF16).rearrange("p (m t) -> p m t", m=2), ("kb", i)) for i in range(3)]
        vbuf = [(al(CH * 258 * 2, BF16).rearrange("p (c e) -> p c e", e=258), ("vb", i)) for i in range(3)]
        pbuf = [(al(1024, BF16).rearrange("p (m q) -> p m q", m=2), ("pt", i)) for i in range(4)]
        o1n = al(2 * 256 * 4, F32).rearrange("p (i e) -> p i e", i=2)
        ofin = [(al(1024, F32), ("of", i)) for i in range(2)]
        ybf = [(al(512, BF16), ("yb", i)) for i in range(2)]
        ybT = [(al(2 * 256 * 2, BF16).rearrange("p (a t) -> p a t", a=2), ("ybT", i)) for i in range(2)]
        sm = al(64 * 4, F32)
        sq = al(1024, F32)
        for vb, vkey in vbuf:
            P.add("dve", lambda e, vb=vb: e.memset(vb[:, :, 256:257], 1.0), writes=[vkey])
        qring = Ring(qbuf); kring = Ring(kbuf); vring = Ring(vbuf); pring = Ring(pbuf)
        ofr = Ring(ofin); ybr = Ring(ybf); ybTr = Ring(ybT)
        sring = Ring([(psf[4], ("PS", 4)), (psf[5], ("PS", 5)), (psb[1][:].bitcast(F32), ("PB", 1))])
        tpa = Ring([(psb[0], ("PB", 0))])
        smi = [0]
        rows = []
        for h in range(H):
            for qt in range(TG // 256):
                m0 = tg * NTB + 2 * qt
                nkb = 4 * m0 + 8
                for kb in range(nkb):
                    i0 = max(0, (kb - 4 * m0) // 4)
                    i1 = 2
                    if DSKIP[h] is not None:
                        i1 = min(2, max(0, (DSKIP[h] + kb - 4 * m0 + 3) // 4))
                    if i1 <= i0:
                        continue
                    rows.append(dict(h=h, qt=qt, kb=kb, m0=m0, nkb=nkb, i0=i0, i1=i1))
        cur = {}

        def emit_S(row):
            h, qt, kb, m0, nkb = row["h"], row["qt"], row["kb"], row["m0"], row["nkb"]
            if cur.get("qh") != h:
                cur["qh"] = h
                qb, qkey = qring.next()
                P.add("sp", lambda e: e.dma_start(out=qb, in_=QT[h * 256:(h + 1) * 256, tok0:tok0 + TG].rearrange("(m p) t -> p m t", p=128)),
                      writes=[qkey], dma_key=("ld", qkey))
                cur["q"] = (qb, qkey)
            if cur.get("unit") != (h, qt):
                cur["unit"] = (h, qt)
                cur["yT"] = ybTr.next()
            if cur.get("chunk") != (h, qt, kb // CH):
                cur["chunk"] = (h, qt, kb // CH)
                kc0 = (kb // CH) * CH
                l0 = kb - kc0
                nk = min(CH, nkb - kc0)
                kb_, kkey = kring.next()
                P.add("sp", lambda e: e.dma_start(
                    out=kb_[:, :, l0 * 128:nk * 128], in_=KT[h * 256:(h + 1) * 256, kb * 128:(kc0 + nk) * 128].rearrange("(m p) t -> p m t", p=128)),
                    writes=[kkey], dma_key=("ld", kkey))
                vb, vkey = vring.next()
                P.add("sp", lambda e: e.dma_start(
                    out=vb[:, l0:nk, 0:256], in_=VV[kb * 128:(kc0 + nk) * 128, h * 256:(h + 1) * 256].rearrange("(c p) e -> p c e", p=128)),
                    writes=[vkey], dma_key=("ld", vkey))
                cur["k"] = (kb_, kkey)
                cur["v"] = (vb, vkey)
            row["q"] = cur["q"]; row["k"] = cur["k"]; row["v"] = cur["v"]; row["yT"] = cur["yT"]
            qb, qkey = row["q"]
            kb_, kkey = row["k"]
            kl = kb % CH
            i0, i1 = row["i0"], row["i1"]
            sps, skey = sring.next()
            row["s"] = (sps, skey)
            for mp in range(2):
                P.add("pe", lambda e, mp=mp: e.matmul(
                    sps[:, mp * 256 + i0 * 128:mp * 256 + i1 * 128], kb_[:, mp, kl * 128:(kl + 1) * 128],
                    qb[:, mp, qt * 256 + i0 * 128:qt * 256 + i1 * 128], start=True, stop=True),
                    reads=[kkey, qkey], writes=[(skey, mp)])

        def emit_rest(row):
            h, qt, kb, m0, nkb = row["h"], row["qt"], row["kb"], row["m0"], row["nkb"]
            sps, skey = row["s"]
            s3 = sps[:, 0:512].rearrange("p (m q) -> p m q", m=2)
            vb, vkey = row["v"]
            i0, i1 = row["i0"], row["i1"]
            kl = kb % CH
            pt, pkey_ = pring.next()
            if WIDE[h]:
                cw = 4 * m0 - kb + 15
                P.add("act", lambda e: e.activation(
                    out=pt[:, :, i0 * 128:i1 * 128], in_=s3[:, :, i0 * 128:i1 * 128], func=AF.Exp, bias=tabw[:, WIDX[h], cw:cw + 1], scale=SCALE),
                    reads=[(skey, 0), (skey, 1)], writes=[(pkey_, i) for i in range(i0, i1)])
            for i in range(i0, i1):
                dl = 4 * (m0 + i) - kb
                if not WIDE[h]:
                    P.add("act", lambda e, i=i, dl=dl: e.activation(
                        out=pt[:, :, i * 128:(i + 1) * 128], in_=s3[:, :, i * 128:(i + 1) * 128], func=AF.Exp, bias=tab[:, h, dl + 3:dl + 4], scale=SCALE),
                        reads=[(skey, 0), (skey, 1)], writes=[(pkey_, i)])
                r = -dl
                if 0 <= r <= 3:
                    for mp in range(2):
                        P.add("dve", lambda e, i=i, r=r, mp=mp: e.tensor_tensor(out=pt[:, mp, i * 128:(i + 1) * 128], in0=pt[:, mp, i * 128:(i + 1) * 128], in1=masks[:, r, :], op=ALU.mult),
                              reads=[(pkey_, i)], writes=[(pkey_, i)])
                kfirst = 0 if DSKIP[h] is None else max(0, 4 * (m0 + i) - DSKIP[h] + 1)
                for mp in range(2):
                    P.add("pe", lambda e, i=i, mp=mp, first=(kb == kfirst), last=(kb == 4 * (m0 + i) + 3): e.matmul(
                        psf[mp * 2 + i][:, 0:257], pt[:, mp, i * 128:(i + 1) * 128], vb[:, kl, 0:257], start=first, stop=last),
                        reads=[(pkey_, i), vkey], writes=[("PS", mp * 2 + i)])
            if kb != nkb - 1:
                return
            yT, yTkey = row["yT"]
            for i in range(2):
                s0 = (smi[0] % 8) * 8
                smi[0] += 1
                P.add("dve", lambda e, i=i, s0=s0: e.reciprocal(out=sm[:, s0:s0 + 1], in_=psf[i][:, 256:257]), reads=[("PS", i)], writes=[("sm", s0)])
                P.add("dve", lambda e, i=i, s0=s0: e.reciprocal(out=sm[:, s0 + 1:s0 + 2], in_=psf[2 + i][:, 256:257]), reads=[("PS", 2 + i)], writes=[("sm", s0 + 1)])
                P.add("dve", lambda e, i=i, s0=s0: e.tensor_scalar(out=o1n[:, i, :], in0=psf[i][:, 0:256], scalar1=sm[:, s0:s0 + 1], scalar2=None, op0=ALU.mult),
                      reads=[("PS", i), ("sm", s0)], writes=[("o1n", i)])
                of, okey = ofr.next()
                yb, ykey = ybr.next()
                P.add("dve", lambda e, s0=s0: e.tensor_scalar(out=sm[:, s0 + 5:s0 + 6], in0=sm[:, s0 + 1:s0 + 2], scalar1=neglam, scalar2=None, op0=ALU.mult),
                      reads=[("sm", s0 + 1), "neglam"], writes=[("sm", s0 + 5)])
                P.add("dve", lambda e, i=i, of=of, s0=s0: e.scalar_tensor_tensor(out=of, in0=psf[2 + i][:, 0:256], scalar=sm[:, s0 + 5:s0 + 6], in1=o1n[:, i, :],
                                                                              op0=ALU.mult, op1=ALU.add), reads=[("PS", 2 + i), ("sm", s0 + 5), ("o1n", i)], writes=[okey])
                P.add("act", lambda e, of=of, s0=s0: e.activation(out=sq, in_=of, func=AF.Square, accum_out=sm[:, s0 + 2:s0 + 3]), reads=[okey], writes=["sq", ("sm", s0 + 2)])
                P.add("act", lambda e, s0=s0: e.activation(out=sm[:, s0 + 3:s0 + 4], in_=sm[:, s0 + 2:s0 + 3], func=AF.Sqrt, bias=epsc[:, 1:2], scale=1.0 / 256),
                      reads=[("sm", s0 + 2)], writes=[("sm", s0 + 3)])
                P.add("dve", lambda e, s0=s0: e.reciprocal(out=sm[:, s0 + 4:s0 + 5], in_=sm[:, s0 + 3:s0 + 4]), reads=[("sm", s0 + 3)], writes=[("sm", s0 + 4)])
                P.add("dve", lambda e, of=of, yb=yb, s0=s0: e.scalar_tensor_tensor(out=yb, in0=of, scalar=sm[:, s0 + 4:s0 + 5], in1=subg, op0=ALU.mult, op1=ALU.mult),
                      reads=[okey, ("sm", s0 + 4), "subg"], writes=[ykey])
                tp, tkey = tpa.next()
                for a in range(2):
                    P.add("pe", lambda e, tp=tp, a=a, yb=yb: e.transpose(out=tp[:, a * 128:(a + 1) * 128], in_=yb[:, a * 128:(a + 1) * 128], identity=ident_b),
                          reads=[ykey], writes=[tkey])
                copy_any(yT[:, :, i * 128:(i + 1) * 128], tp[:, 0:256].rearrange("p (a t) -> p a t", a=2), [tkey], [(yTkey, i)])
            c0t = tok0 + qt * 256
            P.add("sp", lambda e: e.dma_start(out=YBT[h * 256:(h + 1) * 256, c0t:c0t + 256].rearrange("(a p) t -> p a t", p=128), in_=yT),
                  reads=[(yTkey, i_) for i_ in range(2)], dma_key=("st", yTkey))

        LA = 2
        for ri in range(min(LA, len(rows))):
            emit_S(rows[ri])
        for ri, row in enumerate(rows):
            if ri + LA < len(rows):
                emit_S(rows[ri + LA])
            emit_rest(row)
        P.barrier()

        AT = carve(0, KC * TG * 2, BF16).rearrange("p (c t) -> p c t", c=KC)
        soff = KC * TG * 2
        gl = [(carve(soff + i * 1024, 1024, BF16), ("gl", i)) for i in range(3)]
        tl = [(carve(soff + 3072 + i * 2048, 2048, F32), ("tl", i)) for i in range(3)]
        tmpm = [(carve(soff + 3072 + 6144 + i * 2048, 2048, F32), ("tmpm", i)) for i in range(2)]
        fst = [(carve(soff + 3072 + 6144 + 4096 + i * TG * 2, TG * 2, BF16), ("fst", i)) for i in range(3)]
        glr = Ring(gl); tlr = Ring(tl); tmr = Ring(tmpm); fring = Ring(fst)
        load_AT(AT, YAT, KC, tok0, TG)

        def epi_a(ch, ts, ps, pkey):
            g_, gkey = glr.next()
            t_, tkey = tlr.next()
            c0t = tok0 + ts * 512
            P.add("sp", lambda e: e.dma_start(out=g_, in_=GA[ch * 128:(ch + 1) * 128, c0t:c0t + 512]), writes=[gkey], dma_key=("ld", gkey))
            P.add("dve", lambda e: e.tensor_tensor(out=t_, in0=ps[:], in1=g_, op=ALU.mult), reads=[pkey, gkey], writes=[tkey])
            P.add("sp", lambda e: e.dma_start(out=T1[ch * 128:(ch + 1) * 128, c0t:c0t + 512], in_=t_), reads=[tkey], dma_key=("st", tkey))

        gemm_fm(w_br_a, 0, D, AT, TG, epi_a)
        P.barrier()
        load_AT(AT, YBT, KC, tok0, TG)
        put_m = fm_store(MGT)

        def epi_b(ch, ts, ps, pkey):
            g_, gkey = glr.next()
            t_, tkey = tlr.next()
            m_, mkey = tmr.next()
            c0t = tok0 + ts * 512
            P.add("sp", lambda e: e.dma_start(out=g_, in_=GB[ch * 128:(ch + 1) * 128, c0t:c0t + 512]), writes=[gkey], dma_key=("ld", gkey))
            P.add("sp", lambda e: e.dma_start(out=t_, in_=T1[ch * 128:(ch + 1) * 128, c0t:c0t + 512]), writes=[tkey], dma_key=("ld", tkey))
            P.add("dve", lambda e: e.tensor_tensor(out=m_, in0=ps[:], in1=g_, op=ALU.mult), reads=[pkey, gkey], writes=[mkey])
            put_m(ch, ts, lambda o_, skey: P.add("dve", lambda e: e.tensor_tensor(out=o_, in0=m_, in1=t_, op=ALU.add), reads=[mkey, tkey], writes=[skey]))

        gemm_fm(w_br_b, 0, D, AT, TG, epi_b)
        P.barrier()
        load_AT(AT, MGT, KC, tok0, TG)

        def tm_residual(res_src, dst, accs, xl, ost, post=None):
            xlr = Ring(xl); osr = Ring(ost)

            def epi(pi, npc, tb, nt, ps, pkey):
                acc, akey = accs[tb % len(accs)]
                r0 = tok0_cur[0] + tb * 128
                if pi == 0:
                    x_, xkey = xlr.next()
                    P.add("sp", lambda e: e.dma_start(out=x_, in_=res_src[r0:r0 + 128, nt * 512:(nt + 1) * 512]), writes=[xkey], dma_key=("ld", xkey))
                    tgt, tk = (acc, akey) if npc > 1 else osr.next()
                    P.add("dve", lambda e: e.tensor_tensor(out=tgt, in0=ps[:], in1=x_, op=ALU.add), reads=[pkey, xkey], writes=[tk])
                    if npc > 1:
                        return
                elif pi < npc - 1:
                    P.add("dve", lambda e: e.tensor_tensor(out=acc, in0=ps[:], in1=acc, op=ALU.add), reads=[pkey, akey], writes=[akey])
                    return
                else:
                    tgt, tk = osr.next()
                    P.add("dve", lambda e: e.tensor_tensor(out=tgt, in0=ps[:], in1=acc, op=ALU.add), reads=[pkey, akey], writes=[tk])
                P.add("sp", lambda e: e.dma_start(out=dst[r0:r0 + 128, nt * 512:(nt + 1) * 512], in_=tgt), reads=[tk], dma_key=("st", tk))
            return epi

        tok0_cur = [tok0]
        a0 = soff
        accs = [(carve(a0 + i * 2048, 2048, F32), ("acc", i)) for i in range(NTB)]
        xl = [(carve(a0 + NTB * 2048 + i * 2048, 2048, F32), ("xl", i)) for i in range(3)]
        ost = [(carve(a0 + NTB * 2048 + 6144 + i * 2048, 2048, F32), ("ost", i)) for i in range(3)]
        gemm_tm(w_o, KC, 0, D, AT, list(range(NTB)), tm_residual(x_own, X1, accs, xl, ost))
        P.barrier()

        norm_phase(X1, tok0, NTB, g_ffn, AT, off_n)
        P.barrier()
        fst = [(carve(off_n + i * TG * 2, TG * 2, BF16), ("fst", i)) for i in range(3)]
        stmp = [(carve(off_n + 3 * TG * 2 + i * 2048, 2048, F32), ("stmp", i)) for i in range(3)]
        fring = Ring(fst); sring2 = Ring(stmp)
        put_f = fm_store(FFA)
        for pg in range(DFF // 256):
            wg, wgk = wload(wview_fm(w_gu, pg * 256, 256), KC, 256)
            wu, wuk = wload(wview_fm(w_gu, DFF + pg * 256, 256), KC, 256)
            for c2 in range(2):
                for ts in range(TG // 512):
                    pg_, pgk = gb.next()
                    pu_, puk = gb.next()
                    for (ps, pk, wv, wk) in ((pg_, pgk, wg, wgk), (pu_, puk, wu, wuk)):
                        for k in range(KC):
                            rk = [("AT", ts * 4 + i, k // 8) for i in range(4)]
                            P.add("pe", lambda e, ps=ps, wv=wv, k=k, c2=c2, ts=ts: e.matmul(
                                ps[:], wv[:, k, c2 * 128:(c2 + 1) * 128], AT[:, k, ts * 512:(ts + 1) * 512], start=(k == 0), stop=(k == KC - 1)),
                                reads=[wk] + rk, writes=[pk])
                    s_, sk = sring2.next()
                    P.add("act", lambda e, s_=s_, pg_=pg_: e.activation(out=s_, in_=pg_[:], func=AF.Silu), reads=[pgk], writes=[sk])
                    put_f(pg * 2 + c2, ts, lambda o_, skey, s_=s_, sk=sk, pu_=pu_, puk=puk: P.add(
                        "dve", lambda e: e.tensor_tensor(out=o_, in0=pu_[:], in1=s_, op=ALU.mult), reads=[puk, sk], writes=[skey]))
        P.barrier()

        ATd = carve(0, FC * TGD * 2, BF16).rearrange("p (c t) -> p c t", c=FC)
        a0 = FC * TGD * 2
        nd = TGD // 128
        accs = [(carve(a0 + i * 2048, 2048, F32), ("acc", i)) for i in range(nd)]
        xl = [(carve(a0 + nd * 2048 + i * 2048, 2048, F32), ("xl", i)) for i in range(3)]
        ost = [(carve(a0 + nd * 2048 + 6144 + i * 2048, 2048, F32), ("ost", i)) for i in range(3)]
        for sd in range(TG // TGD):
            tok0_cur[0] = tok0 + sd * TGD
            load_AT(ATd, FFA, FC, tok0_cur[0], TGD, nparts=6)
            gemm_tm(w_down, FC, 0, D, ATd, list(range(nd)), tm_residual(X1, X2, accs, xl, ost))
            P.barrier()
        tok0_cur[0] = tok0

        norm_phase(X2, tok0, NTB, g_ple, AT, off_n)
        pT = carve(off_n, PC * TG * 2, BF16).rearrange("p (c t) -> p c t", c=PC)
        pl = carve(off_n + PC * TG * 2, PLE * 4, F32)
        plb = carve(off_n + PC * TG * 2 + PLE * 4, PLE * 2, BF16)
        P.barrier()
        for tb in range(NTB):
            r0 = tok0 + tb * 128
            P.add("sp", lambda e, r0=r0: e.dma_start(out=pl, in_=p_own[r0:r0 + 128, :]), writes=["pl"], dma_key=("ld", "pl"))
            P.add("dve", lambda e: e.tensor_copy(out=plb, in_=pl), reads=["pl"], writes=["plb"])
            tp, tkey = tpr.next()
            for c in range(PC):
                P.add("pe", lambda e, tp=tp, c=c: e.transpose(out=tp[:, c * 128:(c + 1) * 128], in_=plb[:, c * 128:(c + 1) * 128], identity=ident_b),
                      reads=["plb"], writes=[tkey])
            copy_any(pT[:, :, tb * 128:(tb + 1) * 128], tp[:, 0:PC * 128].rearrange("p (c t) -> p c t", c=PC), [tkey], [("pT", tb)])
        a0 = off_n + PC * TG * 2 + PLE * 6
        a0 = (a0 + 63) // 64 * 64
        accs = [(carve(a0 + i * 2048, 2048, F32), ("acc", i)) for i in range(NTB)]
        xl = [(carve(a0 + NTB * 2048 + i * 2048, 2048, F32), ("xl", i)) for i in range(3)]
        ost = [(carve(a0 + NTB * 2048 + 6144 + i * 2048, 2048, F32), ("ost", i)) for i in range(3)]
        sg_ = [(carve(a0 + NTB * 2048 + 12288 + i * 2048, 2048, F32), ("sg", i)) for i in range(2)]
        xlr = Ring(xl); osr = Ring(ost); sgr = Ring(sg_)
        ppr = Ring([(psf[4], ("PS", 4)), (psf[5], ("PS", 5))])
        wpp_cur = {}

        def epi_p(pi, npc, tb, nt, ps, pkey):
            acc, akey = accs[tb]
            if pi == 0 and npc > 1:
                copy_any(acc, ps[:], [pkey], [akey])
                return
            src = ps[:]
            rd = [pkey]
            if npc > 1:
                P.add("dve", lambda e: e.tensor_tensor(out=acc, in0=ps[:], in1=acc, op=ALU.add), reads=[pkey, akey], writes=[akey])
                src = acc
                rd = [akey]
            s_, sk = sgr.next()
            P.add("act", lambda e: e.activation(out=s_, in_=src, func=AF.Sigmoid), reads=rd, writes=[sk])
            if wpp_cur.get("nt") != nt:
                wpp_cur["nt"] = nt
                wpp_cur["w"] = wload(wview_tm(w_pp, 0, PC, nt * 512, 512), PC, 512)
            wv, wkey = wpp_cur["w"]
            pp, ppk = ppr.next()
            for c in range(PC):
                P.add("pe", lambda e, pp=pp, c=c, wv=wv: e.matmul(pp[:], pT[:, c, tb * 128:(tb + 1) * 128], wv[:, c, :], start=(c == 0), stop=(c == PC - 1)),
                      reads=[wkey, ("pT", tb)], writes=[ppk])
            x_, xkey = xlr.next()
            r0 = tok0 + tb * 128
            P.add("sp", lambda e: e.dma_start(out=x_, in_=X2[r0:r0 + 128, nt * 512:(nt + 1) * 512]), writes=[xkey], dma_key=("ld", xkey))
            P.add("dve", lambda e: e.tensor_tensor(out=s_, in0=pp[:], in1=s_, op=ALU.mult), reads=[ppk, sk], writes=[sk])
            o_, ok = osr.next()
            P.add("dve", lambda e: e.tensor_tensor(out=o_, in0=s_, in1=x_, op=ALU.add), reads=[sk, xkey], writes=[ok])
            P.add("sp", lambda e: e.dma_start(out=X3[r0:r0 + 128, nt * 512:(nt + 1) * 512], in_=o_), reads=[ok], dma_key=("st", ok))

        gemm_tm(w_pg, KC, 0, D, AT, list(range(NTB)), epi_p)
        P.barrier()

        xbs = [(carve(i * 4 * D, 4 * D, F32), ("xb", i)) for i in range(2)]
        gbc = carve(8 * D, 4 * D, F32)
        xn = carve(12 * D, 2 * D, BF16)
        ss = carve(14 * D, 16, F32)
        fos = [(carve(14 * D + 64 + i * 4 * D, 4 * D, F32), ("fo", i)) for i in range(2)]
        load_bcast(gbc, g_final, "gbc")
        finals = []
        for tb in range(NTB):
            r0 = tok0 + tb * 128
            xb, xbkey = xbs[tb % 2]
            fo, fokey = fos[tb % 2]
            finals.append(norm_block(X3, r0, gbc, xb, xbkey, xn, ss, tb, None, final_out=(fo, fokey, out_d[r0:r0 + 128, :])))
        P.barrier()

    for tg_ in range(NO // TG):
        do_group(tg_)

    P.emit(final_waits=[op for op in P.dma_last.values() if not (isinstance(op.dma_key, tuple) and op.dma_key[0] == "W")])
    st.close()
    return nc


def alibi_slopes(n_heads):
    return [float(np.exp2(np.float32(-8.0) * np.float32(i) / np.float32(n_heads))) for i in range(1, n_heads + 1)]


def make_in_maps(cfg, inputs):
    D = cfg["D"]; S = cfg["S"]
    x = np.asarray(inputs["x"], dtype=np.float32)
    p = np.asarray(inputs["p"], dtype=np.float32)[0]
    B = x.shape[0]
    NBA = S // 128
    NOB = NBA // 4

    def w(name, shape=None):
        a = np.ascontiguousarray(np.asarray(inputs[name], dtype=np.float32))
        return a.reshape(shape) if shape is not None else a

    G = D // 256
    shared = {
        "g_mix": w("g_mix", (1, D)), "w_in": w("w_in", (D, 7 * D)), "b_gate": w("b_gate", (2, D)),
        "ln_v_g": w("ln_v_g", (1, D)), "ln_v_b": w("ln_v_b", (1, D)),
        "w_s": w("w_s", (G, 128, 128)), "b_s": w("b_s", (1, G * 128)),
        "lambda_q1": w("lambda_q1", (1, 128)), "lambda_k1": w("lambda_k1", (1, 128)),
        "lambda_q2": w("lambda_q2", (1, 128)), "lambda_k2": w("lambda_k2", (1, 128)),
        "subln_g": w("subln_g", (1, 256)),
        "w_br_a": w("w_br_a", (D, D)), "w_br_b": w("w_br_b", (D, D)), "w_o": w("w_o", (D, D)),
        "g_ffn": w("g_ffn", (1, D)), "w_gu": w("w_gu", (D, 2 * cfg["DFF"])), "w_down": w("w_down", (cfg["DFF"], D)),
        "g_ple": w("g_ple", (1, D)), "w_ple_gate": w("w_ple_gate", (D, D)), "w_ple_proj": w("w_ple_proj", (cfg["PLE"], D)),
        "g_final": w("g_final", (1, D)),
    }
    maps = []
    for c in range(4 * B):
        b, j = c // 4, c % 4
        xb = x[b].reshape(NBA, 128, D)
        pb = p[b].reshape(NBA, 128, -1)
        m = dict(shared)
        m["x_all"] = np.ascontiguousarray(x[b])
        m["x_own"] = np.ascontiguousarray(xb[j::4].reshape(NOB * 128, D))
        m["p_own"] = np.ascontiguousarray(pb[j::4].reshape(NOB * 128, -1))
        m["jv"] = np.full((128, 1), float(j), dtype=np.float32)
        maps.append(m)
    return maps


def assemble(cfg, results, B):
    D = cfg["D"]; S = cfg["S"]
    NBA = S // 128
    NOB = NBA // 4
    out = np.empty((B, NBA, 128, D), dtype=np.float32)
    for c in range(4 * B):
        b, j = c // 4, c % 4
        out[b, j::4] = np.asarray(results[c]["out"], dtype=np.float32).reshape(NOB, 128, D)
    return out.reshape(B, S, D)


FULL_CFG = {"D": 4096, "S": 8192, "DFF": 11008, "PLE": 256, "slopes": alibi_slopes(16)}


def kernel(**inputs):
    cfg = FULL_CFG
    nc = build(cfg)
    maps = make_in_maps(cfg, inputs)
    res = run_bass_kernel_spmd(nc, maps, core_ids=list(range(8)))
    return assemble(cfg, res.results, 2)
```

```python
import contextlib
import math
import numpy as np
import concourse.bass as bass
import concourse.mybir as mybir
from concourse.bass_utils import run_bass_kernel_spmd

F32 = mybir.dt.float32
BF16 = mybir.dt.bfloat16
I32 = mybir.dt.int32
AF = mybir.ActivationFunctionType
ALU = mybir.AluOpType
AX = mybir.AxisListType

RMS_EPS = 1e-6
LN_EPS = 1e-5
HEAD_DIM = 128
NEG_BIG = 30000.0


class Op:
    __slots__ = ("eng", "fn", "deps", "inc", "val", "dma_key", "dma_val")


class Prog:
    ENGS = ("pe", "act", "dve", "pool", "sp")

    def __init__(self, nc):
        self.nc = nc
        self.ops = {e: [] for e in self.ENGS}
        self.last_w = {}
        self.readers = {}
        self.dma_cnt = {}
        self.dma_last = {}
        self.last_real = {e: None for e in self.ENGS}

    def add(self, eng, fn, reads=(), writes=(), dma_key=None, extra_deps=()):
        op = Op()
        op.eng = eng; op.fn = fn
        op.inc = False; op.val = None; op.dma_key = dma_key; op.dma_val = None
        deps = []
        for r in reads:
            w = self.last_w.get(r)
            if w is not None:
                deps.append((w, True))
        for wk in writes:
            w = self.last_w.get(wk)
            if w is not None:
                deps.append((w, False))
            for rd in self.readers.get(wk, ()):
                deps.append((rd, False))
        for d in extra_deps:
            deps.append((d, True))
        if dma_key is not None:
            c = self.dma_cnt.get(dma_key, 0) + 1
            self.dma_cnt[dma_key] = c
            op.dma_val = 16 * c
            prev = self.dma_last.get(dma_key)
            if prev is not None:
                deps.append((prev, True))
            self.dma_last[dma_key] = op
        fd = []
        seen = set()
        for d, raw in deps:
            if d is op or id(d) in seen:
                continue
            if d.dma_key is None and d.eng == eng:
                if eng == "pe" or not raw:
                    continue
            seen.add(id(d))
            fd.append(d)
        op.deps = fd
        for d in fd:
            if d.dma_key is None:
                d.inc = True
        for r in reads:
            self.readers.setdefault(r, []).append(op)
        for wk in writes:
            self.last_w[wk] = op
            self.readers[wk] = []
        self.ops[eng].append(op)
        if fn is not None and dma_key is None:
            self.last_real[eng] = op
        return op

    def barrier(self, engs=("pe", "act", "dve", "sp"), keep=("W",)):
        lasts = [self.last_real[e] for e in self.ENGS if self.last_real[e] is not None and e != "pool"]
        dmas = [op for k, op in self.dma_last.items() if not (isinstance(k, tuple) and k[0] in keep)]
        for e in engs:
            self.add(e, None, extra_deps=lasts + dmas)
        for k in list(self.last_w.keys()):
            if not (isinstance(k, tuple) and k[0] in keep):
                del self.last_w[k]
        for k in list(self.readers.keys()):
            if not (isinstance(k, tuple) and k[0] in keep):
                del self.readers[k]

    def emit(self, final_waits=()):
        nc = self.nc
        for e in self.ENGS:
            c = 0
            for op in self.ops[e]:
                if op.dma_key is None and op.inc:
                    assert op.fn is not None
                    c += 1
                    op.val = c
        with contextlib.ExitStack() as st:
            esem = {e: st.enter_context(nc.semaphore("es_" + e)) for e in self.ENGS}
            dsem = {k: st.enter_context(nc.semaphore("ds_%d" % i)) for i, k in enumerate(self.dma_cnt)}
            block = st.enter_context(nc.Block())

            def run(e, engobj):
                waited = {}
                for op in self.ops[e]:
                    need = {}
                    for d in op.deps:
                        if d.dma_key is not None:
                            s, v = dsem[d.dma_key], d.dma_val
                        else:
                            s, v = esem[d.eng], d.val
                        key = id(s)
                        if waited.get(key, 0) >= v:
                            continue
                        if key not in need or need[key][1] < v:
                            need[key] = (s, v)
                    for key, (s, v) in need.items():
                        engobj.wait_ge(s, v)
                        waited[key] = v
                    if op.fn is None:
                        continue
                    ins = op.fn(engobj)
                    if op.dma_key is not None:
                        ins.then_inc(dsem[op.dma_key], 16)
                    elif op.inc:
                        ins.then_inc(esem[e], 1)
                if e == "sp":
                    for op in final_waits:
                        engobj.wait_ge(dsem[op.dma_key], op.dma_val)

            @block.tensor
            def _(t):
                run("pe", t)

            @block.scalar
            def _(t):
                run("act", t)

            @block.vector
            def _(t):
                run("dve", t)

            @block.gpsimd
            def _(t):
                run("pool", t)

            @block.sync
            def _(t):
                run("sp", t)


class Ring:
    def __init__(self, items):
        self.items = items
        self.i = 0

    def next(self):
        it = self.items[self.i % len(self.items)]
        self.i += 1
        return it


def build(cfg):
    D = cfg["D"]; S = cfg["S"]; DFF = cfg["DFF"]; PLE = cfg["PLE"]
    slopes = [float(s) for s in cfg["slopes"]]
    lam_init = 0.2
    H = D // 256; G = D // 256; KC = D // 128; FC = DFF // 128; PC = PLE // 128
    NBA = S // 128; NOB = NBA // 4; NO = NOB * 128
    TG = min(cfg.get("TG", 1024), NO)
    TGK = 512
    TGD = min(cfg.get("TGD", 512), TG)
    CH = min(16, NBA)
    NTAB = NBA + 3
    NTW = NBA + 8
    DSKIP = []
    for sl in slopes:
        d = 0
        while sl * (128 * d - 63) <= 128.0 and d <= NBA:
            d += 1
        DSKIP.append(d if (d <= NBA and cfg.get("skip", True)) else None)
    WIDE = [bool(cfg.get("wide", True)) and sl * 639.0 <= 40.5 for sl in slopes]
    WIDX = {h: i for i, h in enumerate([h for h in range(len(slopes)) if WIDE[h]])}
    NW = max(1, len(WIDX))
    SCALE = HEAD_DIM ** -0.5
    assert TG % 512 == 0 and TGK % 512 == 0 and D % 512 == 0 and DFF % 256 == 0

    nc = bass.Bass("TRN2", target_bir_lowering=False)

    def din(name, shape):
        return nc.dram_tensor(name, list(shape), F32, kind="ExternalInput").ap()

    x_all = din("x_all", [S, D]); x_own = din("x_own", [NO, D]); p_own = din("p_own", [NO, PLE])
    jv_d = din("jv", [128, 1])
    g_mix = din("g_mix", [1, D]); w_in = din("w_in", [D, 7 * D]); b_gate = din("b_gate", [2, D])
    ln_v_g = din("ln_v_g", [1, D]); ln_v_b = din("ln_v_b", [1, D])
    w_s = din("w_s", [G, 128, 128]); b_s = din("b_s", [1, G * 128])
    lq1 = din("lambda_q1", [1, 128]); lk1 = din("lambda_k1", [1, 128])
    lq2 = din("lambda_q2", [1, 128]); lk2 = din("lambda_k2", [1, 128])
    subln_g = din("subln_g", [1, 256])
    w_br_a = din("w_br_a", [D, D]); w_br_b = din("w_br_b", [D, D]); w_o = din("w_o", [D, D])
    g_ffn = din("g_ffn", [1, D]); w_gu = din("w_gu", [D, 2 * DFF]); w_down = din("w_down", [DFF, D])
    g_ple = din("g_ple", [1, D]); w_pg = din("w_ple_gate", [D, D]); w_pp = din("w_ple_proj", [PLE, D])
    g_final = din("g_final", [1, D])
    out_d = nc.dram_tensor("out", [NO, D], F32, kind="ExternalOutput").ap()

    def dscr(name, shape, dt):
        if cfg.get("debug"):
            return nc.dram_tensor(name, list(shape), dt, kind="ExternalOutput").ap()
        return nc.dram_tensor(name, list(shape), dt).ap()

    KT = dscr("KT", [D, S], BF16); VV = dscr("VV", [S, D], BF16)
    QT = dscr("QT", [D, NO], BF16); MT = dscr("MT", [D, NO], BF16)
    YAT = dscr("YAT", [D, NO], BF16); YBT = dscr("YBT", [D, NO], BF16)
    GA = dscr("GA", [D, NO], BF16); GB = dscr("GB", [D, NO], BF16)
    T1 = dscr("T1", [D, NO], F32); MGT = dscr("MGT", [D, NO], BF16)
    X1 = dscr("X1", [NO, D], F32); X2 = dscr("X2", [NO, D], F32); X3 = dscr("X3", [NO, D], F32)
    FFA = dscr("FFA", [DFF, NO], BF16)
    GV = dscr("GV", [NO, D], F32)

    P = Prog(nc)
    st = contextlib.ExitStack()
    ARENA_B = 128 * 1024
    arena = st.enter_context(nc.sbuf_tensor("arena", [128, ARENA_B // 2], BF16))
    wring_t = [st.enter_context(nc.sbuf_tensor("wr%d" % i, [128, 8192], BF16)) for i in range(3)]
    CONST_F = 512 + 15 + H * NTAB + NW * NTW + 4 * KC + KC * 128 + 256 + 2 + 3
    cst = st.enter_context(nc.sbuf_tensor("cst", [128, CONST_F], F32))
    cbf = st.enter_context(nc.sbuf_tensor("cbf", [128, 128 + 4 * 128 + G * 128], BF16))
    psf = [st.enter_context(nc.psum_tensor("psf%d" % i, [128, 512], F32)) for i in range(6)]
    psb = [st.enter_context(nc.psum_tensor("psb%d" % i, [128, 1024], BF16)) for i in range(2)]

    co = [0]

    def cf(n):
        a = cst[:, co[0]:co[0] + n]
        co[0] += n
        return a

    ident_f = cf(128); valf = cf(128); tri_f = cf(128); ones_f = cf(128)
    jv = cf(1); jv128 = cf(1); jsh = cf(4); neglam = cf(1); lamtmp = cf(8)
    tab = cf(H * NTAB).rearrange("p (h c) -> p h c", h=H)
    tabw = cf(NW * NTW).rearrange("p (h c) -> p h c", h=NW)
    cols = cf(4 * KC).rearrange("p (a c) -> p a c", a=4)
    B2 = cf(KC * 128).rearrange("p (c t) -> p c t", c=KC)
    subg = cf(256)
    epsc = cf(2)
    assert co[0] <= CONST_F
    bo = [0]

    def cb(n):
        a = cbf[:, bo[0]:bo[0] + n]
        bo[0] += n
        return a

    ident_b = cb(128)
    masks = cb(4 * 128).rearrange("p (r q) -> p r q", r=4)
    wT = cb(G * 128).rearrange("p (g t) -> p g t", g=G)

    def carve(off, nbytes, dt):
        assert off % 4 == 0 and off + nbytes <= ARENA_B, (off, nbytes)
        a = arena[:, off // 2:(off + nbytes) // 2]
        return a.bitcast(F32) if dt == F32 else a

    wr = Ring([(wring_t[i], ("W", i)) for i in range(3)])

    def wload(src_view, kc, n):
        t, key = wr.next()
        v = t[:].rearrange("p (c n) -> p c n", n=n)[:, 0:kc, :]
        P.add("pool", lambda e, v=v, s=src_view: e.dma_start(out=v, in_=s), writes=[key], dma_key=key)
        return v, key

    def wview_fm(w, c0, n):
        return w[:, c0:c0 + n].rearrange("(c p) n -> p c n", p=128)

    def wview_tm(w, k0, kp, c0, n):
        return w[k0 * 128:(k0 + kp) * 128, c0:c0 + n].rearrange("(c p) n -> p c n", p=128)

    gb = Ring([(psf[i], ("PS", i)) for i in range(4)])

    def gemm_fm(w, col0, ncols, AT, tg, epi, kcn=None, key_at="AT", hook=None):
        kcn = kcn or KC
        for ng in range(ncols // 256):
            wv, wkey = wload(wview_fm(w, col0 + ng * 256, 256), kcn, 256)
            for c2 in range(2):
                for ts in range(tg // 512):
                    ps, pkey = gb.next()
                    for k in range(kcn):
                        rk = [(key_at, ts * 4 + i, k // 8) for i in range(4)]
                        P.add("pe", lambda e, ps=ps, wv=wv, k=k, c2=c2, ts=ts: e.matmul(
                            ps[:], wv[:, k, c2 * 128:(c2 + 1) * 128], AT[:, k, ts * 512:(ts + 1) * 512],
                            start=(k == 0), stop=(k == kcn - 1)),
                            reads=[wkey] + rk, writes=[pkey])
                    epi(ng * 2 + c2, ts, ps, pkey)
            if hook is not None:
                hook(ng)

    def gemm_tm(w, kcn, col0, ncols, AT, tbs, epi, key_at="AT"):
        pieces = [(k0, min(16, kcn - k0)) for k0 in range(0, kcn, 16)]
        for nt in range(ncols // 512):
            for pi, (k0, kp) in enumerate(pieces):
                wv, wkey = wload(wview_tm(w, k0, kp, col0 + nt * 512, 512), kp, 512)
                for tb in tbs:
                    ps, pkey = gb.next()
                    for k in range(kp):
                        P.add("pe", lambda e, ps=ps, wv=wv, k=k, k0=k0, tb=tb: e.matmul(
                            ps[:], AT[:, k0 + k, tb * 128:(tb + 1) * 128], wv[:, k, :],
                            start=(k == 0), stop=(k == kp - 1)),
                            reads=[wkey, (key_at, tb, (k0 + k) // 8)], writes=[pkey])
                    epi(pi, len(pieces), tb, nt, ps, pkey)

    tpr = Ring([(psb[i], ("PB", i)) for i in range(2)])
    evtoggle = [0]

    def copy_any(out, in_, reads, writes):
        evtoggle[0] ^= 1
        if evtoggle[0]:
            P.add("act", lambda e: e.copy(out=out, in_=in_), reads=reads, writes=writes)
        else:
            P.add("dve", lambda e: e.tensor_copy(out=out, in_=in_), reads=reads, writes=writes)

    def load_bcast(dst, vec, key):
        P.add("sp", lambda e: e.dma_start(out=dst, in_=vec.broadcast_to([128, vec.shape[1]])),
              writes=[key], dma_key=("ld", key))

    def norm_block(src, r0, gbc, xb, xbkey, xn, ss, tbslot, AT, final_out=None, key_at="AT", xnkey="xn", defer=False):
        P.add("sp", lambda e: e.dma_start(out=xb, in_=src[r0:r0 + 128, :]), writes=[xbkey], dma_key=("ld", xbkey))
        P.add("act", lambda e: e.activation(out=xn, in_=xb, func=AF.Square, accum_out=ss[:, 0:1]),
              reads=[xbkey], writes=[xnkey, "ss0"])
        P.add("act", lambda e: e.activation(out=ss[:, 1:2], in_=ss[:, 0:1], func=AF.Sqrt, bias=epsc[:, 0:1], scale=1.0 / D),
              reads=["ss0"], writes=["ss1"])
        P.add("dve", lambda e: e.reciprocal(out=ss[:, 2:3], in_=ss[:, 1:2]), reads=["ss1"], writes=["ss2"])
        if final_out is not None:
            fo, fokey, dst = final_out
            P.add("dve", lambda e: e.scalar_tensor_tensor(out=fo, in0=xb, scalar=ss[:, 2:3], in1=gbc, op0=ALU.mult, op1=ALU.mult),
                  reads=[xbkey, "ss2", "gbc"], writes=[fokey])
            return P.add("sp", lambda e: e.dma_start(out=dst, in_=fo), reads=[fokey], dma_key=("st", fokey))
        P.add("dve", lambda e: e.scalar_tensor_tensor(out=xn, in0=xb, scalar=ss[:, 2:3], in1=gbc, op0=ALU.mult, op1=ALU.mult),
              reads=[xbkey, "ss2", "gbc"], writes=[xnkey])

        def back():
            norm_back(xn, xnkey, tbslot, AT, key_at)
        if defer:
            return back
        back()

    def norm_back(xn, xnkey, tbslot, AT, key_at):
        for c0 in range(0, KC, 8):
            nn = min(8, KC - c0)
            tp, tkey = tpr.next()
            for c in range(nn):
                P.add("pe", lambda e, tp=tp, c=c, c0=c0: e.transpose(out=tp[:, c * 128:(c + 1) * 128],
                                                                     in_=xn[:, (c0 + c) * 128:(c0 + c + 1) * 128], identity=ident_b),
                      reads=[xnkey], writes=[tkey])
            copy_any(AT[:, c0:c0 + nn, tbslot * 128:(tbslot + 1) * 128],
                     tp[:, 0:nn * 128].rearrange("p (c t) -> p c t", c=nn), [tkey], [(key_at, tbslot, c0 // 8)])

    def norm_phase(src, row0, ntb, gvec, AT, off):
        xbs = [(carve(off + i * 4 * D, 4 * D, F32), ("xb", i)) for i in range(2)]
        gbc = carve(off + 8 * D, 4 * D, F32)
        xn = carve(off + 12 * D, 2 * D, BF16)
        ss = carve(off + 14 * D, 16, F32)
        load_bcast(gbc, gvec, "gbc")
        for tb in range(ntb):
            xb, xbkey = xbs[tb % 2]
            norm_block(src, row0 + tb * 128, gbc, xb, xbkey, xn, ss, tb, AT)

    def load_AT(AT, src, kcn, c0, ntok, nparts=4):
        step = ((kcn + nparts - 1) // nparts + 7) // 8 * 8
        for i, k0 in enumerate(range(0, kcn, step)):
            k1 = min(kcn, k0 + step)
            P.add("sp", lambda e, k0=k0, k1=k1: e.dma_start(
                out=AT[:, k0:k1, :], in_=src[k0 * 128:k1 * 128, c0:c0 + ntok].rearrange("(c p) t -> p c t", p=128)),
                writes=[("AT", tb, cg) for tb in range(ntok // 128) for cg in range(k0 // 8, (k1 - 1) // 8 + 1)], dma_key=("ldat", i))

    scr_i = carve(0, 128 * 4, F32).bitcast(I32)
    scr_i2 = carve(512, NTAB * 4, F32).bitcast(I32)
    D1 = carve(2048, NTAB * 4, F32); DL = carve(4096, NTAB * 4, F32)
    t1 = carve(6144, NTAB * 4, F32); t2 = carve(8192, NTAB * 4, F32)
    P.add("pool", lambda e: e.iota(out=scr_i, pattern=[[1, 128]], base=0, channel_multiplier=-1), writes=["scr_i"])
    P.add("dve", lambda e: e.tensor_copy(out=valf, in_=scr_i), reads=["scr_i"], writes=["valf"])
    P.add("dve", lambda e: e.tensor_scalar(out=ident_f, in0=valf, scalar1=0.0, scalar2=None, op0=ALU.is_equal), reads=["valf"], writes=["ident_f"])
    P.add("dve", lambda e: e.tensor_scalar(out=tri_f, in0=valf, scalar1=0.0, scalar2=None, op0=ALU.is_ge), reads=["valf"], writes=["tri_f"])
    P.add("dve", lambda e: e.tensor_copy(out=ident_b, in_=ident_f), reads=["ident_f"], writes=["ident_b"])
    P.add("dve", lambda e: e.memset(ones_f, 1.0), writes=["ones_f"])
    P.add("dve", lambda e: e.memset(epsc[:, 0:1], RMS_EPS), writes=["epsc"])
    P.add("dve", lambda e: e.memset(epsc[:, 1:2], LN_EPS), writes=["epsc"])
    P.add("sp", lambda e: e.dma_start(out=jv, in_=jv_d), writes=["jv"], dma_key="s_jv")
    P.add("dve", lambda e: e.tensor_scalar(out=jv128, in0=jv, scalar1=128.0, scalar2=None, op0=ALU.mult), reads=["jv"], writes=["jv128"])
    for r in range(4):
        P.add("dve", lambda e, r=r: e.tensor_scalar(out=jsh[:, r:r + 1], in0=jv, scalar1=128.0, scalar2=-128.0 * r, op0=ALU.mult, op1=ALU.add),
              reads=["jv"], writes=["jsh"])
    for r in range(4):
        P.add("dve", lambda e, r=r: e.tensor_scalar(out=masks[:, r, :], in0=valf, scalar1=jsh[:, r:r + 1], scalar2=0.0, op0=ALU.add, op1=ALU.is_ge),
              reads=["valf", "jsh"], writes=["masks"])
    P.add("pool", lambda e: e.iota(out=scr_i2, pattern=[[-128, NTAB]], base=320, channel_multiplier=1), writes=["scr_i2"])
    P.add("dve", lambda e: e.tensor_copy(out=D1, in_=scr_i2), reads=["scr_i2"], writes=["D1"])
    P.add("dve", lambda e: e.tensor_scalar(out=D1, in0=D1, scalar1=jv128, scalar2=None, op0=ALU.subtract), reads=["D1", "jv128"], writes=["D1"])
    P.add("pool", lambda e: e.iota(out=scr_i2, pattern=[[1, NTAB]], base=-3, channel_multiplier=0), reads=["D1"], writes=["scr_i2"])
    P.add("dve", lambda e: e.tensor_copy(out=DL, in_=scr_i2), reads=["scr_i2"], writes=["DL"])
    P.add("dve", lambda e: e.tensor_scalar(out=DL, in0=DL, scalar1=jv, scalar2=0.0, op0=ALU.add, op1=ALU.is_ge), reads=["DL", "jv"], writes=["DL"])
    P.add("dve", lambda e: e.tensor_tensor(out=t1, in0=D1, in1=DL, op=ALU.mult), reads=["D1", "DL"], writes=["t1"])
    P.add("dve", lambda e: e.tensor_scalar(out=t2, in0=DL, scalar1=1.0, scalar2=NEG_BIG, op0=ALU.subtract, op1=ALU.mult), reads=["DL"], writes=["t2"])
    for h in range(H):
        P.add("dve", lambda e, h=h: e.scalar_tensor_tensor(out=tab[:, h, :], in0=t1, scalar=slopes[h], in1=t2, op0=ALU.mult, op1=ALU.add),
              reads=["t1", "t2"], writes=["tab"])
    DW = carve(10240, NTW * 4, F32)
    scr_i3 = carve(12288, NTW * 4, F32).bitcast(I32)
    P.add("pool", lambda e: e.iota(out=scr_i3, pattern=[[-128, NTW]], base=-320 + 128 * 15, channel_multiplier=1), writes=["scr_i3"])
    P.add("dve", lambda e: e.tensor_copy(out=DW, in_=scr_i3), reads=["scr_i3"], writes=["DW"])
    P.add("dve", lambda e: e.tensor_scalar(out=DW, in0=DW, scalar1=jv128, scalar2=None, op0=ALU.subtract), reads=["DW", "jv128"], writes=["DW"])
    for h, hw in WIDX.items():
        P.add("dve", lambda e, h=h, hw=hw: e.tensor_scalar(out=tabw[:, hw, :], in0=DW, scalar1=slopes[h], scalar2=None, op0=ALU.mult), reads=["DW"], writes=["tabw"])
    lb = carve(16384, 4 * 128 * 4, F32).rearrange("p (a n) -> p a n", a=4)
    junk = carve(20480, 128 * 4, F32)
    for i, v in enumerate((lq1, lk1, lq2, lk2)):
        P.add("sp", lambda e, i=i, v=v: e.dma_start(out=lb[:, i, :], in_=v.broadcast_to([128, 128])), writes=[("lb", i)], dma_key=("s_lb", i))
    for i in range(2):
        P.add("dve", lambda e, i=i: e.tensor_tensor(out=junk, in0=lb[:, 2 * i, :], in1=lb[:, 2 * i + 1, :], op=ALU.mult),
              reads=[("lb", 2 * i), ("lb", 2 * i + 1)], writes=["junk"])
        P.add("dve", lambda e, i=i: e.reduce_sum(out=lamtmp[:, i:i + 1], in_=junk, axis=AX.X), reads=["junk"], writes=[("lt", i)])
        P.add("act", lambda e, i=i: e.activation(out=lamtmp[:, 2 + i:3 + i], in_=lamtmp[:, i:i + 1], func=AF.Exp), reads=[("lt", i)], writes=[("le", i)])
    P.add("dve", lambda e: e.tensor_tensor(out=lamtmp[:, 4:5], in0=lamtmp[:, 3:4], in1=lamtmp[:, 2:3], op=ALU.subtract), reads=[("le", 0), ("le", 1)], writes=["lt4"])
    P.add("dve", lambda e: e.tensor_scalar(out=neglam, in0=lamtmp[:, 4:5], scalar1=-lam_init, scalar2=None, op0=ALU.add), reads=["lt4"], writes=["neglam"])
    P.add("sp", lambda e: e.dma_start(out=subg, in_=subln_g.broadcast_to([128, 256])), writes=["subg"], dma_key="s_subg")
    P.add("dve", lambda e: e.tensor_scalar(out=subg, in0=subg, scalar1=1.0 - lam_init, scalar2=None, op0=ALU.mult), reads=["subg"], writes=["subg"])
    vrows = carve(24576, 128 * 4, F32)
    for a, v in enumerate((ln_v_g, ln_v_b, b_gate[0:1, :], b_gate[1:2, :])):
        P.add("sp", lambda e, v=v: e.dma_start(out=vrows[0:KC, :], in_=v.rearrange("o (c p) -> (o c) p", p=128)), writes=["vrows"], dma_key="s_vr")
        P.add("pe", lambda e: e.transpose(out=psf[4][:, 0:KC], in_=vrows[0:KC, :], identity=ident_f[0:KC, 0:KC]),
              reads=["vrows", "ident_f"], writes=[("PS", 4)])
        P.add("dve", lambda e, a=a: e.tensor_copy(out=cols[:, a, :], in_=psf[4][:, 0:KC]), reads=[("PS", 4)], writes=["cols"])
    wsl = carve(28672, 128 * 4, F32); wtf = carve(32768, 128 * 4, F32)
    bsbc = carve(36864, G * 128 * 4, F32).rearrange("p (g t) -> p g t", g=G)
    P.add("sp", lambda e: e.dma_start(out=bsbc.rearrange("p g t -> p (g t)"), in_=b_s.broadcast_to([128, G * 128])), writes=["bsbc"], dma_key="s_bs")
    for g in range(G):
        P.add("sp", lambda e, g=g: e.dma_start(out=wsl, in_=w_s[g]), writes=["wsl"], dma_key="s_ws")
        P.add("pe", lambda e: e.transpose(out=psf[4][:, 0:128], in_=wsl, identity=ident_f), reads=["wsl", "ident_f"], writes=[("PS", 4)])
        P.add("dve", lambda e: e.tensor_tensor(out=wtf, in0=psf[4][:, 0:128], in1=tri_f, op=ALU.mult), reads=[("PS", 4), "tri_f"], writes=["wtf"])
        P.add("dve", lambda e, g=g: e.tensor_copy(out=wT[:, g, :], in_=wtf), reads=["wtf"], writes=["wT"])
        P.add("pe", lambda e: e.matmul(psf[5][:, 0:128], ones_f, wtf, start=True, stop=True), reads=["wtf", "ones_f"], writes=[("PS", 5)])
        for cc in range(2):
            c = 2 * g + cc
            P.add("dve", lambda e, c=c, g=g: e.scalar_tensor_tensor(out=B2[:, c, :], in0=psf[5][:, 0:128], scalar=cols[:, 1, c:c + 1], in1=bsbc[:, g, :],
                                                                     op0=ALU.mult, op1=ALU.add), reads=[("PS", 5), "cols", "bsbc"], writes=["B2"])
    P.barrier()

    TGK = 512
    NGK = S // TGK
    ATs = [carve(i * KC * TGK * 2, KC * TGK * 2, BF16).rearrange("p (c t) -> p c t", c=KC) for i in range(2)]
    off_n = 2 * KC * TGK * 2
    xb_k = carve(off_n, 4 * D, F32); gbc_k = carve(off_n + 4 * D, 4 * D, F32)
    xn_k = [carve(off_n + 8 * D + i * 2 * D, 2 * D, BF16) for i in range(2)]
    ss_k = carve(off_n + 12 * D, 16, F32)
    soff = off_n + 12 * D + 64
    kst = [(carve(soff + i * TGK * 2, TGK * 2, BF16), ("kst", i)) for i in range(2)]
    vst = [(carve(soff + 2 * TGK * 2 + i * 1024, 1024, BF16), ("vst", i)) for i in range(2)]
    acc_off = soff + 2 * TGK * 2 + 2048
    accs_k = [(carve(acc_off + i * 2048, 2048, F32), ("acc", i)) for i in range(TGK // 128)]
    load_bcast(gbc_k, g_mix, "gbc")
    kring = Ring(kst); vring = Ring(vst)

    nbi = [0]
    backs = []

    def kv_norm(kg, tb, defer=False):
        i = nbi[0] % 2
        nbi[0] += 1
        return norm_block(x_all, kg * TGK + tb * 128, gbc_k, xb_k, ("xb", 0), xn_k[i], ss_k, tb, ATs[kg % 2], key_at=("ATK", kg % 2),
                          xnkey=("xn", i), defer=defer)

    def kv_step(pending):
        if backs:
            backs.pop(0)()
        if pending:
            backs.append(kv_norm(*pending.pop(0), defer=True))

    for tb in range(TGK // 128):
        kv_norm(0, tb)
    for kg in range(NGK):
        pending = [(kg + 1, tb) for tb in range(TGK // 128)] if kg + 1 < NGK else []

        def epi_k(ch, ts, ps, pkey, kg=kg, state={}):
            if ts == 0:
                state["cur"] = kring.next()
            stg, skey = state["cur"]
            copy_any(stg[:, ts * 512:(ts + 1) * 512], ps[:], [pkey], [(skey, ts)])
            if ts == TGK // 512 - 1:
                P.add("sp", lambda e: e.dma_start(out=KT[ch * 128:(ch + 1) * 128, kg * TGK:(kg + 1) * TGK], in_=stg),
                      reads=[(skey, t_) for t_ in range(TGK // 512)], dma_key=("st", skey))

        def hook(ng, pending=pending):
            if ng % 3 == 1:
                kv_step(pending)

        gemm_fm(w_in, 3 * D, D, ATs[kg % 2], TGK, epi_k, key_at=("ATK", kg % 2), hook=hook)
        while pending or backs:
            kv_step(pending)

        def epi_v(pi, npc, tb, nt, ps, pkey, kg=kg):
            acc, akey = accs_k[tb]
            if pi == 0 and npc > 1:
                copy_any(acc, ps[:], [pkey], [akey])
                return
            stg, skey = vring.next()
            if npc > 1:
                P.add("dve", lambda e: e.tensor_tensor(out=stg, in0=ps[:], in1=acc, op=ALU.add), reads=[pkey, akey], writes=[skey])
            else:
                copy_any(stg, ps[:], [pkey], [skey])
            r0 = kg * TGK + tb * 128
            P.add("sp", lambda e: e.dma_start(out=VV[r0:r0 + 128, nt * 512:(nt + 1) * 512], in_=stg), reads=[skey], dma_key=("st", skey))

        gemm_tm(w_in, KC, 4 * D, D, ATs[kg % 2], list(range(TGK // 128)), epi_v, key_at=("ATK", kg % 2))
    P.barrier()

    NTB = TG // 128
    def do_group(tg):
        tok0 = tg * TG
        AT = carve(0, KC * TG * 2, BF16).rearrange("p (c t) -> p c t", c=KC)
        off_n = KC * TG * 2
        norm_phase(x_own, tok0, NTB, g_mix, AT, off_n)
        P.barrier()
        accs = [(carve(off_n + i * 2048, 2048, F32), ("acc", i)) for i in range(NTB)]
        gst = [(carve(off_n + NTB * 2048 + i * 2048, 2048, F32), ("gst", i)) for i in range(3)]
        gstr = Ring(gst)

        def epi_gv(pi, npc, tb, nt, ps, pkey):
            acc, akey = accs[tb]
            if pi == 0 and npc > 1:
                copy_any(acc, ps[:], [pkey], [akey])
                return
            src = ps[:]
            rd = [pkey]
            if npc > 1:
                P.add("dve", lambda e: e.tensor_tensor(out=acc, in0=ps[:], in1=acc, op=ALU.add), reads=[pkey, akey], writes=[akey])
                src = acc
                rd = [akey]
            g_, gkey = gstr.next()
            P.add("act", lambda e: e.activation(out=g_, in_=src, func=AF.Gelu), reads=rd, writes=[gkey])
            r0 = tok0 + tb * 128
            P.add("sp", lambda e: e.dma_start(out=GV[r0:r0 + 128, nt * 512:(nt + 1) * 512], in_=g_), reads=[gkey], dma_key=("st", gkey))

        gemm_tm(w_in, KC, D, D, AT, list(range(NTB)), epi_gv)
        P.barrier()
        gvs = [(carve(off_n + i * 4 * D, 4 * D, F32), ("gvb", i)) for i in range(2)]
        nbf = carve(off_n + 8 * D, 2 * D, BF16)
        mts = carve(off_n + 10 * D, 2 * D, BF16).rearrange("p (c t) -> p c t", c=KC)
        soff = off_n + 12 * D
        stat = carve(soff, 256, F32)
        bnst = carve(soff + 256, (D // 512) * 6 * 4, F32).rearrange("p (n s) -> p n s", s=6)
        soff3 = off_n
        for tb in range(NTB):
            gv, gvkey = gvs[tb % 2]
            r0 = tok0 + tb * 128
            P.add("sp", lambda e, gv=gv, r0=r0: e.dma_start(out=gv, in_=GV[r0:r0 + 128, :]), writes=[gvkey], dma_key=("ld", gvkey))
            for n_ in range(D // 512):
                P.add("dve", lambda e, n_=n_, gv=gv: e.bn_stats(out=bnst[:, n_, :], in_=gv[:, n_ * 512:(n_ + 1) * 512]),
                      reads=[gvkey], writes=["bnst"])
            P.add("dve", lambda e: e.bn_aggr(out=stat[:, 0:2], in_=bnst.rearrange("p n s -> p (n s)")), reads=["bnst"], writes=["st01"])
            P.add("act", lambda e: e.activation(out=stat[:, 2:3], in_=stat[:, 1:2], func=AF.Sqrt, bias=epsc[:, 1:2], scale=1.0), reads=["st01"], writes=["st2"])
            P.add("dve", lambda e: e.reciprocal(out=stat[:, 3:4], in_=stat[:, 2:3]), reads=["st2"], writes=["st3"])
            P.add("dve", lambda e, gv=gv: e.tensor_scalar(out=nbf, in0=gv, scalar1=stat[:, 0:1], scalar2=stat[:, 3:4], op0=ALU.subtract, op1=ALU.mult),
                  reads=[gvkey, "st01", "st3"], writes=["nbf"])
            for c0 in range(0, KC, 4):
                ps, pkey = gb.next()
                for c in range(4):
                    cc = c0 + c
                    P.add("pe", lambda e, ps=ps, c=c, cc=cc: e.matmul(ps[:, c * 128:(c + 1) * 128], nbf[:, cc * 128:(cc + 1) * 128], wT[:, cc // 2, :], start=True, stop=True),
                          reads=["nbf", "wT"], writes=[pkey])
                for c in range(4):
                    cc = c0 + c
                    P.add("dve", lambda e, ps=ps, c=c, cc=cc: e.scalar_tensor_tensor(out=mts[:, cc, :], in0=ps[:, c * 128:(c + 1) * 128], scalar=cols[:, 0, cc:cc + 1], in1=B2[:, cc, :],
                                                                                  op0=ALU.mult, op1=ALU.add), reads=[pkey, "cols", "B2"], writes=["mts"])
            c0t = tok0 + tb * 128
            P.add("sp", lambda e, c0t=c0t: e.dma_start(out=MT[:, c0t:c0t + 128].rearrange("(c p) t -> p c t", p=128), in_=mts), reads=["mts"], dma_key=("st", "mts"))
        P.barrier()
        fst = [(carve(soff3 + i * TG * 2, TG * 2, BF16), ("fst", i)) for i in range(3)]
        mtl = [(carve(soff3 + 3 * TG * 2 + i * 1024, 1024, BF16), ("mtl", i)) for i in range(3)]
        gtmp = [(carve(soff3 + 3 * TG * 2 + 3072 + i * 2048, 2048, F32), ("gtmp", i)) for i in range(2)]
        fring = Ring(fst); mring = Ring(mtl); tring = Ring(gtmp)

        def fm_store(dst):
            state = {}

            def put(ch, ts, producer):
                if ts == 0:
                    state["cur"] = fring.next()
                stg, skey = state["cur"]
                producer(stg[:, ts * 512:(ts + 1) * 512], (skey, ts))
                if ts == TG // 512 - 1:
                    P.add("sp", lambda e: e.dma_start(out=dst[ch * 128:(ch + 1) * 128, tok0:tok0 + TG], in_=stg),
                          reads=[(skey, t_) for t_ in range(TG // 512)], dma_key=("st", skey))
            return put

        put_u = fm_store(YAT)

        def epi_u(ch, ts, ps, pkey):
            ml, mkey = mring.next()
            tt, tkey = tring.next()
            c0t = tok0 + ts * 512
            P.add("sp", lambda e: e.dma_start(out=ml, in_=MT[ch * 128:(ch + 1) * 128, c0t:c0t + 512]), writes=[mkey], dma_key=("ld", mkey))
            P.add("act", lambda e: e.activation(out=tt, in_=ps[:], func=AF.Gelu), reads=[pkey], writes=[tkey])
            put_u(ch, ts, lambda o, skey: P.add("dve", lambda e: e.tensor_tensor(out=o, in0=tt, in1=ml, op=ALU.mult), reads=[tkey, mkey], writes=[skey]))

        gemm_fm(w_in, 0, D, AT, TG, epi_u)
        put_q = fm_store(QT)

        def epi_q(ch, ts, ps, pkey):
            put_q(ch, ts, lambda o, skey: copy_any(o, ps[:], [pkey], [skey]))

        gemm_fm(w_in, 2 * D, D, AT, TG, epi_q)
        for a, dst in ((0, GA), (1, GB)):
            put_g = fm_store(dst)

            def epi_g(ch, ts, ps, pkey, a=a, put_g=put_g):
                put_g(ch, ts, lambda o, skey: P.add("act", lambda e: e.activation(out=o, in_=ps[:], func=AF.Sigmoid, bias=cols[:, 2 + a, ch:ch + 1], scale=1.0),
                                                    reads=[pkey, "cols"], writes=[skey]))

            gemm_fm(w_in, (5 + a) * D, D, AT, TG, epi_g)
        P.barrier()

        o = [0]

        def al(nbytes, dt):
            a = carve(o[0], nbytes, dt)
            o[0] += (nbytes + 63) // 64 * 64
            return a

        qbuf = [(al(2 * TG * 2, BF16).rearrange("p (m t) -> p m t", m=2), ("qb", i)) for i in range(2)]
        kbuf = [(al(2 * CH * 128 * 2, BF16).rearrange("p (m t) -> p m t", m=2), ("kb", i)) for i in range(3)]
        vbuf = [(al(CH * 258 * 2, BF16).rearrange("p (c e) -> p c e", e=258), ("vb", i)) for i in range(3)]
        pbuf = [(al(1024, BF16).rearrange("p (m q) -> p m q", m=2), ("pt", i)) for i in range(4)]
        o1n = al(2 * 256 * 4, F32).rearrange("p (i e) -> p i e", i=2)
        ofin = [(al(1024, F32), ("of", i)) for i in range(2)]
        ybf = [(al(512, BF16), ("yb", i)) for i in range(2)]
        ybT = [(al(2 * 256 * 2, BF16).rearrange("p (a t) -> p a t", a=2), ("ybT", i)) for i in range(2)]
        sm = al(64 * 4, F32)
        sq = al(1024, F32)
        for vb, vkey in vbuf:
            P.add("dve", lambda e, vb=vb: e.memset(vb[:, :, 256:257], 1.0), writes=[vkey])
        qring = Ring(qbuf); kring = Ring(kbuf); vring = Ring(vbuf); pring = Ring(pbuf)
        ofr = Ring(ofin); ybr = Ring(ybf); ybTr = Ring(ybT)
        sring = Ring([(psf[4], ("PS", 4)), (psf[5], ("PS", 5)), (psb[1][:].bitcast(F32), ("PB", 1))])
        tpa = Ring([(psb[0], ("PB", 0))])
        smi = [0]
        rows = []
        for h in range(H):
            for qt in range(TG // 256):
                m0 = tg * NTB + 2 * qt
                nkb = 4 * m0 + 8
                for kb in range(nkb):
                    i0 = max(0, (kb - 4 * m0) // 4)
                    i1 = 2
                    if DSKIP[h] is not None:
                        i1 = min(2, max(0, (DSKIP[h] + kb - 4 * m0 + 3) // 4))
                    if i1 <= i0:
                        continue
                    rows.append(dict(h=h, qt=qt, kb=kb, m0=m0, nkb=nkb, i0=i0, i1=i1))
        cur = {}

        def emit_S(row):
            h, qt, kb, m0, nkb = row["h"], row["qt"], row["kb"], row["m0"], row["nkb"]
            if cur.get("qh") != h:
                cur["qh"] = h
                qb, qkey = qring.next()
                P.add("sp", lambda e: e.dma_start(out=qb, in_=QT[h * 256:(h + 1) * 256, tok0:tok0 + TG].rearrange("(m p) t -> p m t", p=128)),
                      writes=[qkey], dma_key=("ld", qkey))
                cur["q"] = (qb, qkey)
            if cur.get("unit") != (h, qt):
                cur["unit"] = (h, qt)
                cur["yT"] = ybTr.next()
            if cur.get("chunk") != (h, qt, kb // CH):
                cur["chunk"] = (h, qt, kb // CH)
                kc0 = (kb // CH) * CH
                l0 = kb - kc0
                nk = min(CH, nkb - kc0)
                kb_, kkey = kring.next()
                P.add("sp", lambda e: e.dma_start(
                    out=kb_[:, :, l0 * 128:nk * 128], in_=KT[h * 256:(h + 1) * 256, kb * 128:(kc0 + nk) * 128].rearrange("(m p) t -> p m t", p=128)),
                    writes=[kkey], dma_key=("ld", kkey))
                vb, vkey = vring.next()
                P.add("sp", lambda e: e.dma_start(
                    out=vb[:, l0:nk, 0:256], in_=VV[kb * 128:(kc0 + nk) * 128, h * 256:(h + 1) * 256].rearrange("(c p) e -> p c e", p=128)),
                    writes=[vkey], dma_key=("ld", vkey))
                cur["k"] = (kb_, kkey)
                cur["v"] = (vb, vkey)
            row["q"] = cur["q"]; row["k"] = cur["k"]; row["v"] = cur["v"]; row["yT"] = cur["yT"]
            qb, qkey = row["q"]
            kb_, kkey = row["k"]
            kl = kb % CH
            i0, i1 = row["i0"], row["i1"]
            sps, skey = sring.next()
            row["s"] = (sps, skey)
            for mp in range(2):
                P.add("pe", lambda e, mp=mp: e.matmul(
                    sps[:, mp * 256 + i0 * 128:mp * 256 + i1 * 128], kb_[:, mp, kl * 128:(kl + 1) * 128],
                    qb[:, mp, qt * 256 + i0 * 128:qt * 256 + i1 * 128], start=True, stop=True),
                    reads=[kkey, qkey], writes=[(skey, mp)])

        def emit_rest(row):
            h, qt, kb, m0, nkb = row["h"], row["qt"], row["kb"], row["m0"], row["nkb"]
            sps, skey = row["s"]
            s3 = sps[:, 0:512].rearrange("p (m q) -> p m q", m=2)
            vb, vkey = row["v"]
            i0, i1 = row["i0"], row["i1"]
            kl = kb % CH
            pt, pkey_ = pring.next()
            if WIDE[h]:
                cw = 4 * m0 - kb + 15
                P.add("act", lambda e: e.activation(
                    out=pt[:, :, i0 * 128:i1 * 128], in_=s3[:, :, i0 * 128:i1 * 128], func=AF.Exp, bias=tabw[:, WIDX[h], cw:cw + 1], scale=SCALE),
                    reads=[(skey, 0), (skey, 1)], writes=[(pkey_, i) for i in range(i0, i1)])
            for i in range(i0, i1):
                dl = 4 * (m0 + i) - kb
                if not WIDE[h]:
                    P.add("act", lambda e, i=i, dl=dl: e.activation(
                        out=pt[:, :, i * 128:(i + 1) * 128], in_=s3[:, :, i * 128:(i + 1) * 128], func=AF.Exp, bias=tab[:, h, dl + 3:dl + 4], scale=SCALE),
                        reads=[(skey, 0), (skey, 1)], writes=[(pkey_, i)])
                r = -dl
                if 0 <= r <= 3:
                    for mp in range(2):
                        P.add("dve", lambda e, i=i, r=r, mp=mp: e.tensor_tensor(out=pt[:, mp, i * 128:(i + 1) * 128], in0=pt[:, mp, i * 128:(i + 1) * 128], in1=masks[:, r, :], op=ALU.mult),
                              reads=[(pkey_, i)], writes=[(pkey_, i)])
                kfirst = 0 if DSKIP[h] is None else max(0, 4 * (m0 + i) - DSKIP[h] + 1)
                for mp in range(2):
                    P.add("pe", lambda e, i=i, mp=mp, first=(kb == kfirst), last=(kb == 4 * (m0 + i) + 3): e.matmul(
                        psf[mp * 2 + i][:, 0:257], pt[:, mp, i * 128:(i + 1) * 128], vb[:, kl, 0:257], start=first, stop=last),
                        reads=[(pkey_, i), vkey], writes=[("PS", mp * 2 + i)])
            if kb != nkb - 1:
                return
            yT, yTkey = row["yT"]
            for i in range(2):
                s0 = (smi[0] % 8) * 8
                smi[0] += 1
                P.add("dve", lambda e, i=i, s0=s0: e.reciprocal(out=sm[:, s0:s0 + 1], in_=psf[i][:, 256:257]), reads=[("PS", i)], writes=[("sm", s0)])
                P.add("dve", lambda e, i=i, s0=s0: e.reciprocal(out=sm[:, s0 + 1:s0 + 2], in_=psf[2 + i][:, 256:257]), reads=[("PS", 2 + i)], writes=[("sm", s0 + 1)])
                P.add("dve", lambda e, i=i, s0=s0: e.tensor_scalar(out=o1n[:, i, :], in0=psf[i][:, 0:256], scalar1=sm[:, s0:s0 + 1], scalar2=None, op0=ALU.mult),
                      reads=[("PS", i), ("sm", s0)], writes=[("o1n", i)])
                of, okey = ofr.next()
                yb, ykey = ybr.next()
                P.add("dve", lambda e, s0=s0: e.tensor_scalar(out=sm[:, s0 + 5:s0 + 6], in0=sm[:, s0 + 1:s0 + 2], scalar1=neglam, scalar2=None, op0=ALU.mult),
                      reads=[("sm", s0 + 1), "neglam"], writes=[("sm", s0 + 5)])
                P.add("dve", lambda e, i=i, of=of, s0=s0: e.scalar_tensor_tensor(out=of, in0=psf[2 + i][:, 0:256], scalar=sm[:, s0 + 5:s0 + 6], in1=o1n[:, i, :],
                                                                              op0=ALU.mult, op1=ALU.add), reads=[("PS", 2 + i), ("sm", s0 + 5), ("o1n", i)], writes=[okey])
                P.add("act", lambda e, of=of, s0=s0: e.activation(out=sq, in_=of, func=AF.Square, accum_out=sm[:, s0 + 2:s0 + 3]), reads=[okey], writes=["sq", ("sm", s0 + 2)])
                P.add("act", lambda e, s0=s0: e.activation(out=sm[:, s0 + 3:s0 + 4], in_=sm[:, s0 + 2:s0 + 3], func=AF.Sqrt, bias=epsc[:, 1:2], scale=1.0 / 256),
                      reads=[("sm", s0 + 2)], writes=[("sm", s0 + 3)])
                P.add("dve", lambda e, s0=s0: e.reciprocal(out=sm[:, s0 + 4:s0 + 5], in_=sm[:, s0 + 3:s0 + 4]), reads=[("sm", s0 + 3)], writes=[("sm", s0 + 4)])
                P.add("dve", lambda e, of=of, yb=yb, s0=s0: e.scalar_tensor_tensor(out=yb, in0=of, scalar=sm[:, s0 + 4:s0 + 5], in1=subg, op0=ALU.mult, op1=ALU.mult),
                      reads=[okey, ("sm", s0 + 4), "subg"], writes=[ykey])
                tp, tkey = tpa.next()
                for a in range(2):
                    P.add("pe", lambda e, tp=tp, a=a, yb=yb: e.transpose(out=tp[:, a * 128:(a + 1) * 128], in_=yb[:, a * 128:(a + 1) * 128], identity=ident_b),
                          reads=[ykey], writes=[tkey])
                copy_any(yT[:, :, i * 128:(i + 1) * 128], tp[:, 0:256].rearrange("p (a t) -> p a t", a=2), [tkey], [(yTkey, i)])
            c0t = tok0 + qt * 256
            P.add("sp", lambda e: e.dma_start(out=YBT[h * 256:(h + 1) * 256, c0t:c0t + 256].rearrange("(a p) t -> p a t", p=128), in_=yT),
                  reads=[(yTkey, i_) for i_ in range(2)], dma_key=("st", yTkey))

        LA = 2
        for ri in range(min(LA, len(rows))):
            emit_S(rows[ri])
        for ri, row in enumerate(rows):
            if ri + LA < len(rows):
                emit_S(rows[ri + LA])
            emit_rest(row)
        P.barrier()

        AT = carve(0, KC * TG * 2, BF16).rearrange("p (c t) -> p c t", c=KC)
        soff = KC * TG * 2
        gl = [(carve(soff + i * 1024, 1024, BF16), ("gl", i)) for i in range(3)]
        tl = [(carve(soff + 3072 + i * 2048, 2048, F32), ("tl", i)) for i in range(3)]
        tmpm = [(carve(soff + 3072 + 6144 + i * 2048, 2048, F32), ("tmpm", i)) for i in range(2)]
        fst = [(carve(soff + 3072 + 6144 + 4096 + i * TG * 2, TG * 2, BF16), ("fst", i)) for i in range(3)]
        glr = Ring(gl); tlr = Ring(tl); tmr = Ring(tmpm); fring = Ring(fst)
        load_AT(AT, YAT, KC, tok0, TG)

        def epi_a(ch, ts, ps, pkey):
            g_, gkey = glr.next()
            t_, tkey = tlr.next()
            c0t = tok0 + ts * 512
            P.add("sp", lambda e: e.dma_start(out=g_, in_=GA[ch * 128:(ch + 1) * 128, c0t:c0t + 512]), writes=[gkey], dma_key=("ld", gkey))
            P.add("dve", lambda e: e.tensor_tensor(out=t_, in0=ps[:], in1=g_, op=ALU.mult), reads=[pkey, gkey], writes=[tkey])
            P.add("sp", lambda e: e.dma_start(out=T1[ch * 128:(ch + 1) * 128, c0t:c0t + 512], in_=t_), reads=[tkey], dma_key=("st", tkey))

        gemm_fm(w_br_a, 0, D, AT, TG, epi_a)
        P.barrier()
        load_AT(AT, YBT, KC, tok0, TG)
        put_m = fm_store(MGT)

        def epi_b(ch, ts, ps, pkey):
            g_, gkey = glr.next()
            t_, tkey = tlr.next()
            m_, mkey = tmr.next()
            c0t = tok0 + ts * 512
            P.add("sp", lambda e: e.dma_start(out=g_, in_=GB[ch * 128:(ch + 1) * 128, c0t:c0t + 512]), writes=[gkey], dma_key=("ld", gkey))
            P.add("sp", lambda e: e.dma_start(out=t_, in_=T1[ch * 128:(ch + 1) * 128, c0t:c0t + 512]), writes=[tkey], dma_key=("ld", tkey))
            P.add("dve", lambda e: e.tensor_tensor(out=m_, in0=ps[:], in1=g_, op=ALU.mult), reads=[pkey, gkey], writes=[mkey])
            put_m(ch, ts, lambda o_, skey: P.add("dve", lambda e: e.tensor_tensor(out=o_, in0=m_, in1=t_, op=ALU.add), reads=[mkey, tkey], writes=[skey]))

        gemm_fm(w_br_b, 0, D, AT, TG, epi_b)
        P.barrier()
        load_AT(AT, MGT, KC, tok0, TG)

        def tm_residual(res_src, dst, accs, xl, ost, post=None):
            xlr = Ring(xl); osr = Ring(ost)

            def epi(pi, npc, tb, nt, ps, pkey):
                acc, akey = accs[tb % len(accs)]
                r0 = tok0_cur[0] + tb * 128
                if pi == 0:
                    x_, xkey = xlr.next()
                    P.add("sp", lambda e: e.dma_start(out=x_, in_=res_src[r0:r0 + 128, nt * 512:(nt + 1) * 512]), writes=[xkey], dma_key=("ld", xkey))
                    tgt, tk = (acc, akey) if npc > 1 else osr.next()
                    P.add("dve", lambda e: e.tensor_tensor(out=tgt, in0=ps[:], in1=x_, op=ALU.add), reads=[pkey, xkey], writes=[tk])
                    if npc > 1:
                        return
                elif pi < npc - 1:
                    P.add("dve", lambda e: e.tensor_tensor(out=acc, in0=ps[:], in1=acc, op=ALU.add), reads=[pkey, akey], writes=[akey])
                    return
                else:
                    tgt, tk = osr.next()
                    P.add("dve", lambda e: e.tensor_tensor(out=tgt, in0=ps[:], in1=acc, op=ALU.add), reads=[pkey, akey], writes=[tk])
                P.add("sp", lambda e: e.dma_start(out=dst[r0:r0 + 128, nt * 512:(nt + 1) * 512], in_=tgt), reads=[tk], dma_key=("st", tk))
            return epi

        tok0_cur = [tok0]
        a0 = soff
        accs = [(carve(a0 + i * 2048, 2048, F32), ("acc", i)) for i in range(NTB)]
        xl = [(carve(a0 + NTB * 2048 + i * 2048, 2048, F32), ("xl", i)) for i in range(3)]
        ost = [(carve(a0 + NTB * 2048 + 6144 + i * 2048, 2048, F32), ("ost", i)) for i in range(3)]
        gemm_tm(w_o, KC, 0, D, AT, list(range(NTB)), tm_residual(x_own, X1, accs, xl, ost))
        P.barrier()

        norm_phase(X1, tok0, NTB, g_ffn, AT, off_n)
        P.barrier()
        fst = [(carve(off_n + i * TG * 2, TG * 2, BF16), ("fst", i)) for i in range(3)]
        stmp = [(carve(off_n + 3 * TG * 2 + i * 2048, 2048, F32), ("stmp", i)) for i in range(3)]
        fring = Ring(fst); sring2 = Ring(stmp)
        put_f = fm_store(FFA)
        for pg in range(DFF // 256):
            wg, wgk = wload(wview_fm(w_gu, pg * 256, 256), KC, 256)
            wu, wuk = wload(wview_fm(w_gu, DFF + pg * 256, 256), KC, 256)
            for c2 in range(2):
                for ts in range(TG // 512):
                    pg_, pgk = gb.next()
                    pu_, puk = gb.next()
                    for (ps, pk, wv, wk) in ((pg_, pgk, wg, wgk), (pu_, puk, wu, wuk)):
                        for k in range(KC):
                            rk = [("AT", ts * 4 + i, k // 8) for i in range(4)]
                            P.add("pe", lambda e, ps=ps, wv=wv, k=k, c2=c2, ts=ts: e.matmul(
                                ps[:], wv[:, k, c2 * 128:(c2 + 1) * 128], AT[:, k, ts * 512:(ts + 1) * 512], start=(k == 0), stop=(k == KC - 1)),
                                reads=[wk] + rk, writes=[pk])
                    s_, sk = sring2.next()
                    P.add("act", lambda e, s_=s_, pg_=pg_: e.activation(out=s_, in_=pg_[:], func=AF.Silu), reads=[pgk], writes=[sk])
                    put_f(pg * 2 + c2, ts, lambda o_, skey, s_=s_, sk=sk, pu_=pu_, puk=puk: P.add(
                        "dve", lambda e: e.tensor_tensor(out=o_, in0=pu_[:], in1=s_, op=ALU.mult), reads=[puk, sk], writes=[skey]))
        P.barrier()

        ATd = carve(0, FC * TGD * 2, BF16).rearrange("p (c t) -> p c t", c=FC)
        a0 = FC * TGD * 2
        nd = TGD // 128
        accs = [(carve(a0 + i * 2048, 2048, F32), ("acc", i)) for i in range(nd)]
        xl = [(carve(a0 + nd * 2048 + i * 2048, 2048, F32), ("xl", i)) for i in range(3)]
        ost = [(carve(a0 + nd * 2048 + 6144 + i * 2048, 2048, F32), ("ost", i)) for i in range(3)]
        for sd in range(TG // TGD):
            tok0_cur[0] = tok0 + sd * TGD
            load_AT(ATd, FFA, FC, tok0_cur[0], TGD, nparts=6)
            gemm_tm(w_down, FC, 0, D, ATd, list(range(nd)), tm_residual(X1, X2, accs, xl, ost))
            P.barrier()
        tok0_cur[0] = tok0

        norm_phase(X2, tok0, NTB, g_ple, AT, off_n)
        pT = carve(off_n, PC * TG * 2, BF16).rearrange("p (c t) -> p c t", c=PC)
        pl = carve(off_n + PC * TG * 2, PLE * 4, F32)
        plb = carve(off_n + PC * TG * 2 + PLE * 4, PLE * 2, BF16)
        P.barrier()
        for tb in range(NTB):
            r0 = tok0 + tb * 128
            P.add("sp", lambda e, r0=r0: e.dma_start(out=pl, in_=p_own[r0:r0 + 128, :]), writes=["pl"], dma_key=("ld", "pl"))
            P.add("dve", lambda e: e.tensor_copy(out=plb, in_=pl), reads=["pl"], writes=["plb"])
            tp, tkey = tpr.next()
            for c in range(PC):
                P.add("pe", lambda e, tp=tp, c=c: e.transpose(out=tp[:, c * 128:(c + 1) * 128], in_=plb[:, c * 128:(c + 1) * 128], identity=ident_b),
                      reads=["plb"], writes=[tkey])
            copy_any(pT[:, :, tb * 128:(tb + 1) * 128], tp[:, 0:PC * 128].rearrange("p (c t) -> p c t", c=PC), [tkey], [("pT", tb)])
        a0 = off_n + PC * TG * 2 + PLE * 6
        a0 = (a0 + 63) // 64 * 64
        accs = [(carve(a0 + i * 2048, 2048, F32), ("acc", i)) for i in range(NTB)]
        xl = [(carve(a0 + NTB * 2048 + i * 2048, 2048, F32), ("xl", i)) for i in range(3)]
        ost = [(carve(a0 + NTB * 2048 + 6144 + i * 2048, 2048, F32), ("ost", i)) for i in range(3)]
        sg_ = [(carve(a0 + NTB * 2048 + 12288 + i * 2048, 2048, F32), ("sg", i)) for i in range(2)]
        xlr = Ring(xl); osr = Ring(ost); sgr = Ring(sg_)
        ppr = Ring([(psf[4], ("PS", 4)), (psf[5], ("PS", 5))])
        wpp_cur = {}

        def epi_p(pi, npc, tb, nt, ps, pkey):
            acc, akey = accs[tb]
            if pi == 0 and npc > 1:
                copy_any(acc, ps[:], [pkey], [akey])
                return
            src = ps[:]
            rd = [pkey]
            if npc > 1:
                P.add("dve", lambda e: e.tensor_tensor(out=acc, in0=ps[:], in1=acc, op=ALU.add), reads=[pkey, akey], writes=[akey])
                src = acc
                rd = [akey]
            s_, sk = sgr.next()
            P.add("act", lambda e: e.activation(out=s_, in_=src, func=AF.Sigmoid), reads=rd, writes=[sk])
            if wpp_cur.get("nt") != nt:
                wpp_cur["nt"] = nt
                wpp_cur["w"] = wload(wview_tm(w_pp, 0, PC, nt * 512, 512), PC, 512)
            wv, wkey = wpp_cur["w"]
            pp, ppk = ppr.next()
            for c in range(PC):
                P.add("pe", lambda e, pp=pp, c=c, wv=wv: e.matmul(pp[:], pT[:, c, tb * 128:(tb + 1) * 128], wv[:, c, :], start=(c == 0), stop=(c == PC - 1)),
                      reads=[wkey, ("pT", tb)], writes=[ppk])
            x_, xkey = xlr.next()
            r0 = tok0 + tb * 128
            P.add("sp", lambda e: e.dma_start(out=x_, in_=X2[r0:r0 + 128, nt * 512:(nt + 1) * 512]), writes=[xkey], dma_key=("ld", xkey))
            P.add("dve", lambda e: e.tensor_tensor(out=s_, in0=pp[:], in1=s_, op=ALU.mult), reads=[ppk, sk], writes=[sk])
            o_, ok = osr.next()
            P.add("dve", lambda e: e.tensor_tensor(out=o_, in0=s_, in1=x_, op=ALU.add), reads=[sk, xkey], writes=[ok])
            P.add("sp", lambda e: e.dma_start(out=X3[r0:r0 + 128, nt * 512:(nt + 1) * 512], in_=o_), reads=[ok], dma_key=("st", ok))

        gemm_tm(w_pg, KC, 0, D, AT, list(range(NTB)), epi_p)
        P.barrier()

        xbs = [(carve(i * 4 * D, 4 * D, F32), ("xb", i)) for i in range(2)]
        gbc = carve(8 * D, 4 * D, F32)
        xn = carve(12 * D, 2 * D, BF16)
        ss = carve(14 * D, 16, F32)
        fos = [(carve(14 * D + 64 + i * 4 * D, 4 * D, F32), ("fo", i)) for i in range(2)]
        load_bcast(gbc, g_final, "gbc")
        finals = []
        for tb in range(NTB):
            r0 = tok0 + tb * 128
            xb, xbkey = xbs[tb % 2]
            fo, fokey = fos[tb % 2]
            finals.append(norm_block(X3, r0, gbc, xb, xbkey, xn, ss, tb, None, final_out=(fo, fokey, out_d[r0:r0 + 128, :])))
        P.barrier()

    for tg_ in range(NO // TG):
        do_group(tg_)

    P.emit(final_waits=[op for op in P.dma_last.values() if not (isinstance(op.dma_key, tuple) and op.dma_key[0] == "W")])
    st.close()
    return nc


def alibi_slopes(n_heads):
    return [float(np.exp2(np.float32(-8.0) * np.float32(i) / np.float32(n_heads))) for i in range(1, n_heads + 1)]


def make_in_maps(cfg, inputs):
    D = cfg["D"]; S = cfg["S"]
    x = np.asarray(inputs["x"], dtype=np.float32)
    p = np.asarray(inputs["p"], dtype=np.float32)[0]
    B = x.shape[0]
    NBA = S // 128
    NOB = NBA // 4

    def w(name, shape=None):
        a = np.ascontiguousarray(np.asarray(inputs[name], dtype=np.float32))
        return a.reshape(shape) if shape is not None else a

    G = D // 256
    shared = {
        "g_mix": w("g_mix", (1, D)), "w_in": w("w_in", (D, 7 * D)), "b_gate": w("b_gate", (2, D)),
        "ln_v_g": w("ln_v_g", (1, D)), "ln_v_b": w("ln_v_b", (1, D)),
        "w_s": w("w_s", (G, 128, 128)), "b_s": w("b_s", (1, G * 128)),
        "lambda_q1": w("lambda_q1", (1, 128)), "lambda_k1": w("lambda_k1", (1, 128)),
        "lambda_q2": w("lambda_q2", (1, 128)), "lambda_k2": w("lambda_k2", (1, 128)),
        "subln_g": w("subln_g", (1, 256)),
        "w_br_a": w("w_br_a", (D, D)), "w_br_b": w("w_br_b", (D, D)), "w_o": w("w_o", (D, D)),
        "g_ffn": w("g_ffn", (1, D)), "w_gu": w("w_gu", (D, 2 * cfg["DFF"])), "w_down": w("w_down", (cfg["DFF"], D)),
        "g_ple": w("g_ple", (1, D)), "w_ple_gate": w("w_ple_gate", (D, D)), "w_ple_proj": w("w_ple_proj", (cfg["PLE"], D)),
        "g_final": w("g_final", (1, D)),
    }
    maps = []
    for c in range(4 * B):
        b, j = c // 4, c % 4
        xb = x[b].reshape(NBA, 128, D)
        pb = p[b].reshape(NBA, 128, -1)
        m = dict(shared)
        m["x_all"] = np.ascontiguousarray(x[b])
        m["x_own"] = np.ascontiguousarray(xb[j::4].reshape(NOB * 128, D))
        m["p_own"] = np.ascontiguousarray(pb[j::4].reshape(NOB * 128, -1))
        m["jv"] = np.full((128, 1), float(j), dtype=np.float32)
        maps.append(m)
    return maps


def assemble(cfg, results, B):
    D = cfg["D"]; S = cfg["S"]
    NBA = S // 128
    NOB = NBA // 4
    out = np.empty((B, NBA, 128, D), dtype=np.float32)
    for c in range(4 * B):
        b, j = c // 4, c % 4
        out[b, j::4] = np.asarray(results[c]["out"], dtype=np.float32).reshape(NOB, 128, D)
    return out.reshape(B, S, D)


FULL_CFG = {"D": 4096, "S": 8192, "DFF": 11008, "PLE": 256, "slopes": alibi_slopes(16)}


def kernel(**inputs):
    cfg = FULL_CFG
    nc = build(cfg)
    maps = make_in_maps(cfg, inputs)
    res = run_bass_kernel_spmd(nc, maps, core_ids=list(range(8)))
    return assemble(cfg, res.results, 2)
```

```python
import contextlib
import math
import numpy as np
import concourse.bass as bass
import concourse.mybir as mybir
from concourse.bass_utils import run_bass_kernel_spmd

F32 = mybir.dt.float32
BF16 = mybir.dt.bfloat16
I32 = mybir.dt.int32
AF = mybir.ActivationFunctionType
ALU = mybir.AluOpType
AX = mybir.AxisListType

RMS_EPS = 1e-6
LN_EPS = 1e-5
HEAD_DIM = 128
NEG_BIG = 30000.0


class Op:
    __slots__ = ("eng", "fn", "deps", "inc", "val", "dma_key", "dma_val")


class Prog:
    ENGS = ("pe", "act", "dve", "pool", "sp")

    def __init__(self, nc):
        self.nc = nc
        self.ops = {e: [] for e in self.ENGS}
        self.last_w = {}
        self.readers = {}
        self.dma_cnt = {}
        self.dma_last = {}
        self.last_real = {e: None for e in self.ENGS}

    def add(self, eng, fn, reads=(), writes=(), dma_key=None, extra_deps=()):
        op = Op()
        op.eng = eng; op.fn = fn
        op.inc = False; op.val = None; op.dma_key = dma_key; op.dma_val = None
        deps = []
        for r in reads:
            w = self.last_w.get(r)
            if w is not None:
                deps.append((w, True))
        for wk in writes:
            w = self.last_w.get(wk)
            if w is not None:
                deps.append((w, False))
            for rd in self.readers.get(wk, ()):
                deps.append((rd, False))
        for d in extra_deps:
            deps.append((d, True))
        if dma_key is not None:
            c = self.dma_cnt.get(dma_key, 0) + 1
            self.dma_cnt[dma_key] = c
            op.dma_val = 16 * c
            prev = self.dma_last.get(dma_key)
            if prev is not None:
                deps.append((prev, True))
            self.dma_last[dma_key] = op
        fd = []
        seen = set()
        for d, raw in deps:
            if d is op or id(d) in seen:
                continue
            if d.dma_key is None and d.eng == eng:
                if eng == "pe" or not raw:
                    continue
            seen.add(id(d))
            fd.append(d)
        op.deps = fd
        for d in fd:
            if d.dma_key is None:
                d.inc = True
        for r in reads:
            self.readers.setdefault(r, []).append(op)
        for wk in writes:
            self.last_w[wk] = op
            self.readers[wk] = []
        self.ops[eng].append(op)
        if fn is not None and dma_key is None:
            self.last_real[eng] = op
        return op

    def barrier(self, engs=("pe", "act", "dve", "sp"), keep=("W", "WVB")):
        lasts = [self.last_real[e] for e in self.ENGS if self.last_real[e] is not None and e != "pool"]
        dmas = [op for k, op in self.dma_last.items() if not (isinstance(k, tuple) and k[0] in keep)]
        for e in engs:
            self.add(e, None, extra_deps=lasts + dmas)
        for k in list(self.last_w.keys()):
            if not (isinstance(k, tuple) and k[0] in keep):
                del self.last_w[k]
        for k in list(self.readers.keys()):
            if not (isinstance(k, tuple) and k[0] in keep):
                del self.readers[k]

    def emit(self, final_waits=()):
        nc = self.nc
        for e in self.ENGS:
            c = 0
            for op in self.ops[e]:
                if op.dma_key is None and op.inc:
                    assert op.fn is not None
                    c += 1
                    op.val = c
        with contextlib.ExitStack() as st:
            esem = {e: st.enter_context(nc.semaphore("es_" + e)) for e in self.ENGS}
            dsem = {k: st.enter_context(nc.semaphore("ds_%d" % i)) for i, k in enumerate(self.dma_cnt)}
            block = st.enter_context(nc.Block())

            def run(e, engobj):
                waited = {}
                for op in self.ops[e]:
                    need = {}
                    for d in op.deps:
                        if d.dma_key is not None:
                            s, v = dsem[d.dma_key], d.dma_val
                        else:
                            s, v = esem[d.eng], d.val
                        key = id(s)
                        if waited.get(key, 0) >= v:
                            continue
                        if key not in need or need[key][1] < v:
                            need[key] = (s, v)
                    for key, (s, v) in need.items():
                        engobj.wait_ge(s, v)
                        waited[key] = v
                    if op.fn is None:
                        continue
                    ins = op.fn(engobj)
                    if op.dma_key is not None:
                        ins.then_inc(dsem[op.dma_key], 16)
                    elif op.inc:
                        ins.then_inc(esem[e], 1)
                if e == "sp":
                    for op in final_waits:
                        engobj.wait_ge(dsem[op.dma_key], op.dma_val)

            @block.tensor
            def _(t):
                run("pe", t)

            @block.scalar
            def _(t):
                run("act", t)

            @block.vector
            def _(t):
                run("dve", t)

            @block.gpsimd
            def _(t):
                run("pool", t)

            @block.sync
            def _(t):
                run("sp", t)


class Ring:
    def __init__(self, items):
        self.items = items
        self.i = 0

    def next(self):
        it = self.items[self.i % len(self.items)]
        self.i += 1
        return it


def build(cfg):
    D = cfg["D"]; S = cfg["S"]; DFF = cfg["DFF"]; PLE = cfg["PLE"]
    slopes = [float(s) for s in cfg["slopes"]]
    lam_init = 0.2
    H = D // 256; G = D // 256; KC = D // 128; FC = DFF // 128; PC = PLE // 128
    NBA = S // 128; NOB = NBA // 4; NO = NOB * 128
    TG = min(cfg.get("TG", 1024), NO)
    TGK = 512
    TGD = min(cfg.get("TGD", 512), TG)
    CH = min(16, NBA)
    NTAB = NBA + 3
    NTW = NBA + 8
    DSKIP = []
    for sl in slopes:
        d = 0
        while sl * (128 * d - 63) <= 128.0 and d <= NBA:
            d += 1
        DSKIP.append(d if (d <= NBA and cfg.get("skip", True)) else None)
    WIDE = [bool(cfg.get("wide", True)) and sl * 639.0 <= 40.5 for sl in slopes]
    WIDX = {h: i for i, h in enumerate([h for h in range(len(slopes)) if WIDE[h]])}
    NW = max(1, len(WIDX))
    SCALE = HEAD_DIM ** -0.5
    assert TG % 512 == 0 and TGK % 512 == 0 and D % 512 == 0 and DFF % 256 == 0

    nc = bass.Bass("TRN2", target_bir_lowering=False)

    def din(name, shape):
        return nc.dram_tensor(name, list(shape), F32, kind="ExternalInput").ap()

    x_all = din("x_all", [S, D]); x_own = din("x_own", [NO, D]); p_own = din("p_own", [NO, PLE])
    jv_d = din("jv", [128, 1])
    g_mix = din("g_mix", [1, D]); w_in = din("w_in", [D, 7 * D]); b_gate = din("b_gate", [2, D])
    ln_v_g = din("ln_v_g", [1, D]); ln_v_b = din("ln_v_b", [1, D])
    w_s = din("w_s", [G, 128, 128]); b_s = din("b_s", [1, G * 128])
    lq1 = din("lambda_q1", [1, 128]); lk1 = din("lambda_k1", [1, 128])
    lq2 = din("lambda_q2", [1, 128]); lk2 = din("lambda_k2", [1, 128])
    subln_g = din("subln_g", [1, 256])
    w_br_a = din("w_br_a", [D, D]); w_br_b = din("w_br_b", [D, D]); w_o = din("w_o", [D, D])
    g_ffn = din("g_ffn", [1, D]); w_gu = din("w_gu", [D, 2 * DFF]); w_down = din("w_down", [DFF, D])
    g_ple = din("g_ple", [1, D]); w_pg = din("w_ple_gate", [D, D]); w_pp = din("w_ple_proj", [PLE, D])
    g_final = din("g_final", [1, D])
    out_d = nc.dram_tensor("out", [NO, D], F32, kind="ExternalOutput").ap()

    def dscr(name, shape, dt):
        if cfg.get("debug"):
            return nc.dram_tensor(name, list(shape), dt, kind="ExternalOutput").ap()
        return nc.dram_tensor(name, list(shape), dt).ap()

    KT = dscr("KT", [D, S], BF16); VV = dscr("VV", [S, D], BF16)
    QT = dscr("QT", [D, NO], BF16); MT = dscr("MT", [D, NO], BF16)
    YAT = dscr("YAT", [D, NO], BF16); YBT = dscr("YBT", [D, NO], BF16)
    GA = dscr("GA", [D, NO], BF16); GB = dscr("GB", [D, NO], BF16)
    T1 = dscr("T1", [D, NO], F32); MGT = dscr("MGT", [D, NO], BF16)
    X1 = dscr("X1", [NO, D], F32); X2 = dscr("X2", [NO, D], F32); X3 = dscr("X3", [NO, D], F32)
    FFA = dscr("FFA", [DFF, NO], BF16)
    WVB = dscr("WVB", [D, D], BF16)

    P = Prog(nc)
    st = contextlib.ExitStack()
    ARENA_B = 128 * 1024
    arena = st.enter_context(nc.sbuf_tensor("arena", [128, ARENA_B // 2], BF16))
    wring_t = [st.enter_context(nc.sbuf_tensor("wr%d" % i, [128, 8192], BF16)) for i in range(3)]
    CONST_F = 512 + 15 + H * NTAB + NW * NTW + 4 * KC + KC * 128 + 256 + 2 + 3
    cst = st.enter_context(nc.sbuf_tensor("cst", [128, CONST_F], F32))
    cbf = st.enter_context(nc.sbuf_tensor("cbf", [128, 128 + 4 * 128 + G * 128], BF16))
    psf = [st.enter_context(nc.psum_tensor("psf%d" % i, [128, 512], F32)) for i in range(6)]
    psb = [st.enter_context(nc.psum_tensor("psb%d" % i, [128, 1024], BF16)) for i in range(2)]

    co = [0]

    def cf(n):
        a = cst[:, co[0]:co[0] + n]
        co[0] += n
        return a

    ident_f = cf(128); valf = cf(128); tri_f = cf(128); ones_f = cf(128)
    jv = cf(1); jv128 = cf(1); jsh = cf(4); neglam = cf(1); lamtmp = cf(8)
    tab = cf(H * NTAB).rearrange("p (h c) -> p h c", h=H)
    tabw = cf(NW * NTW).rearrange("p (h c) -> p h c", h=NW)
    cols = cf(4 * KC).rearrange("p (a c) -> p a c", a=4)
    B2 = cf(KC * 128).rearrange("p (c t) -> p c t", c=KC)
    subg = cf(256)
    epsc = cf(2)
    assert co[0] <= CONST_F
    bo = [0]

    def cb(n):
        a = cbf[:, bo[0]:bo[0] + n]
        bo[0] += n
        return a

    ident_b = cb(128)
    masks = cb(4 * 128).rearrange("p (r q) -> p r q", r=4)
    wT = cb(G * 128).rearrange("p (g t) -> p g t", g=G)

    def carve(off, nbytes, dt):
        assert off % 4 == 0 and off + nbytes <= ARENA_B, (off, nbytes)
        a = arena[:, off // 2:(off + nbytes) // 2]
        return a.bitcast(F32) if dt == F32 else a

    wr = Ring([(wring_t[i], ("W", i)) for i in range(3)])

    def wload(src_view, kc, n, reads=()):
        t, key = wr.next()
        v = t[:].rearrange("p (c n) -> p c n", n=n)[:, 0:kc, :]
        P.add("pool", lambda e, v=v, s=src_view: e.dma_start(out=v, in_=s), reads=list(reads), writes=[key], dma_key=key)
        return v, key

    def wview_fm(w, c0, n):
        return w[:, c0:c0 + n].rearrange("(c p) n -> p c n", p=128)

    def wview_tm(w, k0, kp, c0, n):
        return w[k0 * 128:(k0 + kp) * 128, c0:c0 + n].rearrange("(c p) n -> p c n", p=128)

    gb = Ring([(psf[i], ("PS", i)) for i in range(4)])

    def gemm_fm(w, col0, ncols, AT, tg, epi, kcn=None, key_at="AT", hook=None):
        kcn = kcn or KC
        for ng in range(ncols // 256):
            wv, wkey = wload(wview_fm(w, col0 + ng * 256, 256), kcn, 256)
            for c2 in range(2):
                for ts in range(tg // 512):
                    ps, pkey = gb.next()
                    for k in range(kcn):
                        rk = [(key_at, ts * 4 + i, k // 8) for i in range(4)]
                        P.add("pe", lambda e, ps=ps, wv=wv, k=k, c2=c2, ts=ts: e.matmul(
                            ps[:], wv[:, k, c2 * 128:(c2 + 1) * 128], AT[:, k, ts * 512:(ts + 1) * 512],
                            start=(k == 0), stop=(k == kcn - 1)),
                            reads=[wkey] + rk, writes=[pkey])
                    epi(ng * 2 + c2, ts, ps, pkey)
            if hook is not None:
                hook(ng)

    def gemm_tm(w, kcn, col0, ncols, AT, tbs, epi, key_at="AT", wreads=None):
        pieces = [(k0, min(16, kcn - k0)) for k0 in range(0, kcn, 16)]
        for nt in range(ncols // 512):
            for pi, (k0, kp) in enumerate(pieces):
                wv, wkey = wload(wview_tm(w, k0, kp, col0 + nt * 512, 512), kp, 512, reads=(wreads(col0 + nt * 512) if wreads else ()))
                for tb in tbs:
                    ps, pkey = gb.next()
                    for k in range(kp):
                        P.add("pe", lambda e, ps=ps, wv=wv, k=k, k0=k0, tb=tb: e.matmul(
                            ps[:], AT[:, k0 + k, tb * 128:(tb + 1) * 128], wv[:, k, :],
                            start=(k == 0), stop=(k == kp - 1)),
                            reads=[wkey, (key_at, tb, (k0 + k) // 8)], writes=[pkey])
                    epi(pi, len(pieces), tb, nt, ps, pkey)

    tpr = Ring([(psb[i], ("PB", i)) for i in range(2)])
    evtoggle = [0]

    def copy_any(out, in_, reads, writes):
        evtoggle[0] ^= 1
        if evtoggle[0]:
            P.add("act", lambda e: e.copy(out=out, in_=in_), reads=reads, writes=writes)
        else:
            P.add("dve", lambda e: e.tensor_copy(out=out, in_=in_), reads=reads, writes=writes)

    def load_bcast(dst, vec, key):
        P.add("sp", lambda e: e.dma_start(out=dst, in_=vec.broadcast_to([128, vec.shape[1]])),
              writes=[key], dma_key=("ld", key))

    def norm_block(src, r0, gbc, xb, xbkey, xn, ss, tbslot, AT, final_out=None, key_at="AT", xnkey="xn", defer=False):
        P.add("sp", lambda e: e.dma_start(out=xb, in_=src[r0:r0 + 128, :]), writes=[xbkey], dma_key=("ld", xbkey))
        P.add("act", lambda e: e.activation(out=xn, in_=xb, func=AF.Square, accum_out=ss[:, 0:1]),
              reads=[xbkey], writes=[xnkey, "ss0"])
        P.add("act", lambda e: e.activation(out=ss[:, 1:2], in_=ss[:, 0:1], func=AF.Sqrt, bias=epsc[:, 0:1], scale=1.0 / D),
              reads=["ss0"], writes=["ss1"])
        P.add("dve", lambda e: e.reciprocal(out=ss[:, 2:3], in_=ss[:, 1:2]), reads=["ss1"], writes=["ss2"])
        if final_out is not None:
            fo, fokey, dst = final_out
            P.add("dve", lambda e: e.scalar_tensor_tensor(out=fo, in0=xb, scalar=ss[:, 2:3], in1=gbc, op0=ALU.mult, op1=ALU.mult),
                  reads=[xbkey, "ss2", "gbc"], writes=[fokey])
            return P.add("sp", lambda e: e.dma_start(out=dst, in_=fo), reads=[fokey], dma_key=("st", fokey))
        P.add("dve", lambda e: e.scalar_tensor_tensor(out=xn, in0=xb, scalar=ss[:, 2:3], in1=gbc, op0=ALU.mult, op1=ALU.mult),
              reads=[xbkey, "ss2", "gbc"], writes=[xnkey])

        def back():
            norm_back(xn, xnkey, tbslot, AT, key_at)
        if defer:
            return back
        back()

    def norm_back(xn, xnkey, tbslot, AT, key_at):
        for c0 in range(0, KC, 8):
            nn = min(8, KC - c0)
            tp, tkey = tpr.next()
            for c in range(nn):
                P.add("pe", lambda e, tp=tp, c=c, c0=c0: e.transpose(out=tp[:, c * 128:(c + 1) * 128],
                                                                     in_=xn[:, (c0 + c) * 128:(c0 + c + 1) * 128], identity=ident_b),
                      reads=[xnkey], writes=[tkey])
            copy_any(AT[:, c0:c0 + nn, tbslot * 128:(tbslot + 1) * 128],
                     tp[:, 0:nn * 128].rearrange("p (c t) -> p c t", c=nn), [tkey], [(key_at, tbslot, c0 // 8)])

    def norm_phase(src, row0, ntb, gvec, AT, off):
        xbs = [(carve(off + i * 4 * D, 4 * D, F32), ("xb", i)) for i in range(2)]
        gbc = carve(off + 8 * D, 4 * D, F32)
        xn = carve(off + 12 * D, 2 * D, BF16)
        ss = carve(off + 14 * D, 16, F32)
        load_bcast(gbc, gvec, "gbc")
        for tb in range(ntb):
            xb, xbkey = xbs[tb % 2]
            norm_block(src, row0 + tb * 128, gbc, xb, xbkey, xn, ss, tb, AT)

    def load_AT(AT, src, kcn, c0, ntok, nparts=4):
        step = ((kcn + nparts - 1) // nparts + 7) // 8 * 8
        for i, k0 in enumerate(range(0, kcn, step)):
            k1 = min(kcn, k0 + step)
            P.add("sp", lambda e, k0=k0, k1=k1: e.dma_start(
                out=AT[:, k0:k1, :], in_=src[k0 * 128:k1 * 128, c0:c0 + ntok].rearrange("(c p) t -> p c t", p=128)),
                writes=[("AT", tb, cg) for tb in range(ntok // 128) for cg in range(k0 // 8, (k1 - 1) // 8 + 1)], dma_key=("ldat", i))

    scr_i = carve(0, 128 * 4, F32).bitcast(I32)
    scr_i2 = carve(512, NTAB * 4, F32).bitcast(I32)
    D1 = carve(2048, NTAB * 4, F32); DL = carve(4096, NTAB * 4, F32)
    t1 = carve(6144, NTAB * 4, F32); t2 = carve(8192, NTAB * 4, F32)
    P.add("pool", lambda e: e.iota(out=scr_i, pattern=[[1, 128]], base=0, channel_multiplier=-1), writes=["scr_i"])
    P.add("dve", lambda e: e.tensor_copy(out=valf, in_=scr_i), reads=["scr_i"], writes=["valf"])
    P.add("dve", lambda e: e.tensor_scalar(out=ident_f, in0=valf, scalar1=0.0, scalar2=None, op0=ALU.is_equal), reads=["valf"], writes=["ident_f"])
    P.add("dve", lambda e: e.tensor_scalar(out=tri_f, in0=valf, scalar1=0.0, scalar2=None, op0=ALU.is_ge), reads=["valf"], writes=["tri_f"])
    P.add("dve", lambda e: e.tensor_copy(out=ident_b, in_=ident_f), reads=["ident_f"], writes=["ident_b"])
    P.add("dve", lambda e: e.memset(ones_f, 1.0), writes=["ones_f"])
    P.add("dve", lambda e: e.memset(epsc[:, 0:1], RMS_EPS), writes=["epsc"])
    P.add("dve", lambda e: e.memset(epsc[:, 1:2], LN_EPS), writes=["epsc"])
    P.add("sp", lambda e: e.dma_start(out=jv, in_=jv_d), writes=["jv"], dma_key="s_jv")
    P.add("dve", lambda e: e.tensor_scalar(out=jv128, in0=jv, scalar1=128.0, scalar2=None, op0=ALU.mult), reads=["jv"], writes=["jv128"])
    for r in range(4):
        P.add("dve", lambda e, r=r: e.tensor_scalar(out=jsh[:, r:r + 1], in0=jv, scalar1=128.0, scalar2=-128.0 * r, op0=ALU.mult, op1=ALU.add),
              reads=["jv"], writes=["jsh"])
    for r in range(4):
        P.add("dve", lambda e, r=r: e.tensor_scalar(out=masks[:, r, :], in0=valf, scalar1=jsh[:, r:r + 1], scalar2=0.0, op0=ALU.add, op1=ALU.is_ge),
              reads=["valf", "jsh"], writes=["masks"])
    P.add("pool", lambda e: e.iota(out=scr_i2, pattern=[[-128, NTAB]], base=320, channel_multiplier=1), writes=["scr_i2"])
    P.add("dve", lambda e: e.tensor_copy(out=D1, in_=scr_i2), reads=["scr_i2"], writes=["D1"])
    P.add("dve", lambda e: e.tensor_scalar(out=D1, in0=D1, scalar1=jv128, scalar2=None, op0=ALU.subtract), reads=["D1", "jv128"], writes=["D1"])
    P.add("pool", lambda e: e.iota(out=scr_i2, pattern=[[1, NTAB]], base=-3, channel_multiplier=0), reads=["D1"], writes=["scr_i2"])
    P.add("dve", lambda e: e.tensor_copy(out=DL, in_=scr_i2), reads=["scr_i2"], writes=["DL"])
    P.add("dve", lambda e: e.tensor_scalar(out=DL, in0=DL, scalar1=jv, scalar2=0.0, op0=ALU.add, op1=ALU.is_ge), reads=["DL", "jv"], writes=["DL"])
    P.add("dve", lambda e: e.tensor_tensor(out=t1, in0=D1, in1=DL, op=ALU.mult), reads=["D1", "DL"], writes=["t1"])
    P.add("dve", lambda e: e.tensor_scalar(out=t2, in0=DL, scalar1=1.0, scalar2=NEG_BIG, op0=ALU.subtract, op1=ALU.mult), reads=["DL"], writes=["t2"])
    for h in range(H):
        P.add("dve", lambda e, h=h: e.scalar_tensor_tensor(out=tab[:, h, :], in0=t1, scalar=slopes[h], in1=t2, op0=ALU.mult, op1=ALU.add),
              reads=["t1", "t2"], writes=["tab"])
    DW = carve(10240, NTW * 4, F32)
    scr_i3 = carve(12288, NTW * 4, F32).bitcast(I32)
    P.add("pool", lambda e: e.iota(out=scr_i3, pattern=[[-128, NTW]], base=-320 + 128 * 15, channel_multiplier=1), writes=["scr_i3"])
    P.add("dve", lambda e: e.tensor_copy(out=DW, in_=scr_i3), reads=["scr_i3"], writes=["DW"])
    P.add("dve", lambda e: e.tensor_scalar(out=DW, in0=DW, scalar1=jv128, scalar2=None, op0=ALU.subtract), reads=["DW", "jv128"], writes=["DW"])
    for h, hw in WIDX.items():
        P.add("dve", lambda e, h=h, hw=hw: e.tensor_scalar(out=tabw[:, hw, :], in0=DW, scalar1=slopes[h], scalar2=None, op0=ALU.mult), reads=["DW"], writes=["tabw"])
    lb = carve(16384, 4 * 128 * 4, F32).rearrange("p (a n) -> p a n", a=4)
    junk = carve(20480, 128 * 4, F32)
    for i, v in enumerate((lq1, lk1, lq2, lk2)):
        P.add("sp", lambda e, i=i, v=v: e.dma_start(out=lb[:, i, :], in_=v.broadcast_to([128, 128])), writes=[("lb", i)], dma_key=("s_lb", i))
    for i in range(2):
        P.add("dve", lambda e, i=i: e.tensor_tensor(out=junk, in0=lb[:, 2 * i, :], in1=lb[:, 2 * i + 1, :], op=ALU.mult),
              reads=[("lb", 2 * i), ("lb", 2 * i + 1)], writes=["junk"])
        P.add("dve", lambda e, i=i: e.reduce_sum(out=lamtmp[:, i:i + 1], in_=junk, axis=AX.X), reads=["junk"], writes=[("lt", i)])
        P.add("act", lambda e, i=i: e.activation(out=lamtmp[:, 2 + i:3 + i], in_=lamtmp[:, i:i + 1], func=AF.Exp), reads=[("lt", i)], writes=[("le", i)])
    P.add("dve", lambda e: e.tensor_tensor(out=lamtmp[:, 4:5], in0=lamtmp[:, 3:4], in1=lamtmp[:, 2:3], op=ALU.subtract), reads=[("le", 0), ("le", 1)], writes=["lt4"])
    P.add("dve", lambda e: e.tensor_scalar(out=neglam, in0=lamtmp[:, 4:5], scalar1=-lam_init, scalar2=None, op0=ALU.add), reads=["lt4"], writes=["neglam"])
    P.add("sp", lambda e: e.dma_start(out=subg, in_=subln_g.broadcast_to([128, 256])), writes=["subg"], dma_key="s_subg")
    P.add("dve", lambda e: e.tensor_scalar(out=subg, in0=subg, scalar1=1.0 - lam_init, scalar2=None, op0=ALU.mult), reads=["subg"], writes=["subg"])
    vrows = carve(24576, 128 * 4, F32)
    for a, v in enumerate((ln_v_g, ln_v_b, b_gate[0:1, :], b_gate[1:2, :])):
        P.add("sp", lambda e, v=v: e.dma_start(out=vrows[0:KC, :], in_=v.rearrange("o (c p) -> (o c) p", p=128)), writes=["vrows"], dma_key="s_vr")
        P.add("pe", lambda e: e.transpose(out=psf[4][:, 0:KC], in_=vrows[0:KC, :], identity=ident_f[0:KC, 0:KC]),
              reads=["vrows", "ident_f"], writes=[("PS", 4)])
        P.add("dve", lambda e, a=a: e.tensor_copy(out=cols[:, a, :], in_=psf[4][:, 0:KC]), reads=[("PS", 4)], writes=["cols"])
    wsl = carve(28672, 128 * 4, F32); wtf = carve(32768, 128 * 4, F32)
    bsbc = carve(36864, G * 128 * 4, F32).rearrange("p (g t) -> p g t", g=G)
    P.add("sp", lambda e: e.dma_start(out=bsbc.rearrange("p g t -> p (g t)"), in_=b_s.broadcast_to([128, G * 128])), writes=["bsbc"], dma_key="s_bs")
    for g in range(G):
        P.add("sp", lambda e, g=g: e.dma_start(out=wsl, in_=w_s[g]), writes=["wsl"], dma_key="s_ws")
        P.add("pe", lambda e: e.transpose(out=psf[4][:, 0:128], in_=wsl, identity=ident_f), reads=["wsl", "ident_f"], writes=[("PS", 4)])
        P.add("dve", lambda e: e.tensor_tensor(out=wtf, in0=psf[4][:, 0:128], in1=tri_f, op=ALU.mult), reads=[("PS", 4), "tri_f"], writes=["wtf"])
        P.add("dve", lambda e, g=g: e.tensor_copy(out=wT[:, g, :], in_=wtf), reads=["wtf"], writes=["wT"])
        P.add("pe", lambda e: e.matmul(psf[5][:, 0:128], ones_f, wtf, start=True, stop=True), reads=["wtf", "ones_f"], writes=[("PS", 5)])
        for cc in range(2):
            c = 2 * g + cc
            P.add("dve", lambda e, c=c, g=g: e.scalar_tensor_tensor(out=B2[:, c, :], in0=psf[5][:, 0:128], scalar=cols[:, 1, c:c + 1], in1=bsbc[:, g, :],
                                                                     op0=ALU.mult, op1=ALU.add), reads=[("PS", 5), "cols", "bsbc"], writes=["B2"])
    P.barrier()

    TGK = 512
    NGK = S // TGK
    ATs = [carve(i * KC * TGK * 2, KC * TGK * 2, BF16).rearrange("p (c t) -> p c t", c=KC) for i in range(2)]
    off_n = 2 * KC * TGK * 2
    xb_k = carve(off_n, 4 * D, F32); gbc_k = carve(off_n + 4 * D, 4 * D, F32)
    xn_k = [carve(off_n + 8 * D + i * 2 * D, 2 * D, BF16) for i in range(2)]
    ss_k = carve(off_n + 12 * D, 16, F32)
    soff = off_n + 12 * D + 64
    kst = [(carve(soff + i * TGK * 2, TGK * 2, BF16), ("kst", i)) for i in range(2)]
    vst = [(carve(soff + 2 * TGK * 2 + i * 1024, 1024, BF16), ("vst", i)) for i in range(2)]
    acc_off = soff + 2 * TGK * 2 + 2048
    accs_k = [(carve(acc_off + i * 2048, 2048, F32), ("acc", i)) for i in range(TGK // 128)]
    load_bcast(gbc_k, g_mix, "gbc")
    kring = Ring(kst); vring = Ring(vst)

    nbi = [0]
    backs = []

    def kv_norm(kg, tb, defer=False):
        i = nbi[0] % 2
        nbi[0] += 1
        return norm_block(x_all, kg * TGK + tb * 128, gbc_k, xb_k, ("xb", 0), xn_k[i], ss_k, tb, ATs[kg % 2], key_at=("ATK", kg % 2),
                          xnkey=("xn", i), defer=defer)

    def kv_step(pending):
        if backs:
            backs.pop(0)()
        if pending:
            backs.append(kv_norm(*pending.pop(0), defer=True))

    for tb in range(TGK // 128):
        kv_norm(0, tb)
    for kg in range(NGK):
        pending = [(kg + 1, tb) for tb in range(TGK // 128)] if kg + 1 < NGK else []

        def epi_k(ch, ts, ps, pkey, kg=kg, state={}):
            if ts == 0:
                state["cur"] = kring.next()
            stg, skey = state["cur"]
            copy_any(stg[:, ts * 512:(ts + 1) * 512], ps[:], [pkey], [(skey, ts)])
            if ts == TGK // 512 - 1:
                P.add("sp", lambda e: e.dma_start(out=KT[ch * 128:(ch + 1) * 128, kg * TGK:(kg + 1) * TGK], in_=stg),
                      reads=[(skey, t_) for t_ in range(TGK // 512)], dma_key=("st", skey))

        def hook(ng, pending=pending):
            if ng % 3 == 1:
                kv_step(pending)

        gemm_fm(w_in, 3 * D, D, ATs[kg % 2], TGK, epi_k, key_at=("ATK", kg % 2), hook=hook)
        while pending or backs:
            kv_step(pending)

        def epi_v(pi, npc, tb, nt, ps, pkey, kg=kg):
            acc, akey = accs_k[tb]
            if pi == 0 and npc > 1:
                copy_any(acc, ps[:], [pkey], [akey])
                return
            stg, skey = vring.next()
            if npc > 1:
                P.add("dve", lambda e: e.tensor_tensor(out=stg, in0=ps[:], in1=acc, op=ALU.add), reads=[pkey, akey], writes=[skey])
            else:
                copy_any(stg, ps[:], [pkey], [skey])
            r0 = kg * TGK + tb * 128
            P.add("sp", lambda e: e.dma_start(out=VV[r0:r0 + 128, nt * 512:(nt + 1) * 512], in_=stg), reads=[skey], dma_key=("st", skey))

        if 1 <= kg <= D // 512:
            P.add("pool", lambda e, p_=kg - 1: e.dma_start(out=WVB[:, p_ * 512:(p_ + 1) * 512], in_=w_in[:, D + p_ * 512:D + (p_ + 1) * 512]),
                  writes=[("WVB", kg - 1)], dma_key=("WVB", (kg - 1) % 2))
        gemm_tm(w_in, KC, 4 * D, D, ATs[kg % 2], list(range(TGK // 128)), epi_v, key_at=("ATK", kg % 2))
    P.barrier()

    NTB = TG // 128
    def do_group(tg):
        tok0 = tg * TG
        AT = carve(0, KC * TG * 2, BF16).rearrange("p (c t) -> p c t", c=KC)
        off_n = KC * TG * 2
        norm_phase(x_own, tok0, NTB, g_mix, AT, off_n)
        P.barrier()
        gvs = [carve(off_n + i * 4 * D, 4 * D, F32) for i in range(2)]
        nbf = carve(off_n + 8 * D, 2 * D, BF16)
        mts = carve(off_n + 10 * D, 2 * D, BF16).rearrange("p (c t) -> p c t", c=KC)
        soff = off_n + 12 * D
        stat = carve(soff, 256, F32)
        bnst = carve(soff + 256, (D // 512) * 6 * 4, F32).rearrange("p (n s) -> p n s", s=6)
        soff2 = soff + 256 + (D // 512) * 24
        soff2 = (soff2 + 63) // 64 * 64
        accs = [(carve(soff2 + i * 2048, 2048, F32), ("acc", i)) for i in range(2)]
        soff3 = off_n
        for sg in range(NTB // 2):
            def epi_gv(pi, npc, tb, nt, ps, pkey, sg=sg):
                li = tb - 2 * sg
                acc, akey = accs[li]
                if pi == 0 and npc > 1:
                    copy_any(acc, ps[:], [pkey], [akey])
                    return
                src = ps[:]
                rd = [pkey]
                if npc > 1:
                    P.add("dve", lambda e: e.tensor_tensor(out=acc, in0=ps[:], in1=acc, op=ALU.add), reads=[pkey, akey], writes=[akey])
                    src = acc
                    rd = [akey]
                P.add("act", lambda e: e.activation(out=gvs[li][:, nt * 512:(nt + 1) * 512], in_=src, func=AF.Gelu), reads=rd, writes=[("gv", li, nt)])

            gemm_tm(WVB, KC, 0, D, AT, [2 * sg, 2 * sg + 1], epi_gv, wreads=(lambda c0: [("WVB", c0 // 512)]))
            for li in range(2):
                tb = 2 * sg + li
                gv = gvs[li]
                for n_ in range(D // 512):
                    P.add("dve", lambda e, n_=n_, gv=gv: e.bn_stats(out=bnst[:, n_, :], in_=gv[:, n_ * 512:(n_ + 1) * 512]),
                          reads=[("gv", li, n_)], writes=["bnst"])
                P.add("dve", lambda e: e.bn_aggr(out=stat[:, 0:2], in_=bnst.rearrange("p n s -> p (n s)")), reads=["bnst"], writes=["st01"])
                P.add("act", lambda e: e.activation(out=stat[:, 2:3], in_=stat[:, 1:2], func=AF.Sqrt, bias=epsc[:, 1:2], scale=1.0), reads=["st01"], writes=["st2"])
                P.add("dve", lambda e: e.reciprocal(out=stat[:, 3:4], in_=stat[:, 2:3]), reads=["st2"], writes=["st3"])
                P.add("dve", lambda e, gv=gv: e.tensor_scalar(out=nbf, in0=gv, scalar1=stat[:, 0:1], scalar2=stat[:, 3:4], op0=ALU.subtract, op1=ALU.mult),
                      reads=[("gv", li, n_) for n_ in range(D // 512)] + ["st01", "st3"], writes=["nbf"])
                for c0 in range(0, KC, 4):
                    ps, pkey = gb.next()
                    for c in range(4):
                        cc = c0 + c
                        P.add("pe", lambda e, ps=ps, c=c, cc=cc: e.matmul(ps[:, c * 128:(c + 1) * 128], nbf[:, cc * 128:(cc + 1) * 128], wT[:, cc // 2, :], start=True, stop=True),
                              reads=["nbf", "wT"], writes=[pkey])
                    for c in range(4):
                        cc = c0 + c
                        P.add("dve", lambda e, ps=ps, c=c, cc=cc: e.scalar_tensor_tensor(out=mts[:, cc, :], in0=ps[:, c * 128:(c + 1) * 128], scalar=cols[:, 0, cc:cc + 1], in1=B2[:, cc, :],
                                                                                      op0=ALU.mult, op1=ALU.add), reads=[pkey, "cols", "B2"], writes=["mts"])
                c0t = tok0 + tb * 128
                P.add("sp", lambda e, c0t=c0t: e.dma_start(out=MT[:, c0t:c0t + 128].rearrange("(c p) t -> p c t", p=128), in_=mts), reads=["mts"], dma_key=("st", "mts"))
        P.barrier()
        fst = [(carve(soff3 + i * TG * 2, TG * 2, BF16), ("fst", i)) for i in range(3)]
        mtl = [(carve(soff3 + 3 * TG * 2 + i * 1024, 1024, BF16), ("mtl", i)) for i in range(3)]
        gtmp = [(carve(soff3 + 3 * TG * 2 + 3072 + i * 2048, 2048, F32), ("gtmp", i)) for i in range(2)]
        fring = Ring(fst); mring = Ring(mtl); tring = Ring(gtmp)

        def fm_store(dst):
            state = {}

            def put(ch, ts, producer):
                if ts == 0:
                    state["cur"] = fring.next()
                stg, skey = state["cur"]
                producer(stg[:, ts * 512:(ts + 1) * 512], (skey, ts))
                if ts == TG // 512 - 1:
                    P.add("sp", lambda e: e.dma_start(out=dst[ch * 128:(ch + 1) * 128, tok0:tok0 + TG], in_=stg),
                          reads=[(skey, t_) for t_ in range(TG // 512)], dma_key=("st", skey))
            return put

        put_u = fm_store(YAT)

        def epi_u(ch, ts, ps, pkey):
            ml, mkey = mring.next()
            tt, tkey = tring.next()
            c0t = tok0 + ts * 512
            P.add("sp", lambda e: e.dma_start(out=ml, in_=MT[ch * 128:(ch + 1) * 128, c0t:c0t + 512]), writes=[mkey], dma_key=("ld", mkey))
            P.add("act", lambda e: e.activation(out=tt, in_=ps[:], func=AF.Gelu), reads=[pkey], writes=[tkey])
            put_u(ch, ts, lambda o, skey: P.add("dve", lambda e: e.tensor_tensor(out=o, in0=tt, in1=ml, op=ALU.mult), reads=[tkey, mkey], writes=[skey]))

        gemm_fm(w_in, 0, D, AT, TG, epi_u)
        put_q = fm_store(QT)

        def epi_q(ch, ts, ps, pkey):
            put_q(ch, ts, lambda o, skey: copy_any(o, ps[:], [pkey], [skey]))

        gemm_fm(w_in, 2 * D, D, AT, TG, epi_q)
        for a, dst in ((0, GA), (1, GB)):
            put_g = fm_store(dst)

            def epi_g(ch, ts, ps, pkey, a=a, put_g=put_g):
                put_g(ch, ts, lambda o, skey: P.add("act", lambda e: e.activation(out=o, in_=ps[:], func=AF.Sigmoid, bias=cols[:, 2 + a, ch:ch + 1], scale=1.0),
                                                    reads=[pkey, "cols"], writes=[skey]))

            gemm_fm(w_in, (5 + a) * D, D, AT, TG, epi_g)
        P.barrier()

        o = [0]

        def al(nbytes, dt):
            a = carve(o[0], nbytes, dt)
            o[0] += (nbytes + 63) // 64 * 64
            return a

        qbuf = [(al(2 * TG * 2, BF16).rearrange("p (m t) -> p m t", m=2), ("qb", i)) for i in range(2)]
        kbuf = [(al(2 * CH * 128 * 2, BF16).rearrange("p (m t) -> p m t", m=2), ("kb", i)) for i in range(3)]
        vbuf = [(al(CH * 258 * 2, BF16).rearrange("p (c e) -> p c e", e=258), ("vb", i)) for i in range(3)]
        pbuf = [(al(1024, BF16).rearrange("p (m q) -> p m q", m=2), ("pt", i)) for i in range(4)]
        o1n = al(2 * 256 * 4, F32).rearrange("p (i e) -> p i e", i=2)
        ofin = [(al(1024, F32), ("of", i)) for i in range(2)]
        ybf = [(al(512, BF16), ("yb", i)) for i in range(2)]
        ybT = [(al(2 * 256 * 2, BF16).rearrange("p (a t) -> p a t", a=2), ("ybT", i)) for i in range(2)]
        sm = al(64 * 4, F32)
        sq = al(1024, F32)
        for vb, vkey in vbuf:
            P.add("dve", lambda e, vb=vb: e.memset(vb[:, :, 256:257], 1.0), writes=[vkey])
        qring = Ring(qbuf); kring = Ring(kbuf); vring = Ring(vbuf); pring = Ring(pbuf)
        ofr = Ring(ofin); ybr = Ring(ybf); ybTr = Ring(ybT)
        sring = Ring([(psf[4], ("PS", 4)), (psf[5], ("PS", 5)), (psb[1][:].bitcast(F32), ("PB", 1))])
        tpa = Ring([(psb[0], ("PB", 0))])
        smi = [0]
        rows = []
        for h in range(H):
            for qt in range(TG // 256):
                m0 = tg * NTB + 2 * qt
                nkb = 4 * m0 + 8
                for kb in range(nkb):
                    i0 = max(0, (kb - 4 * m0) // 4)
                    i1 = 2
                    if DSKIP[h] is not None:
                        i1 = min(2, max(0, (DSKIP[h] + kb - 4 * m0 + 3) // 4))
                    if i1 <= i0:
                        continue
                    rows.append(dict(h=h, qt=qt, kb=kb, m0=m0, nkb=nkb, i0=i0, i1=i1))
        cur = {}

        def emit_S(row):
            h, qt, kb, m0, nkb = row["h"], row["qt"], row["kb"], row["m0"], row["nkb"]
            if cur.get("qh") != h:
                cur["qh"] = h
                qb, qkey = qring.next()
                P.add("sp", lambda e: e.dma_start(out=qb, in_=QT[h * 256:(h + 1) * 256, tok0:tok0 + TG].rearrange("(m p) t -> p m t", p=128)),
                      writes=[qkey], dma_key=("ld", qkey))
                cur["q"] = (qb, qkey)
            if cur.get("unit") != (h, qt):
                cur["unit"] = (h, qt)
                cur["yT"] = ybTr.next()
            if cur.get("chunk") != (h, qt, kb // CH):
                cur["chunk"] = (h, qt, kb // CH)
                kc0 = (kb // CH) * CH
                l0 = kb - kc0
                nk = min(CH, nkb - kc0)
                kb_, kkey = kring.next()
                P.add("sp", lambda e: e.dma_start(
                    out=kb_[:, :, l0 * 128:nk * 128], in_=KT[h * 256:(h + 1) * 256, kb * 128:(kc0 + nk) * 128].rearrange("(m p) t -> p m t", p=128)),
                    writes=[kkey], dma_key=("ld", kkey))
                vb, vkey = vring.next()
                P.add("sp", lambda e: e.dma_start(
                    out=vb[:, l0:nk, 0:256], in_=VV[kb * 128:(kc0 + nk) * 128, h * 256:(h + 1) * 256].rearrange("(c p) e -> p c e", p=128)),
                    writes=[vkey], dma_key=("ld", vkey))
                cur["k"] = (kb_, kkey)
                cur["v"] = (vb, vkey)
            row["q"] = cur["q"]; row["k"] = cur["k"]; row["v"] = cur["v"]; row["yT"] = cur["yT"]
            qb, qkey = row["q"]
            kb_, kkey = row["k"]
            kl = kb % CH
            i0, i1 = row["i0"], row["i1"]
            sps, skey = sring.next()
            row["s"] = (sps, skey)
            for mp in range(2):
                P.add("pe", lambda e, mp=mp: e.matmul(
                    sps[:, mp * 256 + i0 * 128:mp * 256 + i1 * 128], kb_[:, mp, kl * 128:(kl + 1) * 128],
                    qb[:, mp, qt * 256 + i0 * 128:qt * 256 + i1 * 128], start=True, stop=True),
                    reads=[kkey, qkey], writes=[(skey, mp)])

        def emit_rest(row):
            h, qt, kb, m0, nkb = row["h"], row["qt"], row["kb"], row["m0"], row["nkb"]
            sps, skey = row["s"]
            s3 = sps[:, 0:512].rearrange("p (m q) -> p m q", m=2)
            vb, vkey = row["v"]
            i0, i1 = row["i0"], row["i1"]
            kl = kb % CH
            pt, pkey_ = pring.next()
            if WIDE[h]:
                cw = 4 * m0 - kb + 15
                P.add("act", lambda e: e.activation(
                    out=pt[:, :, i0 * 128:i1 * 128], in_=s3[:, :, i0 * 128:i1 * 128], func=AF.Exp, bias=tabw[:, WIDX[h], cw:cw + 1], scale=SCALE),
                    reads=[(skey, 0), (skey, 1)], writes=[(pkey_, i) for i in range(i0, i1)])
            for i in range(i0, i1):
                dl = 4 * (m0 + i) - kb
                if not WIDE[h]:
                    P.add("act", lambda e, i=i, dl=dl: e.activation(
                        out=pt[:, :, i * 128:(i + 1) * 128], in_=s3[:, :, i * 128:(i + 1) * 128], func=AF.Exp, bias=tab[:, h, dl + 3:dl + 4], scale=SCALE),
                        reads=[(skey, 0), (skey, 1)], writes=[(pkey_, i)])
                r = -dl
                if 0 <= r <= 3:
                    for mp in range(2):
                        P.add("dve", lambda e, i=i, r=r, mp=mp: e.tensor_tensor(out=pt[:, mp, i * 128:(i + 1) * 128], in0=pt[:, mp, i * 128:(i + 1) * 128], in1=masks[:, r, :], op=ALU.mult),
                              reads=[(pkey_, i)], writes=[(pkey_, i)])
                kfirst = 0 if DSKIP[h] is None else max(0, 4 * (m0 + i) - DSKIP[h] + 1)
                for mp in range(2):
                    P.add("pe", lambda e, i=i, mp=mp, first=(kb == kfirst), last=(kb == 4 * (m0 + i) + 3): e.matmul(
                        psf[mp * 2 + i][:, 0:257], pt[:, mp, i * 128:(i + 1) * 128], vb[:, kl, 0:257], start=first, stop=last),
                        reads=[(pkey_, i), vkey], writes=[("PS", mp * 2 + i)])
            if kb != nkb - 1:
                return
            yT, yTkey = row["yT"]
            for i in range(2):
                s0 = (smi[0] % 8) * 8
                smi[0] += 1
                P.add("dve", lambda e, i=i, s0=s0: e.reciprocal(out=sm[:, s0:s0 + 1], in_=psf[i][:, 256:257]), reads=[("PS", i)], writes=[("sm", s0)])
                P.add("dve", lambda e, i=i, s0=s0: e.reciprocal(out=sm[:, s0 + 1:s0 + 2], in_=psf[2 + i][:, 256:257]), reads=[("PS", 2 + i)], writes=[("sm", s0 + 1)])
                P.add("dve", lambda e, i=i, s0=s0: e.tensor_scalar(out=o1n[:, i, :], in0=psf[i][:, 0:256], scalar1=sm[:, s0:s0 + 1], scalar2=None, op0=ALU.mult),
                      reads=[("PS", i), ("sm", s0)], writes=[("o1n", i)])
                of, okey = ofr.next()
                yb, ykey = ybr.next()
                P.add("dve", lambda e, s0=s0: e.tensor_scalar(out=sm[:, s0 + 5:s0 + 6], in0=sm[:, s0 + 1:s0 + 2], scalar1=neglam, scalar2=None, op0=ALU.mult),
                      reads=[("sm", s0 + 1), "neglam"], writes=[("sm", s0 + 5)])
                P.add("dve", lambda e, i=i, of=of, s0=s0: e.scalar_tensor_tensor(out=of, in0=psf[2 + i][:, 0:256], scalar=sm[:, s0 + 5:s0 + 6], in1=o1n[:, i, :],
                                                                              op0=ALU.mult, op1=ALU.add), reads=[("PS", 2 + i), ("sm", s0 + 5), ("o1n", i)], writes=[okey])
                P.add("act", lambda e, of=of, s0=s0: e.activation(out=sq, in_=of, func=AF.Square, accum_out=sm[:, s0 + 2:s0 + 3]), reads=[okey], writes=["sq", ("sm", s0 + 2)])
                P.add("act", lambda e, s0=s0: e.activation(out=sm[:, s0 + 3:s0 + 4], in_=sm[:, s0 + 2:s0 + 3], func=AF.Sqrt, bias=epsc[:, 1:2], scale=1.0 / 256),
                      reads=[("sm", s0 + 2)], writes=[("sm", s0 + 3)])
                P.add("dve", lambda e, s0=s0: e.reciprocal(out=sm[:, s0 + 4:s0 + 5], in_=sm[:, s0 + 3:s0 + 4]), reads=[("sm", s0 + 3)], writes=[("sm", s0 + 4)])
                P.add("dve", lambda e, of=of, yb=yb, s0=s0: e.scalar_tensor_tensor(out=yb, in0=of, scalar=sm[:, s0 + 4:s0 + 5], in1=subg, op0=ALU.mult, op1=ALU.mult),
                      reads=[okey, ("sm", s0 + 4), "subg"], writes=[ykey])
                tp, tkey = tpa.next()
                for a in range(2):
                    P.add("pe", lambda e, tp=tp, a=a, yb=yb: e.transpose(out=tp[:, a * 128:(a + 1) * 128], in_=yb[:, a * 128:(a + 1) * 128], identity=ident_b),
                          reads=[ykey], writes=[tkey])
                copy_any(yT[:, :, i * 128:(i + 1) * 128], tp[:, 0:256].rearrange("p (a t) -> p a t", a=2), [tkey], [(yTkey, i)])
            c0t = tok0 + qt * 256
            P.add("sp", lambda e: e.dma_start(out=YBT[h * 256:(h + 1) * 256, c0t:c0t + 256].rearrange("(a p) t -> p a t", p=128), in_=yT),
                  reads=[(yTkey, i_) for i_ in range(2)], dma_key=("st", yTkey))

        LA = 2
        for ri in range(min(LA, len(rows))):
            emit_S(rows[ri])
        for ri, row in enumerate(rows):
            if ri + LA < len(rows):
                emit_S(rows[ri + LA])
            emit_rest(row)
        P.barrier()

        AT = carve(0, KC * TG * 2, BF16).rearrange("p (c t) -> p c t", c=KC)
        soff = KC * TG * 2
        gl = [(carve(soff + i * 1024, 1024, BF16), ("gl", i)) for i in range(3)]
        tl = [(carve(soff + 3072 + i * 2048, 2048, F32), ("tl", i)) for i in range(3)]
        tmpm = [(carve(soff + 3072 + 6144 + i * 2048, 2048, F32), ("tmpm", i)) for i in range(2)]
        fst = [(carve(soff + 3072 + 6144 + 4096 + i * TG * 2, TG * 2, BF16), ("fst", i)) for i in range(3)]
        glr = Ring(gl); tlr = Ring(tl); tmr = Ring(tmpm); fring = Ring(fst)
        load_AT(AT, YAT, KC, tok0, TG)

        def epi_a(ch, ts, ps, pkey):
            g_, gkey = glr.next()
            t_, tkey = tlr.next()
            c0t = tok0 + ts * 512
            P.add("sp", lambda e: e.dma_start(out=g_, in_=GA[ch * 128:(ch + 1) * 128, c0t:c0t + 512]), writes=[gkey], dma_key=("ld", gkey))
            P.add("dve", lambda e: e.tensor_tensor(out=t_, in0=ps[:], in1=g_, op=ALU.mult), reads=[pkey, gkey], writes=[tkey])
            P.add("sp", lambda e: e.dma_start(out=T1[ch * 128:(ch + 1) * 128, c0t:c0t + 512], in_=t_), reads=[tkey], dma_key=("st", tkey))

        gemm_fm(w_br_a, 0, D, AT, TG, epi_a)
        P.barrier()
        load_AT(AT, YBT, KC, tok0, TG)
        put_m = fm_store(MGT)

        def epi_b(ch, ts, ps, pkey):
            g_, gkey = glr.next()
            t_, tkey = tlr.next()
            m_, mkey = tmr.next()
            c0t = tok0 + ts * 512
            P.add("sp", lambda e: e.dma_start(out=g_, in_=GB[ch * 128:(ch + 1) * 128, c0t:c0t + 512]), writes=[gkey], dma_key=("ld", gkey))
            P.add("sp", lambda e: e.dma_start(out=t_, in_=T1[ch * 128:(ch + 1) * 128, c0t:c0t + 512]), writes=[tkey], dma_key=("ld", tkey))
            P.add("dve", lambda e: e.tensor_tensor(out=m_, in0=ps[:], in1=g_, op=ALU.mult), reads=[pkey, gkey], writes=[mkey])
            put_m(ch, ts, lambda o_, skey: P.add("dve", lambda e: e.tensor_tensor(out=o_, in0=m_, in1=t_, op=ALU.add), reads=[mkey, tkey], writes=[skey]))

        gemm_fm(w_br_b, 0, D, AT, TG, epi_b)
        P.barrier()
        load_AT(AT, MGT, KC, tok0, TG)

        def tm_residual(res_src, dst, accs, xl, ost, post=None):
            xlr = Ring(xl); osr = Ring(ost)

            def epi(pi, npc, tb, nt, ps, pkey):
                acc, akey = accs[tb % len(accs)]
                r0 = tok0_cur[0] + tb * 128
                if pi == 0:
                    x_, xkey = xlr.next()
                    P.add("sp", lambda e: e.dma_start(out=x_, in_=res_src[r0:r0 + 128, nt * 512:(nt + 1) * 512]), writes=[xkey], dma_key=("ld", xkey))
                    tgt, tk = (acc, akey) if npc > 1 else osr.next()
                    P.add("dve", lambda e: e.tensor_tensor(out=tgt, in0=ps[:], in1=x_, op=ALU.add), reads=[pkey, xkey], writes=[tk])
                    if npc > 1:
                        return
                elif pi < npc - 1:
                    P.add("dve", lambda e: e.tensor_tensor(out=acc, in0=ps[:], in1=acc, op=ALU.add), reads=[pkey, akey], writes=[akey])
                    return
                else:
                    tgt, tk = osr.next()
                    P.add("dve", lambda e: e.tensor_tensor(out=tgt, in0=ps[:], in1=acc, op=ALU.add), reads=[pkey, akey], writes=[tk])
                P.add("sp", lambda e: e.dma_start(out=dst[r0:r0 + 128, nt * 512:(nt + 1) * 512], in_=tgt), reads=[tk], dma_key=("st", tk))
            return epi

        tok0_cur = [tok0]
        a0 = soff
        accs = [(carve(a0 + i * 2048, 2048, F32), ("acc", i)) for i in range(NTB)]
        xl = [(carve(a0 + NTB * 2048 + i * 2048, 2048, F32), ("xl", i)) for i in range(3)]
        ost = [(carve(a0 + NTB * 2048 + 6144 + i * 2048, 2048, F32), ("ost", i)) for i in range(3)]
        gemm_tm(w_o, KC, 0, D, AT, list(range(NTB)), tm_residual(x_own, X1, accs, xl, ost))
        P.barrier()

        norm_phase(X1, tok0, NTB, g_ffn, AT, off_n)
        P.barrier()
        fst = [(carve(off_n + i * TG * 2, TG * 2, BF16), ("fst", i)) for i in range(3)]
        stmp = [(carve(off_n + 3 * TG * 2 + i * 2048, 2048, F32), ("stmp", i)) for i in range(3)]
        fring = Ring(fst); sring2 = Ring(stmp)
        put_f = fm_store(FFA)
        for pg in range(DFF // 256):
            wg, wgk = wload(wview_fm(w_gu, pg * 256, 256), KC, 256)
            wu, wuk = wload(wview_fm(w_gu, DFF + pg * 256, 256), KC, 256)
            for c2 in range(2):
                for ts in range(TG // 512):
                    pg_, pgk = gb.next()
                    pu_, puk = gb.next()
                    for (ps, pk, wv, wk) in ((pg_, pgk, wg, wgk), (pu_, puk, wu, wuk)):
                        for k in range(KC):
                            rk = [("AT", ts * 4 + i, k // 8) for i in range(4)]
                            P.add("pe", lambda e, ps=ps, wv=wv, k=k, c2=c2, ts=ts: e.matmul(
                                ps[:], wv[:, k, c2 * 128:(c2 + 1) * 128], AT[:, k, ts * 512:(ts + 1) * 512], start=(k == 0), stop=(k == KC - 1)),
                                reads=[wk] + rk, writes=[pk])
                    s_, sk = sring2.next()
                    P.add("act", lambda e, s_=s_, pg_=pg_: e.activation(out=s_, in_=pg_[:], func=AF.Silu), reads=[pgk], writes=[sk])
                    put_f(pg * 2 + c2, ts, lambda o_, skey, s_=s_, sk=sk, pu_=pu_, puk=puk: P.add(
                        "dve", lambda e: e.tensor_tensor(out=o_, in0=pu_[:], in1=s_, op=ALU.mult), reads=[puk, sk], writes=[skey]))
        P.barrier()

        ATd = carve(0, FC * TGD * 2, BF16).rearrange("p (c t) -> p c t", c=FC)
        a0 = FC * TGD * 2
        nd = TGD // 128
        accs = [(carve(a0 + i * 2048, 2048, F32), ("acc", i)) for i in range(nd)]
        xl = [(carve(a0 + nd * 2048 + i * 2048, 2048, F32), ("xl", i)) for i in range(3)]
        ost = [(carve(a0 + nd * 2048 + 6144 + i * 2048, 2048, F32), ("ost", i)) for i in range(3)]
        for sd in range(TG // TGD):
            tok0_cur[0] = tok0 + sd * TGD
            load_AT(ATd, FFA, FC, tok0_cur[0], TGD, nparts=6)
            gemm_tm(w_down, FC, 0, D, ATd, list(range(nd)), tm_residual(X1, X2, accs, xl, ost))
            P.barrier()
        tok0_cur[0] = tok0

        norm_phase(X2, tok0, NTB, g_ple, AT, off_n)
        pT = carve(off_n, PC * TG * 2, BF16).rearrange("p (c t) -> p c t", c=PC)
        pl = carve(off_n + PC * TG * 2, PLE * 4, F32)
        plb = carve(off_n + PC * TG * 2 + PLE * 4, PLE * 2, BF16)
        P.barrier()
        for tb in range(NTB):
            r0 = tok0 + tb * 128
            P.add("sp", lambda e, r0=r0: e.dma_start(out=pl, in_=p_own[r0:r0 + 128, :]), writes=["pl"], dma_key=("ld", "pl"))
            P.add("dve", lambda e: e.tensor_copy(out=plb, in_=pl), reads=["pl"], writes=["plb"])
            tp, tkey = tpr.next()
            for c in range(PC):
                P.add("pe", lambda e, tp=tp, c=c: e.transpose(out=tp[:, c * 128:(c + 1) * 128], in_=plb[:, c * 128:(c + 1) * 128], identity=ident_b),
                      reads=["plb"], writes=[tkey])
            copy_any(pT[:, :, tb * 128:(tb + 1) * 128], tp[:, 0:PC * 128].rearrange("p (c t) -> p c t", c=PC), [tkey], [("pT", tb)])
        a0 = off_n + PC * TG * 2 + PLE * 6
        a0 = (a0 + 63) // 64 * 64
        accs = [(carve(a0 + i * 2048, 2048, F32), ("acc", i)) for i in range(NTB)]
        xl = [(carve(a0 + NTB * 2048 + i * 2048, 2048, F32), ("xl", i)) for i in range(3)]
        ost = [(carve(a0 + NTB * 2048 + 6144 + i * 2048, 2048, F32), ("ost", i)) for i in range(3)]
        sg_ = [(carve(a0 + NTB * 2048 + 12288 + i * 2048, 2048, F32), ("sg", i)) for i in range(2)]
        xlr = Ring(xl); osr = Ring(ost); sgr = Ring(sg_)
        ppr = Ring([(psf[4], ("PS", 4)), (psf[5], ("PS", 5))])
        wpp_cur = {}

        def epi_p(pi, npc, tb, nt, ps, pkey):
            acc, akey = accs[tb]
            if pi == 0 and npc > 1:
                copy_any(acc, ps[:], [pkey], [akey])
                return
            src = ps[:]
            rd = [pkey]
            if npc > 1:
                P.add("dve", lambda e: e.tensor_tensor(out=acc, in0=ps[:], in1=acc, op=ALU.add), reads=[pkey, akey], writes=[akey])
                src = acc
                rd = [akey]
            s_, sk = sgr.next()
            P.add("act", lambda e: e.activation(out=s_, in_=src, func=AF.Sigmoid), reads=rd, writes=[sk])
            if wpp_cur.get("nt") != nt:
                wpp_cur["nt"] = nt
                wpp_cur["w"] = wload(wview_tm(w_pp, 0, PC, nt * 512, 512), PC, 512)
            wv, wkey = wpp_cur["w"]
            pp, ppk = ppr.next()
            for c in range(PC):
                P.add("pe", lambda e, pp=pp, c=c, wv=wv: e.matmul(pp[:], pT[:, c, tb * 128:(tb + 1) * 128], wv[:, c, :], start=(c == 0), stop=(c == PC - 1)),
                      reads=[wkey, ("pT", tb)], writes=[ppk])
            x_, xkey = xlr.next()
            r0 = tok0 + tb * 128
            P.add("sp", lambda e: e.dma_start(out=x_, in_=X2[r0:r0 + 128, nt * 512:(nt + 1) * 512]), writes=[xkey], dma_key=("ld", xkey))
            P.add("dve", lambda e: e.tensor_tensor(out=s_, in0=pp[:], in1=s_, op=ALU.mult), reads=[ppk, sk], writes=[sk])
            o_, ok = osr.next()
            P.add("dve", lambda e: e.tensor_tensor(out=o_, in0=s_, in1=x_, op=ALU.add), reads=[sk, xkey], writes=[ok])
            P.add("sp", lambda e: e.dma_start(out=X3[r0:r0 + 128, nt * 512:(nt + 1) * 512], in_=o_), reads=[ok], dma_key=("st", ok))

        gemm_tm(w_pg, KC, 0, D, AT, list(range(NTB)), epi_p)
        P.barrier()

        xbs = [(carve(i * 4 * D, 4 * D, F32), ("xb", i)) for i in range(2)]
        gbc = carve(8 * D, 4 * D, F32)
        xn = carve(12 * D, 2 * D, BF16)
        ss = carve(14 * D, 16, F32)
        fos = [(carve(14 * D + 64 + i * 4 * D, 4 * D, F32), ("fo", i)) for i in range(2)]
        load_bcast(gbc, g_final, "gbc")
        finals = []
        for tb in range(NTB):
            r0 = tok0 + tb * 128
            xb, xbkey = xbs[tb % 2]
            fo, fokey = fos[tb % 2]
            finals.append(norm_block(X3, r0, gbc, xb, xbkey, xn, ss, tb, None, final_out=(fo, fokey, out_d[r0:r0 + 128, :])))
        P.barrier()

    for tg_ in range(NO // TG):
        do_group(tg_)

    P.emit(final_waits=[op for op in P.dma_last.values() if not (isinstance(op.dma_key, tuple) and op.dma_key[0] in ("W", "WVB"))])
    st.close()
    return nc


def alibi_slopes(n_heads):
    return [float(np.exp2(np.float32(-8.0) * np.float32(i) / np.float32(n_heads))) for i in range(1, n_heads + 1)]


def make_in_maps(cfg, inputs):
    D = cfg["D"]; S = cfg["S"]
    x = np.asarray(inputs["x"], dtype=np.float32)
    p = np.asarray(inputs["p"], dtype=np.float32)[0]
    B = x.shape[0]
    NBA = S // 128
    NOB = NBA // 4

    def w(name, shape=None):
        a = np.ascontiguousarray(np.asarray(inputs[name], dtype=np.float32))
        return a.reshape(shape) if shape is not None else a

    G = D // 256
    shared = {
        "g_mix": w("g_mix", (1, D)), "w_in": w("w_in", (D, 7 * D)), "b_gate": w("b_gate", (2, D)),
        "ln_v_g": w("ln_v_g", (1, D)), "ln_v_b": w("ln_v_b", (1, D)),
        "w_s": w("w_s", (G, 128, 128)), "b_s": w("b_s", (1, G * 128)),
        "lambda_q1": w("lambda_q1", (1, 128)), "lambda_k1": w("lambda_k1", (1, 128)),
        "lambda_q2": w("lambda_q2", (1, 128)), "lambda_k2": w("lambda_k2", (1, 128)),
        "subln_g": w("subln_g", (1, 256)),
        "w_br_a": w("w_br_a", (D, D)), "w_br_b": w("w_br_b", (D, D)), "w_o": w("w_o", (D, D)),
        "g_ffn": w("g_ffn", (1, D)), "w_gu": w("w_gu", (D, 2 * cfg["DFF"])), "w_down": w("w_down", (cfg["DFF"], D)),
        "g_ple": w("g_ple", (1, D)), "w_ple_gate": w("w_ple_gate", (D, D)), "w_ple_proj": w("w_ple_proj", (cfg["PLE"], D)),
        "g_final": w("g_final", (1, D)),
    }
    maps = []
    for c in range(4 * B):
        b, j = c // 4, c % 4
        xb = x[b].reshape(NBA, 128, D)
        pb = p[b].reshape(NBA, 128, -1)
        m = dict(shared)
        m["x_all"] = np.ascontiguousarray(x[b])
        m["x_own"] = np.ascontiguousarray(xb[j::4].reshape(NOB * 128, D))
        m["p_own"] = np.ascontiguousarray(pb[j::4].reshape(NOB * 128, -1))
        m["jv"] = np.full((128, 1), float(j), dtype=np.float32)
        maps.append(m)
    return maps


def assemble(cfg, results, B):
    D = cfg["D"]; S = cfg["S"]
    NBA = S // 128
    NOB = NBA // 4
    out = np.empty((B, NBA, 128, D), dtype=np.float32)
    for c in range(4 * B):
        b, j = c // 4, c % 4
        out[b, j::4] = np.asarray(results[c]["out"], dtype=np.float32).reshape(NOB, 128, D)
    return out.reshape(B, S, D)


FULL_CFG = {"D": 4096, "S": 8192, "DFF": 11008, "PLE": 256, "slopes": alibi_slopes(16)}


def kernel(**inputs):
    cfg = FULL_CFG
    nc = build(cfg)
    maps = make_in_maps(cfg, inputs)
    res = run_bass_kernel_spmd(nc, maps, core_ids=list(range(8)))
    return assemble(cfg, res.results, 2)
```
